# Optimizing a Trainium2 kernel written in Bass

```python
import math
import jax, jax.numpy as jnp
from jax import lax
import numpy as np

D_MODEL = 2048
BATCH = 2
SEQ = 4096
DEPTH = 1

GRID_W = 64
CTX_LEN = 256
DIFF_HEAD_DIM = 64
DIFF_WIDTH = D_MODEL // 2
N_DIFF_HEADS = DIFF_WIDTH // (2 * DIFF_HEAD_DIM)
FOURIER_WIDTH = D_MODEL // 2
N_FOURIER_GROUPS = 4
FOURIER_GROUP_DIM = FOURIER_WIDTH // N_FOURIER_GROUPS
D_FF = 4 * D_MODEL
N_MOD = 6
Q_BLOCK = 128
ROPE_BASE = 10000.0
EPS = 1e-6
IN_COLS = 3 * DIFF_WIDTH + FOURIER_WIDTH + 2 * D_MODEL

kernel_name = "hybrid_diffattn_fourier_dit_block"


def rmsnorm(x, g):
    xf = x.astype(jnp.float32)
    y = xf * lax.rsqrt(jnp.mean(xf * xf, axis=-1, keepdims=True) + EPS)
    return (y * g.astype(jnp.float32)).astype(x.dtype)


def modulate(h, shift, scale):
    return h * (1.0 + scale) + shift


def axial_tables(seq_len):
    rows_n = seq_len // GRID_W
    row = jnp.repeat(jnp.arange(rows_n), GRID_W)
    col = jnp.tile(jnp.arange(GRID_W), rows_n)
    half = DIFF_HEAD_DIM // 2
    inv_freq = ROPE_BASE ** (-jnp.arange(0, half, 2, dtype=jnp.float32) / half)

    def cs(pos):
        ang = pos.astype(jnp.float32)[:, None] * inv_freq[None, :]
        ang = jnp.concatenate([ang, ang], axis=-1)
        return jnp.cos(ang), jnp.sin(ang)

    cr, sr = cs(row)
    cc, sc = cs(col)
    return cr, sr, cc, sc


def _rotate(x, cos, sin):
    cos = cos[None, :, None, None, :]
    sin = sin[None, :, None, None, :]
    x1, x2 = jnp.split(x, 2, axis=-1)
    return x * cos + jnp.concatenate([-x2, x1], axis=-1) * sin


def axial_rope(x, tabs):
    cr, sr, cc, sc = tabs
    half = DIFF_HEAD_DIM // 2
    xf = x.astype(jnp.float32)
    y = jnp.concatenate([_rotate(xf[..., :half], cr, sr), _rotate(xf[..., half:], cc, sc)], axis=-1)
    return y.astype(x.dtype)


def diff_attention(q, k, v, lam):
    B, Lq, H, _, Dh = q.shape
    nblk = Lq // Q_BLOCK
    scale = Dh ** -0.5
    kf = k.astype(jnp.float32)
    vf = v.astype(jnp.float32)
    qb = q.astype(jnp.float32).reshape(B, nblk, Q_BLOCK, H, 2, Dh).transpose(1, 0, 2, 3, 4, 5)

    def block(qblk):
        s = jnp.einsum('bqhcd,bkhcd->bhcqk', qblk, kf) * scale
        p = jax.nn.softmax(s, axis=-1)
        a = p[:, :, 0] - lam * p[:, :, 1]
        return jnp.einsum('bhqk,bkhe->bqhe', a, vf)

    o = lax.map(block, qb)
    return o.transpose(1, 0, 2, 3, 4).reshape(B, Lq, H, 2 * Dh)


def fourier_mix(f):
    B, L, _ = f.shape
    u = f.reshape(B, L, N_FOURIER_GROUPS, FOURIER_GROUP_DIM).astype(jnp.float32)
    y = jnp.fft.fft2(u, axes=(1, 3), norm="ortho").real
    return y.reshape(B, L, FOURIER_WIDTH).astype(f.dtype)


def split_proj(p):
    B, L, _ = p.shape
    o = 0
    q = p[..., o:o + DIFF_WIDTH].reshape(B, L, N_DIFF_HEADS, 2, DIFF_HEAD_DIM); o += DIFF_WIDTH
    k = p[..., o:o + DIFF_WIDTH].reshape(B, L, N_DIFF_HEADS, 2, DIFF_HEAD_DIM); o += DIFF_WIDTH
    v = p[..., o:o + DIFF_WIDTH].reshape(B, L, N_DIFF_HEADS, 2 * DIFF_HEAD_DIM); o += DIFF_WIDTH
    f = p[..., o:o + FOURIER_WIDTH]; o += FOURIER_WIDTH
    ga = p[..., o:o + D_MODEL]; o += D_MODEL
    gf = p[..., o:o + D_MODEL]
    return q, k, v, f, ga, gf


def merge_branches(attn_o, f, ga, gf, g_subln, lam_init, w_attn_br, w_four_br, w_out):
    B, L = attn_o.shape[:2]
    a = (rmsnorm(attn_o, g_subln) * (1.0 - lam_init)).reshape(B, L, DIFF_WIDTH)
    ya = a @ w_attn_br
    yf = fourier_mix(f) @ w_four_br
    merged = jax.nn.sigmoid(ga) * ya + jax.nn.sigmoid(gf) * yf
    return merged @ w_out


def sq_relu_mlp(h, w1, w2):
    z = jax.nn.relu(h @ w1)
    return (z * z) @ w2


def setup_inputs(seed: int = 0) -> dict:
    key = jax.random.key(seed)
    ks = jax.random.split(key, 24)
    nrm = lambda k, shape, s: jax.random.normal(k, shape, jnp.float32) * s
    gain = lambda k, shape: 1.0 + 0.02 * jax.random.normal(k, shape, jnp.float32)
    return {
        "x": nrm(ks[0], (BATCH, SEQ, D_MODEL), 1.0),
        "c": nrm(ks[1], (BATCH, D_MODEL), 1.0),
        "ctx": nrm(ks[2], (BATCH, CTX_LEN, D_MODEL), 1.0),
        "c_ctx": nrm(ks[3], (D_MODEL,), 1.0),
        "w_ada": nrm(ks[4], (DEPTH, D_MODEL, N_MOD * D_MODEL), 0.5 * D_MODEL ** -0.5),
        "b_ada": nrm(ks[5], (DEPTH, N_MOD * D_MODEL), 0.02),
        "g_norm1": gain(ks[6], (DEPTH, D_MODEL)),
        "w_in": nrm(ks[7], (DEPTH, D_MODEL, IN_COLS), D_MODEL ** -0.5),
        "lam_q1": nrm(ks[8], (DEPTH, DIFF_HEAD_DIM), 0.1),
        "lam_k1": nrm(ks[9], (DEPTH, DIFF_HEAD_DIM), 0.1),
        "lam_q2": nrm(ks[10], (DEPTH, DIFF_HEAD_DIM), 0.1),
        "lam_k2": nrm(ks[11], (DEPTH, DIFF_HEAD_DIM), 0.1),
        "g_subln": gain(ks[12], (DEPTH, 2 * DIFF_HEAD_DIM)),
        "w_attn_br": nrm(ks[13], (DEPTH, DIFF_WIDTH, D_MODEL), DIFF_WIDTH ** -0.5),
        "w_four_br": nrm(ks[14], (DEPTH, FOURIER_WIDTH, D_MODEL), FOURIER_WIDTH ** -0.5),
        "w_out": nrm(ks[15], (DEPTH, D_MODEL, D_MODEL), D_MODEL ** -0.5),
        "g_norm2": gain(ks[16], (DEPTH, D_MODEL)),
        "w_mlp_in": nrm(ks[17], (DEPTH, D_MODEL, D_FF), D_MODEL ** -0.5),
        "w_mlp_out": nrm(ks[18], (DEPTH, D_FF, D_MODEL), D_FF ** -0.5),
        "g_final": gain(ks[19], (D_MODEL,)),
    }


def reference(x, c, ctx, c_ctx, w_ada, b_ada, g_norm1, w_in, lam_q1, lam_k1, lam_q2, lam_k2,
              g_subln, w_attn_br, w_four_br, w_out, g_norm2, w_mlp_in, w_mlp_out, g_final):
    tabs = axial_tables(x.shape[1])
    xl = x
    xc = ctx
    for l in range(DEPTH):
        lam_init = 0.8 - 0.6 * math.exp(-0.3 * l)
        last = l == DEPTH - 1
        mod = (jax.nn.silu(c) @ w_ada[l] + b_ada[l])[:, None, :]
        sh1, sc1, gt1, sh2, sc2, gt2 = jnp.split(mod, N_MOD, axis=-1)
        modc = jax.nn.silu(c_ctx) @ w_ada[l] + b_ada[l]
        sh1c, sc1c, gt1c, sh2c, sc2c, gt2c = jnp.split(modc, N_MOD, axis=-1)
        lam = (jnp.exp(jnp.sum(lam_q1[l].astype(jnp.float32) * lam_k1[l].astype(jnp.float32)))
               - jnp.exp(jnp.sum(lam_q2[l].astype(jnp.float32) * lam_k2[l].astype(jnp.float32)))
               + lam_init)

        hl = modulate(rmsnorm(xl, g_norm1[l]), sh1, sc1)
        hc = modulate(rmsnorm(xc, g_norm1[l]), sh1c, sc1c)
        ql, kl, vl, fl, gal, gfl = split_proj(hl @ w_in[l])
        qc, kc, vc, fc, gac, gfc = split_proj(hc @ w_in[l])
        k_all = jnp.concatenate([axial_rope(kl, tabs), kc.astype(kl.dtype)], axis=1)
        v_all = jnp.concatenate([vl, vc.astype(vl.dtype)], axis=1)
        ol = diff_attention(axial_rope(ql, tabs), k_all, v_all, lam)
        xl_mid = xl + gt1 * merge_branches(ol, fl, gal, gfl, g_subln[l], lam_init,
                                           w_attn_br[l], w_four_br[l], w_out[l])
        if not last:
            oc = diff_attention(qc, kc, vc, lam)
            xc = xc + gt1c * merge_branches(oc, fc, gac, gfc, g_subln[l], lam_init,
                                            w_attn_br[l], w_four_br[l], w_out[l])
            xc = xc + gt2c * sq_relu_mlp(modulate(rmsnorm(xc, g_norm2[l]), sh2c, sc2c),
                                         w_mlp_in[l], w_mlp_out[l])
        xl = xl_mid + gt2 * sq_relu_mlp(modulate(rmsnorm(xl_mid, g_norm2[l]), sh2, sc2),
                                        w_mlp_in[l], w_mlp_out[l])
    return rmsnorm(xl, g_final)
```

```python
import math
from contextlib import ExitStack

import numpy as np
import ml_dtypes

import concourse.bass as bass
import concourse.mybir as mybir
from concourse.bass_utils import run_bass_kernel_spmd

F32 = mybir.dt.float32
BF16 = mybir.dt.bfloat16
AF = mybir.ActivationFunctionType
ALU = mybir.AluOpType

D = 2048
SEQ = 4096
CTX = 256
NKEY = SEQ + CTX
NKT = NKEY // 128
OWN = 1024
NH = 8
DFF = 8192
EPS = 1e-6
LAM_INIT = 0.8 - 0.6 * math.exp(0.0)
VW = 130

DEBUG = False
STOP = -1
SUB = -1


class Sem:
    __slots__ = ("h", "count")

    def __init__(self, h):
        self.h = h
        self.count = 0


class Buf:
    __slots__ = ("name", "w", "r", "x")

    def __init__(self, name="", x=False):
        self.name = name
        self.w = None
        self.r = {}
        self.x = x


def PBuf():
    return Buf("psum", True)


ENGS = ("pe", "act", "dve", "pool", "sp")


class Rec:
    def __init__(self, nc):
        self.nc = nc
        self.esem = {e: Sem(nc.alloc_semaphore("es_" + e)) for e in ENGS}
        self.known = {e: {} for e in ENGS}
        self.stream = {e: [] for e in ENGS}
        self.dsems = []
        self.nds = 0

    def dsem(self):
        s = Sem(self.nc.alloc_semaphore("ds%d" % self.nds))
        self.nds += 1
        self.dsems.append(s)
        return s

    def _waits(self, e, reads, writes):
        deps = {}

        def add(ev):
            if ev is None:
                return
            k, v = ev
            if deps.get(k, 0) < v:
                deps[k] = v

        for b in reads:
            add(b.w)
        for b in writes:
            add(b.w)
            for k, v in b.r.items():
                add((k, v))
        kn = self.known[e]
        for k, v in deps.items():
            if e == "pe" and k is self.esem["pe"]:
                continue
            if kn.get(k, 0) >= v:
                continue
            kn[k] = v
            self.stream[e].append(lambda eng, h=k.h, v=v: eng.wait_ge(h, v))

    def _post(self, ev, reads, writes):
        k, v = ev
        for b in reads:
            if b.r.get(k, 0) < v:
                b.r[k] = v
        for b in writes:
            b.w = ev
            b.r = {}

    def op(self, e, fn, reads=(), writes=(), signal=True):
        xr = [b for b in reads if b.x]
        if xr:
            reads = [b for b in reads if not b.x]
            writes = list(writes) + xr
        self._waits(e, reads, writes)
        s = self.esem[e]
        if signal:
            s.count += 1
            v = s.count
            self.stream[e].append(lambda eng, fn=fn, h=s.h: fn(eng).then_inc(h, 1))
        else:
            v = s.count + 1
            self.stream[e].append(lambda eng, fn=fn: fn(eng))
        self._post((s, v), reads, writes)

    def dma(self, q, fn, ds, reads=(), writes=()):
        self._waits(q, reads, writes)
        ds.count += 16
        self.stream[q].append(lambda eng, fn=fn, h=ds.h: fn(eng).then_inc(h, 16))
        self._post((ds, ds.count), reads, writes)

    def flush(self):
        for q in ("sp",):
            kn = self.known[q]
            for s in self.dsems:
                if s.count > 0 and kn.get(s, 0) < s.count:
                    kn[s] = s.count
                    self.stream[q].append(lambda eng, h=s.h, v=s.count: eng.wait_ge(h, v))
        st = self.stream
        with self.nc.Block() as blk:
            @blk.tensor
            def _(e):
                for f in st["pe"]:
                    f(e)

            @blk.scalar
            def _(e):
                for f in st["act"]:
                    f(e)

            @blk.vector
            def _(e):
                for f in st["dve"]:
                    f(e)

            @blk.gpsimd
            def _(e):
                for f in st["pool"]:
                    f(e)

            @blk.sync
            def _(e):
                for f in st["sp"]:
                    f(e)
        self.stream = {e: [] for e in ENGS}


def _rope_tables(pos):
    pos = np.asarray(pos, dtype=np.float64)
    row = pos // 64
    col = pos % 64
    inv = 10000.0 ** (-(np.arange(0, 32, 2, dtype=np.float64)) / 32.0)
    C = np.zeros((128, len(pos)), np.float64)
    S = np.zeros((128, len(pos)), np.float64)
    for p in range(128):
        d = p % 64
        pp = row if d < 32 else col
        f = inv[d % 16]
        C[p] = np.cos(pp * f)
        sgn = -1.0 if (d % 32) < 16 else 1.0
        S[p] = sgn * np.sin(pp * f)
    return C, S


def _perm_matrix():
    P = np.zeros((128, 128), np.float32)
    for m in range(128):
        d = m % 64
        pm = m + 16 if (d % 32) < 16 else m - 16
        P[pm, m] = 1.0
    return P


_CONST_CACHE = {}


def _consts():
    if _CONST_CACHE:
        return _CONST_CACHE
    bf = ml_dtypes.bfloat16
    C, S = _rope_tables(np.arange(SEQ))
    _CONST_CACHE["ropeC"] = C.astype(np.float32)
    _CONST_CACHE["ropeS"] = S.astype(np.float32)
    _CONST_CACHE["perm"] = _perm_matrix().astype(bf)
    _CONST_CACHE["ident_bf"] = np.eye(128, dtype=np.float32).astype(bf)
    _CONST_CACHE["ident_f"] = np.eye(128, dtype=np.float32)
    j = np.arange(256, dtype=np.float64)
    ang = 2.0 * np.pi * np.outer(j, j) / 256.0
    cg = np.cos(ang) / 16.0
    sg = -np.sin(ang) / 16.0
    tab = np.concatenate([cg, sg], axis=1)
    _CONST_CACHE["dftg"] = np.ascontiguousarray(tab.reshape(2, 128, 512).transpose(1, 0, 2)).astype(bf)
    t = np.arange(SEQ, dtype=np.int64)
    dftp = []
    for jq in range(4):
        k = np.arange(jq * OWN, (jq + 1) * OWN, dtype=np.int64)
        prod = (np.outer(t, k) % SEQ).astype(np.float64)
        ang = 2.0 * np.pi * prod / SEQ
        c = (np.cos(ang) / 64.0).astype(np.float32)
        s = (np.sin(ang) / 64.0).astype(np.float32)
        cs = np.stack([c, s], axis=1)
        cs = cs.reshape(32, 128, 2, 2, 512)
        cs = cs.transpose(3, 1, 0, 2, 4)
        dftp.append(np.ascontiguousarray(cs).astype(bf))
    _CONST_CACHE["dftp"] = dftp
    return _CONST_CACHE


def build_nc():
    nc = bass.Bass("TRN2", target_bir_lowering=False)
    rec = Rec(nc)

    def din(name, shape, dt=F32):
        return nc.dram_tensor(name, list(shape), dt, kind="ExternalInput").ap()

    skind = "ExternalOutput" if DEBUG else "Internal"

    def dscr(name, shape, dt):
        return nc.dram_tensor(name, list(shape), dt, kind=skind).ap()

    xb = din("xb", [SEQ, D])
    ctxb = din("ctxb", [CTX, D])
    xown = din("xown", [OWN, D])
    small_r = din("small_r", [64, 128])
    bada_r = din("bada_r", [96, 128])
    lamv = din("lamv", [256])
    gsub = din("gsub", [128, 1])
    gfin = din("gfin", [D])
    w_ada = din("w_ada", [D, 6 * D])
    w_in = din("w_in", [D, 8192])
    w_abr = din("w_abr", [1024, D])
    w_fbr = din("w_fbr", [1024, D])
    w_out = din("w_out", [D, D])
    w_m1 = din("w_m1", [D, DFF])
    w_m2 = din("w_m2", [DFF, D])
    ropeC = din("ropeC", [128, SEQ])
    ropeS = din("ropeS", [128, SEQ])
    ropeCo = din("ropeCo", [128, OWN])
    ropeSo = din("ropeSo", [128, OWN])
    perm_d = din("perm", [128, 128], BF16)
    identb_d = din("ident_bf", [128, 128], BF16)
    identf_d = din("ident_f", [128, 128], F32)
    dftg_d = din("dftg", [128, 2, 512], BF16)
    dftp_d = din("dftp", [2, 128, 32 * 2 * 512], BF16)
    out_d = nc.dram_tensor("out", [OWN, D], F32, kind="ExternalOutput").ap()

    hT_s = dscr("hT_s", [11, 128, 16 * 512], BF16)
    kT_s = dscr("kT_s", [NH, 128, NKEY], BF16)
    vA_s = dscr("vA_s", [NH, 128, NKT * VW], BF16)
    u_s = dscr("u_s", [4, 128, 32 * 512], BF16)
    qT_s = dscr("qT_s", [128, NH * OWN], BF16)
    aT_s = dscr("aT_s", [128, NH * OWN], BF16)
    yfT_s = dscr("yfT_s", [128, 8 * OWN], BF16)
    mT_s = dscr("mT_s", [128, 16 * OWN], BF16)
    xmid_s = dscr("xmid_s", [128, 16 * OWN], F32)
    h2T_s = dscr("h2T_s", [128, 16 * OWN], BF16)
    xout_s = dscr("xout_s", [128, 16 * OWN], F32)
    B_hT = [Buf("hT_s%d" % i) for i in range(11)]
    B_kT = Buf("kT_s")
    B_vA = Buf("vA_s")
    B_u = Buf("u_s")
    B_qT = Buf("qT_s")
    B_aT = Buf("aT_s")
    B_yfT = Buf("yfT_s")
    B_mT = Buf("mT_s")
    B_xmid = Buf("xmid_s")
    B_h2T = Buf("h2T_s")
    B_xout = Buf("xout_s")

    op = rec.op
    dma = rec.dma

    with ExitStack() as top:
        def sbt(es, name, shape, dt):
            return es.enter_context(nc.sbuf_tensor(name, list(shape), dt))

        def pst(es, name, shape, dt):
            return es.enter_context(nc.psum_tensor(name, list(shape), dt))

        vecs = sbt(top, "vecs", [128, 8, 16], F32)
        V_G1, V_SH1, V_G1C, V_SH1C, V_GT1, V_G2, V_SH2, V_GT2 = range(8)
        lamneg = sbt(top, "lamneg", [128, 1], F32)
        gsub_s = sbt(top, "gsub_s", [128, 1], F32)
        nhalf = sbt(top, "nhalf", [128, 512], F32)
        ident_bf = sbt(top, "ident_bf_s", [128, 128], BF16)
        ident_f = sbt(top, "ident_f_s", [128, 128], F32)
        perm = sbt(top, "perm_s", [128, 128], BF16)
        ones_bf = sbt(top, "ones_bf", [128, 128], BF16)
        B_vecs, B_lam, B_gsub, B_nhalf, B_idb, B_idf, B_perm, B_ones = [Buf() for _ in range(8)]

        with ExitStack() as es:
            bada_t = sbt(es, "bada_t", [96, 128], F32)
            small_t = sbt(es, "small_t", [64, 128], F32)
            lam_t = sbt(es, "lam_t", [128, 256], F32)
            junk64 = sbt(es, "junk64", [128, 64], F32)
            badaT = sbt(es, "badaT", [128, 96], F32)
            smallT = sbt(es, "smallT", [128, 64], F32)
            s_bf = sbt(es, "s_bf", [128, 16, 2], BF16)
            modT = sbt(es, "modT", [128, 96, 2], F32)
            lsc = sbt(es, "lsc", [128, 8], F32)
            wblk = [sbt(es, "wadab%d" % i, [128, 16, 512], BF16) for i in range(3)]
            p_tr1 = pst(es, "p_tr1", [128, 512], F32)
            p_tr2 = pst(es, "p_tr2", [128, 512], F32)
            p_mod = pst(es, "p_mod", [128, 512], F32)
            Bs = {n: Buf(n) for n in ("bada_t", "small_t", "lam_t", "junk64", "badaT", "smallT", "s_bf",
                                       "modT", "lsc", "p_tr1", "p_tr2", "p_mod")}
            for n_ in ("p_tr1", "p_tr2", "p_mod"):
                Bs[n_].x = True
            Bw = [Buf("wblk%d" % i) for i in range(3)]
            dw = [rec.dsem() for _ in range(3)]
            d0 = rec.dsem()

            dma("sp", lambda e: e.dma_start(out=bada_t[:], in_=bada_r), d0, writes=[Bs["bada_t"]])
            dma("sp", lambda e: e.dma_start(out=small_t[:], in_=small_r), d0, writes=[Bs["small_t"]])
            dma("sp", lambda e: e.dma_start(out=ident_f[:], in_=identf_d), d0, writes=[B_idf])
            dma("sp", lambda e: e.dma_start(out=ident_bf[:], in_=identb_d), d0, writes=[B_idb])
            dma("sp", lambda e: e.dma_start(out=perm[:], in_=perm_d), d0, writes=[B_perm])
            dma("sp", lambda e: e.dma_start(out=lam_t[:], in_=lamv.partition_broadcast(128)), d0,
                writes=[Bs["lam_t"]])
            dma("sp", lambda e: e.dma_start(out=gsub_s[:], in_=gsub), d0, writes=[B_gsub])
            op("dve", lambda e: e.memset(nhalf[:], -0.5), writes=[B_nhalf])
            op("dve", lambda e: e.memset(ones_bf[:], 1.0), writes=[B_ones])

            if SUB == 1:
                rec.flush()
                return nc
            w_ada_v = w_ada.rearrange("(k p) n -> p k n", p=128)
            NB = 24

            def load_wada(cb):
                i = cb % 3
                dma("pool", lambda e, i=i, cb=cb: e.dma_start(out=wblk[i][:], in_=w_ada_v[:, :, cb * 512:(cb + 1) * 512]),
                    dw[i], writes=[Bw[i]])

            load_wada(0)
            load_wada(1)

            op("pe", lambda e: e.transpose(out=p_tr1[:, 0:96], in_=bada_t[:], identity=ident_f[0:96, 0:96]),
               reads=[Bs["bada_t"], B_idf], writes=[Bs["p_tr1"]])
            op("pe", lambda e: e.transpose(out=p_tr2[:, 0:64], in_=small_t[:], identity=ident_f[0:64, 0:64]),
               reads=[Bs["small_t"], B_idf], writes=[Bs["p_tr2"]])
            op("dve", lambda e: e.tensor_copy(out=badaT[:], in_=p_tr1[:, 0:96]), reads=[Bs["p_tr1"]], writes=[Bs["badaT"]])
            op("dve", lambda e: e.tensor_copy(out=smallT[:], in_=p_tr2[:, 0:64]), reads=[Bs["p_tr2"]], writes=[Bs["smallT"]])
            if SUB == 2:
                rec.flush()
                return nc
            for v in range(2):
                op("act", lambda e, v=v: e.activation(out=s_bf[:, :, v], in_=smallT[:, v * 16:(v + 1) * 16], func=AF.Silu),
                   reads=[Bs["smallT"]], writes=[Bs["s_bf"]])

            if SUB == 3:
                rec.flush()
                return nc
            for cb in range(NB):
                if cb + 2 < NB:
                    load_wada(cb + 2)
                i = cb % 3
                for j in range(4):
                    n = cb * 4 + j
                    for kc in range(16):
                        op("pe", lambda e, i=i, j=j, n=n, kc=kc: e.matmul(
                            p_mod[:, n * 2:(n + 1) * 2], lhsT=wblk[i][:, kc, j * 128:(j + 1) * 128], rhs=s_bf[:, kc, :],
                            start=(kc == 0), stop=(kc == 15)),
                           reads=[Bw[i], Bs["s_bf"]], writes=[Bs["p_mod"]], signal=(kc == 15 and j == 3))

            if SUB == 4:
                rec.flush()
                return nc
            for v in range(2):
                op("dve", lambda e, v=v: e.tensor_tensor(out=modT[:, :, v], in0=p_mod[:, v:192:2], in1=badaT[:], op=ALU.add),
                   reads=[Bs["p_mod"], Bs["badaT"]], writes=[Bs["modT"]])
            g1T = smallT[:, 32:48]
            g2T = smallT[:, 48:64]
            RB = [Bs["modT"], Bs["smallT"]]
            op("dve", lambda e: e.scalar_tensor_tensor(out=vecs[:, V_G1, :], in0=modT[:, 16:32, 0], scalar=1.0, in1=g1T,
                                                       op0=ALU.add, op1=ALU.mult), reads=RB, writes=[B_vecs])
            op("dve", lambda e: e.scalar_tensor_tensor(out=vecs[:, V_G1C, :], in0=modT[:, 16:32, 1], scalar=1.0, in1=g1T,
                                                       op0=ALU.add, op1=ALU.mult), reads=RB, writes=[B_vecs])
            op("dve", lambda e: e.scalar_tensor_tensor(out=vecs[:, V_G2, :], in0=modT[:, 64:80, 0], scalar=1.0, in1=g2T,
                                                       op0=ALU.add, op1=ALU.mult), reads=RB, writes=[B_vecs])
            op("dve", lambda e: e.tensor_copy(out=vecs[:, V_SH1, :], in_=modT[:, 0:16, 0]), reads=RB, writes=[B_vecs])
            op("dve", lambda e: e.tensor_copy(out=vecs[:, V_SH1C, :], in_=modT[:, 0:16, 1]), reads=RB, writes=[B_vecs])
            op("dve", lambda e: e.tensor_copy(out=vecs[:, V_GT1, :], in_=modT[:, 32:48, 0]), reads=RB, writes=[B_vecs])
            op("dve", lambda e: e.tensor_copy(out=vecs[:, V_SH2, :], in_=modT[:, 48:64, 0]), reads=RB, writes=[B_vecs])
            op("dve", lambda e: e.tensor_copy(out=vecs[:, V_GT2, :], in_=modT[:, 80:96, 0]), reads=RB, writes=[B_vecs])
            if SUB == 5:
                rec.flush()
                return nc
            for q in range(2):
                op("dve", lambda e, q=q: e.tensor_tensor(out=junk64[:], in0=lam_t[:, q * 128:q * 128 + 64],
                                                         in1=lam_t[:, q * 128 + 64:q * 128 + 128], op=ALU.mult),
                   reads=[Bs["lam_t"]], writes=[Bs["junk64"]])
                op("dve", lambda e, q=q: e.tensor_reduce(out=lsc[:, q:q + 1], in_=junk64[:], axis=mybir.AxisListType.X, op=ALU.add),
                   reads=[Bs["junk64"]], writes=[Bs["lsc"]])
            if SUB == 6:
                rec.flush()
                return nc
            op("act", lambda e: e.activation(out=lsc[:, 2:4], in_=lsc[:, 0:2], func=AF.Exp), reads=[Bs["lsc"]], writes=[Bs["lsc"]])
            if SUB == 7:
                rec.flush()
                return nc
            op("dve", lambda e: e.tensor_tensor(out=lsc[:, 4:5], in0=lsc[:, 3:4], in1=lsc[:, 2:3], op=ALU.subtract),
               reads=[Bs["lsc"]], writes=[Bs["lsc"]])
            op("dve", lambda e: e.tensor_scalar(out=lamneg[:], in0=lsc[:, 4:5], scalar1=-LAM_INIT, scalar2=None, op0=ALU.add),
               reads=[Bs["lsc"]], writes=[B_lam])
            if SUB == 8:
                rec.flush()
                return nc
            op("dve", lambda e: e.tensor_scalar(out=gsub_s[:], in0=gsub_s[:], scalar1=1.0 - LAM_INIT, scalar2=None, op0=ALU.mult),
               reads=[B_gsub], writes=[B_gsub])
            rec.flush()
            if STOP == 0:
                return nc

        def norm_transpose_group(es_bufs, src_rows, ntile, gvec, shvec, hT_dst, B_hT_dst):
            (xt, Bxt, dxt, xn, Bxn, st, Bst, ptr, Bptr) = es_bufs
            for t in range(ntile):
                i = norm_transpose_group.ctr % 2
                norm_transpose_group.ctr += 1
                dma("sp", lambda e, i=i, t=t: e.dma_start(out=xt[i][:], in_=src_rows[t * 128:(t + 1) * 128, :]), dxt[i],
                    writes=[Bxt[i]])
                op("act", lambda e, i=i: e.activation(out=xn[i][:], in_=xt[i][:], func=AF.Square, accum_out=st[i][:, 0:1]),
                   reads=[Bxt[i]], writes=[Bxn[i], Bst[i]])
                op("dve", lambda e, i=i: e.tensor_scalar(out=st[i][:, 1:2], in0=st[i][:, 0:1], scalar1=1.0 / D, scalar2=EPS,
                                                         op0=ALU.mult, op1=ALU.add), reads=[Bst[i]], writes=[Bst[i]])
                op("pool", lambda e, i=i: e.tensor_tensor(out=st[i][:, 2:3], in0=st[i][:, 1:2], in1=nhalf[:, 0:1], op=ALU.pow),
                   reads=[Bst[i], B_nhalf], writes=[Bst[i]])
                op("act", lambda e, i=i: e.activation(out=xn[i][:], in_=xt[i][:], func=AF.Copy, scale=st[i][:, 2:3]),
                   reads=[Bxt[i], Bst[i]], writes=[Bxn[i]])
                for half in range(2):
                    for k8 in range(8):
                        kc = half * 8 + k8
                        op("pe", lambda e, i=i, half=half, k8=k8, kc=kc: e.transpose(
                            out=ptr[half][:, k8, :], in_=xn[i][:, kc * 128:(kc + 1) * 128], identity=ident_bf[:]),
                           reads=[Bxn[i], B_idb], writes=[Bptr[half]], signal=(k8 == 7))
                    for k8 in range(8):
                        kc = half * 8 + k8
                        op("dve", lambda e, half=half, k8=k8, kc=kc, t=t: e.tensor_scalar(
                            out=hT_dst[:, kc, t * 128:(t + 1) * 128], in0=ptr[half][:, k8, :],
                            scalar1=vecs[:, gvec, kc:kc + 1], scalar2=vecs[:, shvec, kc:kc + 1], op0=ALU.mult, op1=ALU.add),
                           reads=[Bptr[half], B_vecs], writes=[B_hT_dst])

        norm_transpose_group.ctr = 0

        def alloc_norm_bufs(es):
            xt = [sbt(es, "xt%d" % i, [128, D], F32) for i in range(2)]
            xn = [sbt(es, "xn%d" % i, [128, D], BF16) for i in range(2)]
            st = [sbt(es, "nst%d" % i, [128, 4], F32) for i in range(2)]
            ptr = [pst(es, "ptr%d" % i, [128, 8, 128], BF16) for i in range(2)]
            return (xt, [Buf() for _ in range(2)], [rec.dsem() for _ in range(2)], xn, [Buf() for _ in range(2)],
                    st, [Buf() for _ in range(2)], ptr, [PBuf() for _ in range(2)])

        def rope_evac(pk, Bpk, ntok, tabC, tabS, Btab, ksb, Bksb, pk2, Bpk2, t1, Bt1, t2, Bt2, kout, Bkout):
            op("act", lambda e: e.activation(out=ksb[:, 0:ntok], in_=pk[:, 0:ntok], func=AF.Copy), reads=[Bpk], writes=[Bksb])
            op("pe", lambda e: e.matmul(pk2[:, 0:ntok], lhsT=perm[:], rhs=ksb[:, 0:ntok], start=True, stop=True),
               reads=[B_perm, Bksb], writes=[Bpk2])
            op("dve", lambda e: e.tensor_tensor(out=t1[:, 0:ntok], in0=pk[:, 0:ntok], in1=tabC, op=ALU.mult),
               reads=[Bpk, Btab], writes=[Bt1])
            op("dve", lambda e: e.tensor_tensor(out=t2[:, 0:ntok], in0=pk2[:, 0:ntok], in1=tabS, op=ALU.mult),
               reads=[Bpk2, Btab], writes=[Bt2])
            op("pool", lambda e: e.tensor_tensor(out=kout, in0=t1[:, 0:ntok], in1=t2[:, 0:ntok], op=ALU.add),
               reads=[Bt1, Bt2], writes=[Bkout])

        w_in_v = w_in.rearrange("(k p) n -> p k n", p=128)

        with ExitStack() as es:
            nb = alloc_norm_bufs(es)
            wkv = sbt(es, "wkv", [128, 16, 2048], BF16)
            B_wkv = Buf()
            d_wkv = rec.dsem()
            hT_g = [sbt(es, "hT_g%d" % i, [128, 16, 512], BF16) for i in range(2)]
            B_hTg = [Buf() for _ in range(2)]
            d_hst = [rec.dsem() for _ in range(2)]
            rC = [sbt(es, "rC%d" % i, [128, 512], F32) for i in range(2)]
            rS = [sbt(es, "rS%d" % i, [128, 512], F32) for i in range(2)]
            B_r = [Buf() for _ in range(2)]
            d_r = [rec.dsem() for _ in range(2)]
            ksb = sbt(es, "ksb", [128, 512], BF16)
            t1 = sbt(es, "t1", [128, 512], F32)
            t2 = sbt(es, "t2", [128, 512], F32)
            B_ksb, B_t1, B_t2 = Buf(), Buf(), Buf()
            kout = [sbt(es, "kout%d" % i, [128, 512], BF16) for i in range(2)]
            B_kout = [Buf() for _ in range(2)]
            d_kout = [rec.dsem() for _ in range(2)]
            vbuf = sbt(es, "vbuf", [128, NH, 4, VW], BF16)
            B_vbuf = Buf()
            d_vbuf = rec.dsem()
            pk = [pst(es, "pk%d" % i, [128, 512], F32) for i in range(2)]
            B_pk = [PBuf() for _ in range(2)]
            pk2 = pst(es, "pk2", [128, 512], F32)
            B_pk2 = PBuf()
            pv = [pst(es, "pv%d" % i, [128, 512], F32) for i in range(2)]
            B_pv = [PBuf() for _ in range(2)]

            for half in range(2):
                dma("pool", lambda e, half=half: e.dma_start(out=wkv[:, :, half * 1024:(half + 1) * 1024],
                                                             in_=w_in_v[:, :, 1024 + half * 1024:2048 + half * 1024]),
                    d_wkv, writes=[B_wkv])
            op("dve", lambda e: e.memset(vbuf[:, :, :, 128:VW], 1.0), writes=[B_vbuf])

            groups = [("lat", g) for g in range(8)] + [("ctx", 0), ("own", 0), ("own", 1)]
            kctr = 0
            pctr = 0
            for gi, (kind, g) in enumerate(groups):
                i = gi % 2
                ntile = 2 if kind == "ctx" else 4
                ntok = ntile * 128
                if kind == "lat":
                    src = xb[g * 512:(g + 1) * 512, :]
                    gv, sv = V_G1, V_SH1
                elif kind == "ctx":
                    src = ctxb
                    gv, sv = V_G1C, V_SH1C
                else:
                    src = xown[g * 512:(g + 1) * 512, :]
                    gv, sv = V_G1, V_SH1
                if kind == "lat":
                    dma("sp", lambda e, i=i, g=g: e.dma_start(out=rC[i][:], in_=ropeC[:, g * 512:(g + 1) * 512]), d_r[i],
                        writes=[B_r[i]])
                    dma("sp", lambda e, i=i, g=g: e.dma_start(out=rS[i][:], in_=ropeS[:, g * 512:(g + 1) * 512]), d_r[i],
                        writes=[B_r[i]])
                norm_transpose_group(nb, src, ntile, gv, sv, hT_g[i], B_hTg[i])
                hidx = gi
                dma("sp", lambda e, i=i, hidx=hidx, ntok=ntok: e.dma_start(
                    out=hT_s[hidx].rearrange("p (k t) -> p k t", t=512)[:, :, 0:ntok], in_=hT_g[i][:, :, 0:ntok]),
                    d_hst[i], reads=[B_hTg[i]], writes=[B_hT[hidx]])
                if SUB == 10:
                    rec.flush()
                    return nc
                if kind == "own":
                    continue
                tok0 = g * 512 if kind == "lat" else SEQ
                for h in range(NH):
                    pi = pctr % 2
                    pctr += 1
                    for kc in range(16):
                        op("pe", lambda e, i=i, pi=pi, h=h, kc=kc, ntok=ntok: e.matmul(
                            pk[pi][:, 0:ntok], lhsT=wkv[:, kc, h * 128:(h + 1) * 128], rhs=hT_g[i][:, kc, 0:ntok],
                            start=(kc == 0), stop=(kc == 15)),
                           reads=[B_wkv, B_hTg[i]], writes=[B_pk[pi]], signal=(kc == 15))
                    ko = kctr % 2
                    kctr += 1
                    if kind == "lat":
                        rope_evac(pk[pi], B_pk[pi], ntok, rC[i][:], rS[i][:], B_r[i], ksb, B_ksb, pk2, B_pk2, t1, B_t1, t2, B_t2,
                                  kout[ko][:, 0:ntok], B_kout[ko])
                    else:
                        op("act", lambda e, pi=pi, ko=ko, ntok=ntok: e.activation(out=kout[ko][:, 0:ntok], in_=pk[pi][:, 0:ntok],
                                                                                  func=AF.Copy),
                           reads=[B_pk[pi]], writes=[B_kout[ko]])
                    dma("sp", lambda e, ko=ko, h=h, tok0=tok0, ntok=ntok: e.dma_start(
                        out=kT_s[h, :, tok0:tok0 + ntok], in_=kout[ko][:, 0:ntok]), d_kout[ko],
                        reads=[B_kout[ko]], writes=[B_kT])
                if SUB == 11:
                    rec.flush()
                    return nc
                for t in range(ntile):
                    for cc in range(2):
                        pi = pctr % 2
                        pctr += 1
                        for kc in range(16):
                            op("pe", lambda e, i=i, pi=pi, t=t, cc=cc, kc=kc: e.matmul(
                                pv[pi][:], lhsT=hT_g[i][:, kc, t * 128:(t + 1) * 128],
                                rhs=wkv[:, kc, 1024 + cc * 512:1024 + (cc + 1) * 512], start=(kc == 0), stop=(kc == 15)),
                               reads=[B_wkv, B_hTg[i]], writes=[B_pv[pi]], signal=(kc == 15))
                        op("act", lambda e, pi=pi, t=t, cc=cc: e.activation(
                            out=vbuf[:, cc * 4:(cc + 1) * 4, t, 0:128], in_=pv[pi][:].rearrange("p (h c) -> p h c", c=128),
                            func=AF.Copy), reads=[B_pv[pi]], writes=[B_vbuf])
                if SUB == 12:
                    rec.flush()
                    return nc
                tile0 = tok0 // 128
                for h in range(NH):
                    dma("sp", lambda e, h=h, tile0=tile0, ntile=ntile: e.dma_start(
                        out=vA_s[h].rearrange("p (t c) -> p t c", c=VW)[:, tile0:tile0 + ntile, :],
                        in_=vbuf[:, h, 0:ntile, :]), d_vbuf, reads=[B_vbuf], writes=[B_vA])
            rec.flush()
            if STOP == 1:
                return nc

        with ExitStack() as es:
            wqf = sbt(es, "wqf", [128, 16, 2048], BF16)
            B_wqf = Buf()
            d_wqf = rec.dsem()
            hT_g = [sbt(es, "hT2_g%d" % i, [128, 16, 512], BF16) for i in range(2)]
            B_hTg = [Buf() for _ in range(2)]
            d_hl = [rec.dsem() for _ in range(2)]
            rC = [sbt(es, "rC2%d" % i, [128, 512], F32) for i in range(2)]
            rS = [sbt(es, "rS2%d" % i, [128, 512], F32) for i in range(2)]
            B_r = [Buf() for _ in range(2)]
            d_r = [rec.dsem() for _ in range(2)]
            ksb = sbt(es, "ksb2", [128, 512], BF16)
            t1 = sbt(es, "t12", [128, 512], F32)
            t2 = sbt(es, "t22", [128, 512], F32)
            B_ksb, B_t1, B_t2 = Buf(), Buf(), Buf()
            kout = [sbt(es, "qout%d" % i, [128, 512], BF16) for i in range(2)]
            B_kout = [Buf() for _ in range(2)]
            d_kout = [rec.dsem() for _ in range(2)]
            dftg = sbt(es, "dftg_s", [128, 2, 512], BF16)
            B_dftg = Buf()
            fT = [sbt(es, "fT%d" % i, [128, 8, 512], BF16) for i in range(2)]
            B_fT = [Buf() for _ in range(2)]
            ubuf = [sbt(es, "ubuf%d" % i, [128, 4, 4, 512], BF16) for i in range(2)]
            B_ubuf = [Buf() for _ in range(2)]
            d_ubuf = [rec.dsem() for _ in range(2)]
            pk = [pst(es, "pq%d" % i, [128, 512], F32) for i in range(2)]
            B_pk = [PBuf() for _ in range(2)]
            pk2 = pst(es, "pq2", [128, 512], F32)
            B_pk2 = PBuf()
            pu = [pst(es, "pu%d" % i, [128, 512], F32) for i in range(2)]
            B_pu = [PBuf() for _ in range(2)]

            dma("pool", lambda e: e.dma_start(out=wqf[:, :, 0:1024], in_=w_in_v[:, :, 0:1024]), d_wqf, writes=[B_wqf])
            dma("pool", lambda e: e.dma_start(out=wqf[:, :, 1024:2048], in_=w_in_v[:, :, 3072:4096]), d_wqf, writes=[B_wqf])
            dma("sp", lambda e: e.dma_start(out=dftg[:], in_=dftg_d), d_wqf, writes=[B_dftg])
            groups = [("own", 0), ("own", 1)] + [("lat", g) for g in range(8)]
            pctr = 0
            kctr = 0
            for gi, (kind, g) in enumerate(groups):
                i = gi % 2
                hidx = (9 + g) if kind == "own" else g
                dma("sp", lambda e, i=i, hidx=hidx: e.dma_start(out=hT_g[i][:], in_=hT_s[hidx].rearrange("p (k t) -> p k t", t=512)),
                    d_hl[i], reads=[B_hT[hidx]], writes=[B_hTg[i]])
                if kind == "own":
                    dma("sp", lambda e, i=i, g=g: e.dma_start(out=rC[i][:], in_=ropeCo[:, g * 512:(g + 1) * 512]), d_r[i],
                        writes=[B_r[i]])
                    dma("sp", lambda e, i=i, g=g: e.dma_start(out=rS[i][:], in_=ropeSo[:, g * 512:(g + 1) * 512]), d_r[i],
                        writes=[B_r[i]])
                    for h in range(NH):
                        pi = pctr % 2
                        pctr += 1
                        for kc in range(16):
                            op("pe", lambda e, i=i, pi=pi, h=h, kc=kc: e.matmul(
                                pk[pi][:], lhsT=wqf[:, kc, h * 128:(h + 1) * 128], rhs=hT_g[i][:, kc, :],
                                start=(kc == 0), stop=(kc == 15)),
                               reads=[B_wqf, B_hTg[i]], writes=[B_pk[pi]], signal=(kc == 15))
                        ko = kctr % 2
                        kctr += 1
                        rope_evac(pk[pi], B_pk[pi], 512, rC[i][:], rS[i][:], B_r[i], ksb, B_ksb, pk2, B_pk2, t1, B_t1, t2, B_t2,
                                  kout[ko][:], B_kout[ko])
                        dma("sp", lambda e, ko=ko, h=h, g=g: e.dma_start(
                            out=qT_s[:, h * OWN + g * 512:h * OWN + (g + 1) * 512], in_=kout[ko][:]), d_kout[ko],
                            reads=[B_kout[ko]], writes=[B_qT])
                    continue
                fi = g % 2
                for ct in range(8):
                    pi = pctr % 2
                    pctr += 1
                    for kc in range(16):
                        op("pe", lambda e, i=i, pi=pi, ct=ct, kc=kc: e.matmul(
                            pk[pi][:], lhsT=wqf[:, kc, 1024 + ct * 128:1024 + (ct + 1) * 128], rhs=hT_g[i][:, kc, :],
                            start=(kc == 0), stop=(kc == 15)),
                           reads=[B_wqf, B_hTg[i]], writes=[B_pk[pi]], signal=(kc == 15))
                    op("act", lambda e, pi=pi, fi=fi, ct=ct: e.activation(out=fT[fi][:, ct, :], in_=pk[pi][:], func=AF.Copy),
                       reads=[B_pk[pi]], writes=[B_fT[fi]])
                for gg in range(4):
                    for t in range(4):
                        ui = (gg * 4 + t) % 2
                        for k2 in range(2):
                            op("pe", lambda e, fi=fi, ui=ui, gg=gg, t=t, k2=k2: e.matmul(
                                pu[ui][:], lhsT=fT[fi][:, gg * 2 + k2, t * 128:(t + 1) * 128], rhs=dftg[:, k2, :],
                                start=(k2 == 0), stop=(k2 == 1)),
                               reads=[B_fT[fi], B_dftg], writes=[B_pu[ui]], signal=(k2 == 1))
                        op("dve", lambda e, fi=fi, ui=ui, gg=gg, t=t: e.tensor_copy(out=ubuf[fi][:, gg, t, :], in_=pu[ui][:]),
                           reads=[B_pu[ui]], writes=[B_ubuf[fi]])
                for gg in range(4):
                    dma("sp", lambda e, fi=fi, gg=gg, g=g: e.dma_start(
                        out=u_s[gg].rearrange("p (t c) -> p t c", c=512)[:, g * 4:(g + 1) * 4, :], in_=ubuf[fi][:, gg, :, :]),
                        d_ubuf[fi], reads=[B_ubuf[fi]], writes=[B_u])
            rec.flush()
            if STOP == 2:
                return nc

        with ExitStack() as es:
            qT = sbt(es, "qT", [128, NH, OWN], BF16)
            B_q = Buf()
            d_q = rec.dsem()
            kTh = [sbt(es, "kTh%d" % i, [128, NKEY], BF16) for i in range(2)]
            vAh = [sbt(es, "vAh%d" % i, [128, NKT, VW], BF16) for i in range(2)]
            B_kv = [Buf() for _ in range(2)]
            d_kv = [rec.dsem() for _ in range(2)]
            NE = 3
            E = [[sbt(es, "E%d_%d" % (i, c), [128, 512], BF16) for c in range(2)] for i in range(NE)]
            B_E = [[Buf() for c in range(2)] for i in range(NE)]
            osb = sbt(es, "osb", [128, 128], F32)
            o2 = sbt(es, "o2", [128, 128], F32)
            junk = sbt(es, "ajunk", [128, 128], F32)
            abf = sbt(es, "abf", [128, 128], BF16)
            sc = sbt(es, "asc", [128, 8], F32)
            B_osb, B_o2, B_junk, B_abf, B_sc = [Buf() for _ in range(5)]
            aT = sbt(es, "aT", [128, NH, OWN], BF16)
            B_aTs = Buf()
            d_a = rec.dsem()
            pS = [[pst(es, "pS%d_%d" % (i, c), [128, 512], F32) for c in range(2)] for i in range(2)]
            B_pS = [[PBuf() for c in range(2)] for i in range(2)]
            pO = [pst(es, "pO%d" % i, [128, 512], F32) for i in range(3)]
            B_pO = [PBuf() for _ in range(3)]
            pT = pst(es, "pT", [128, 8, 128], BF16)
            B_pT = PBuf()

            dma("sp", lambda e: e.dma_start(out=qT[:], in_=qT_s.rearrange("p (h t) -> p h t", t=OWN)), d_q, reads=[B_qT],
                writes=[B_q])

            def load_kv(h):
                i = h % 2
                dma("sp", lambda e, i=i, h=h: e.dma_start(out=kTh[i][:], in_=kT_s[h]), d_kv[i], reads=[B_kT], writes=[B_kv[i]])
                dma("sp", lambda e, i=i, h=h: e.dma_start(out=vAh[i][:], in_=vA_s[h].rearrange("p (t c) -> p t c", c=VW)),
                    d_kv[i], reads=[B_vA], writes=[B_kv[i]])

            ocp = [[sbt(es, "ocp%d_%d" % (i, b), [128, 3 * VW], F32) for b in range(3)] for i in range(2)]
            B_ocp = [[Buf() for b in range(3)] for i in range(2)]
            load_kv(0)
            steps = [(h, qc, kt) for h in range(NH) for qc in range(2) for kt in range(NKT)]
            NS = len(steps)

            def emit_qk(s):
                h, qc, kt = steps[s]
                hi, si, q0 = h % 2, s % 2, qc * 512
                for c in range(2):
                    op("pe", lambda e, hi=hi, si=si, c=c, kt=kt, h=h, q0=q0: e.matmul(
                        pS[si][c][:], lhsT=kTh[hi][c * 64:(c + 1) * 64, kt * 128:(kt + 1) * 128],
                        rhs=qT[c * 64:(c + 1) * 64, h, q0:q0 + 512], start=True, stop=True),
                       reads=[B_kv[hi], B_q], writes=[B_pS[si][c]])

            def emit_exp(s):
                si, ei = s % 2, s % NE
                for c in range(2):
                    op("act", lambda e, si=si, ei=ei, c=c: e.activation(out=E[ei][c][:], in_=pS[si][c][:], func=AF.Exp),
                       reads=[B_pS[si][c]], writes=[B_E[ei][c]])

            def emit_av(s):
                h, qc, kt = steps[s]
                hi, ei = h % 2, s % NE
                for c in range(2):
                    for qs in range(4):
                        idx = c * 4 + qs
                        bank, pos = idx // 3, idx % 3
                        op("pe", lambda e, ei=ei, c=c, qs=qs, bank=bank, pos=pos, hi=hi, kt=kt: e.matmul(
                            pO[bank][:, pos * VW:(pos + 1) * VW], lhsT=E[ei][c][:, qs * 128:(qs + 1) * 128],
                            rhs=vAh[hi][:, kt, :], start=(kt == 0 and pos == 0), stop=(kt == NKT - 1),
                            skip_group_check=True),
                           reads=[B_E[ei][c], B_kv[hi]], writes=[B_pO[bank]], signal=(qs == 3))

            ectr = [0, 0]

            def emit_epi(h, qc):
                q0 = qc * 512
                oi = ectr[0] % 2
                ectr[0] += 1
                for b in range(3):
                    op("dve", lambda e, oi=oi, b=b: e.tensor_copy(out=ocp[oi][b][:], in_=pO[b][:, 0:3 * VW]),
                       reads=[B_pO[b]], writes=[B_ocp[oi][b]])
                for qs in range(4):
                    b0, p0 = qs // 3, qs % 3
                    b1, p1 = (4 + qs) // 3, (4 + qs) % 3
                    O0 = ocp[oi][b0][:, p0 * VW:p0 * VW + 128]
                    S0 = ocp[oi][b0][:, p0 * VW + 128:p0 * VW + 129]
                    O1 = ocp[oi][b1][:, p1 * VW:p1 * VW + 128]
                    S1 = ocp[oi][b1][:, p1 * VW + 128:p1 * VW + 129]
                    R0, R1 = B_ocp[oi][b0], B_ocp[oi][b1]
                    op("dve", lambda e, S0=S0: e.reciprocal(out=sc[:, 0:1], in_=S0), reads=[R0], writes=[B_sc])
                    op("dve", lambda e, S1=S1: e.reciprocal(out=sc[:, 1:2], in_=S1), reads=[R1], writes=[B_sc])
                    op("dve", lambda e: e.tensor_tensor(out=sc[:, 2:3], in0=sc[:, 1:2], in1=lamneg[:], op=ALU.mult),
                       reads=[B_sc, B_lam], writes=[B_sc])
                    op("dve", lambda e, O0=O0: e.tensor_scalar(out=osb[:], in0=O0, scalar1=sc[:, 0:1], scalar2=None, op0=ALU.mult),
                       reads=[R0, B_sc], writes=[B_osb])
                    op("dve", lambda e, O1=O1: e.scalar_tensor_tensor(out=o2[:], in0=O1, scalar=sc[:, 2:3], in1=osb[:],
                                                                      op0=ALU.mult, op1=ALU.add),
                       reads=[R1, B_sc, B_osb], writes=[B_o2])
                    op("dve", lambda e: e.tensor_tensor(out=junk[:], in0=o2[:], in1=o2[:], op=ALU.mult),
                       reads=[B_o2], writes=[B_junk])
                    op("dve", lambda e: e.tensor_reduce(out=sc[:, 3:4], in_=junk[:], axis=mybir.AxisListType.X, op=ALU.add),
                       reads=[B_junk], writes=[B_sc])
                    op("dve", lambda e: e.tensor_scalar(out=sc[:, 4:5], in0=sc[:, 3:4], scalar1=1.0 / 128, scalar2=EPS,
                                                        op0=ALU.mult, op1=ALU.add), reads=[B_sc], writes=[B_sc])
                    op("pool", lambda e: e.tensor_tensor(out=sc[:, 5:6], in0=sc[:, 4:5], in1=nhalf[:, 0:1], op=ALU.pow),
                       reads=[B_sc, B_nhalf], writes=[B_sc])
                    op("dve", lambda e: e.tensor_scalar(out=abf[:], in0=o2[:], scalar1=sc[:, 5:6], scalar2=None, op0=ALU.mult),
                       reads=[B_o2, B_sc], writes=[B_abf])
                    ti = ectr[1] % 8
                    ectr[1] += 1
                    op("pe", lambda e, ti=ti: e.transpose(out=pT[:, ti, :], in_=abf[:], identity=ident_bf[:]),
                       reads=[B_abf, B_idb], writes=[B_pT])
                    op("dve", lambda e, ti=ti, h=h, q0=q0, qs=qs: e.tensor_scalar(
                        out=aT[:, h, q0 + qs * 128:q0 + (qs + 1) * 128], in0=pT[:, ti, :], scalar1=gsub_s[:, 0:1], scalar2=None,
                        op0=ALU.mult), reads=[B_pT, B_gsub], writes=[B_aTs])

            emit_qk(0)
            emit_qk(1)
            for s in range(NS):
                h, qc, kt = steps[s]
                if qc == 0 and kt == 0 and h + 1 < NH:
                    load_kv(h + 1)
                emit_exp(s)
                emit_av(s)
                if s + 2 < NS:
                    emit_qk(s + 2)
                if kt == NKT - 1:
                    emit_epi(h, qc)
            dma("sp", lambda e: e.dma_start(out=aT_s.rearrange("p (h t) -> p h t", t=OWN), in_=aT[:]), d_a, reads=[B_aTs],
                writes=[B_aT])
            rec.flush()
            if STOP == 3:
                return nc

        with ExitStack() as es:
            tab = sbt(es, "ptab", [128, 32, 2, 512], BF16)
            B_tab = Buf()
            d_tab = rec.dsem()
            ug = [sbt(es, "ug%d" % i, [128, 32, 512], BF16) for i in range(2)]
            B_ug = [Buf() for _ in range(2)]
            d_ug = [rec.dsem() for _ in range(2)]
            yfT = sbt(es, "yfT", [128, 8, OWN], BF16)
            B_yf = Buf()
            d_yf = rec.dsem()
            py = [pst(es, "pyf%d" % i, [128, 512], F32) for i in range(2)]
            B_py = [PBuf() for _ in range(2)]
            uctr = 0
            pctr = 0
            for kch in range(2):
                dma("sp", lambda e, kch=kch: e.dma_start(out=tab[:], in_=dftp_d[kch].rearrange("p (t c k) -> p t c k", c=2, k=512)),
                    d_tab, writes=[B_tab])
                for gg in range(4):
                    ui = uctr % 2
                    uctr += 1
                    dma("sp", lambda e, ui=ui, gg=gg: e.dma_start(out=ug[ui][:], in_=u_s[gg].rearrange("p (t c) -> p t c", c=512)),
                        d_ug[ui], reads=[B_u], writes=[B_ug[ui]])
                    for half in range(2):
                        pi = pctr % 2
                        pctr += 1
                        for tt in range(32):
                            for cs in range(2):
                                op("pe", lambda e, ui=ui, pi=pi, half=half, tt=tt, cs=cs: e.matmul(
                                    py[pi][:], lhsT=ug[ui][:, tt, cs * 256 + half * 128:cs * 256 + (half + 1) * 128],
                                    rhs=tab[:, tt, cs, :], start=(tt == 0 and cs == 0), stop=(tt == 31 and cs == 1)),
                                   reads=[B_ug[ui], B_tab], writes=[B_py[pi]], signal=(tt == 31 and cs == 1))
                        op("act", lambda e, pi=pi, gg=gg, half=half, kch=kch: e.activation(
                            out=yfT[:, gg * 2 + half, kch * 512:(kch + 1) * 512], in_=py[pi][:], func=AF.Copy),
                           reads=[B_py[pi]], writes=[B_yf])
            dma("sp", lambda e: e.dma_start(out=yfT_s.rearrange("p (c t) -> p c t", t=OWN), in_=yfT[:]), d_yf, reads=[B_yf],
                writes=[B_yfT])
            rec.flush()
            if STOP == 4:
                return nc

        with ExitStack() as es:
            hTo = sbt(es, "hTo", [128, 16, OWN], BF16)
            aT = sbt(es, "aT5", [128, 8, OWN], BF16)
            yfT = sbt(es, "yfT5", [128, 8, OWN], BF16)
            B_in = Buf()
            d_in = rec.dsem()
            mT = sbt(es, "mT", [128, 16, OWN], BF16)
            B_m = Buf()
            d_m = rec.dsem()
            wb = [sbt(es, "wb5_%d" % i, [128, 48, 256], BF16) for i in range(2)]
            B_wb = [Buf() for _ in range(2)]
            d_wb = [rec.dsem() for _ in range(2)]
            sg = [sbt(es, "sg%d" % i, [128, 512], F32) for i in range(2)]
            m1 = [sbt(es, "m1_%d" % i, [128, 512], F32) for i in range(2)]
            B_sg = [Buf() for _ in range(2)]
            B_m1 = [Buf() for _ in range(2)]
            pp = [[pst(es, "p5_%d_%d" % (i, k), [128, 512], F32) for k in range(4)] for i in range(2)]
            B_pp = [[PBuf() for k in range(4)] for i in range(2)]

            for g in range(2):
                dma("sp", lambda e, g=g: e.dma_start(out=hTo[:, :, g * 512:(g + 1) * 512],
                                                      in_=hT_s[9 + g].rearrange("p (k t) -> p k t", t=512)), d_in,
                    reads=[B_hT[9 + g]], writes=[B_in])
            dma("sp", lambda e: e.dma_start(out=aT[:], in_=aT_s.rearrange("p (h t) -> p h t", t=OWN)), d_in, reads=[B_aT],
                writes=[B_in])
            dma("sp", lambda e: e.dma_start(out=yfT[:], in_=yfT_s.rearrange("p (c t) -> p c t", t=OWN)), d_in, reads=[B_yfT],
                writes=[B_in])
            w_abr_v = w_abr.rearrange("(k p) n -> p k n", p=128)
            w_fbr_v = w_fbr.rearrange("(k p) n -> p k n", p=128)

            def load_wb(nbk):
                i = nbk % 2
                c0 = nbk * 256
                dma("pool", lambda e, i=i, c0=c0: e.dma_start(out=wb[i][:, 0:8, :], in_=w_abr_v[:, :, c0:c0 + 256]), d_wb[i],
                    writes=[B_wb[i]])
                dma("pool", lambda e, i=i, c0=c0: e.dma_start(out=wb[i][:, 8:16, :], in_=w_fbr_v[:, :, c0:c0 + 256]), d_wb[i],
                    writes=[B_wb[i]])
                dma("pool", lambda e, i=i, c0=c0: e.dma_start(out=wb[i][:, 16:32, :], in_=w_in_v[:, :, 4096 + c0:4096 + c0 + 256]),
                    d_wb[i], writes=[B_wb[i]])
                dma("pool", lambda e, i=i, c0=c0: e.dma_start(out=wb[i][:, 32:48, :], in_=w_in_v[:, :, 6144 + c0:6144 + c0 + 256]),
                    d_wb[i], writes=[B_wb[i]])

            load_wb(0)
            pctr = 0
            for nbk in range(8):
                if nbk + 1 < 8:
                    load_wb(nbk + 1)
                wi = nbk % 2
                for j in range(2):
                    n = nbk * 2 + j
                    for ch in range(2):
                        pi = pctr % 2
                        pctr += 1
                        specs = [(0, 8, aT), (8, 8, yfT), (16, 16, hTo), (32, 16, hTo)]
                        for k, (w0, nk, src) in enumerate(specs):
                            for kc in range(nk):
                                op("pe", lambda e, wi=wi, pi=pi, k=k, w0=w0, kc=kc, nk=nk, src=src, j=j, ch=ch: e.matmul(
                                    pp[pi][k][:], lhsT=wb[wi][:, w0 + kc, j * 128:(j + 1) * 128],
                                    rhs=src[:, kc, ch * 512:(ch + 1) * 512], start=(kc == 0), stop=(kc == nk - 1)),
                                   reads=[B_wb[wi], B_in], writes=[B_pp[pi][k]], signal=(kc == nk - 1))
                        op("act", lambda e, pi=pi: e.activation(out=sg[0][:], in_=pp[pi][2][:], func=AF.Sigmoid),
                           reads=[B_pp[pi][2]], writes=[B_sg[0]])
                        op("act", lambda e, pi=pi: e.activation(out=sg[1][:], in_=pp[pi][3][:], func=AF.Sigmoid),
                           reads=[B_pp[pi][3]], writes=[B_sg[1]])
                        op("dve", lambda e, pi=pi: e.tensor_tensor(out=m1[0][:], in0=pp[pi][0][:], in1=sg[0][:], op=ALU.mult),
                           reads=[B_pp[pi][0], B_sg[0]], writes=[B_m1[0]])
                        op("dve", lambda e, pi=pi: e.tensor_tensor(out=m1[1][:], in0=pp[pi][1][:], in1=sg[1][:], op=ALU.mult),
                           reads=[B_pp[pi][1], B_sg[1]], writes=[B_m1[1]])
                        op("pool", lambda e, n=n, ch=ch: e.tensor_tensor(out=mT[:, n, ch * 512:(ch + 1) * 512], in0=m1[0][:],
                                                                         in1=m1[1][:], op=ALU.add),
                           reads=[B_m1[0], B_m1[1]], writes=[B_m])
            dma("sp", lambda e: e.dma_start(out=mT_s.rearrange("p (k t) -> p k t", t=OWN), in_=mT[:]), d_m, reads=[B_m],
                writes=[B_mT])
            rec.flush()
            if STOP == 5:
                return nc

        with ExitStack() as es:
            mT = sbt(es, "mT6", [128, 16, OWN], BF16)
            B_m = Buf()
            d_m = rec.dsem()
            xT = sbt(es, "xT6", [128, 16, OWN], F32)
            B_xT = Buf()
            d_x = rec.dsem()
            xt = [sbt(es, "xt6_%d" % i, [128, D], F32) for i in range(2)]
            B_xt = [Buf() for _ in range(2)]
            d_xt = [rec.dsem() for _ in range(2)]
            wo = [sbt(es, "wo%d" % i, [128, 16, 256], BF16) for i in range(2)]
            B_wo = [Buf() for _ in range(2)]
            d_wo = [rec.dsem() for _ in range(2)]
            sq = [sbt(es, "sq%d" % i, [128, 512], BF16) for i in range(2)]
            B_sq = [Buf() for _ in range(2)]
            ms = sbt(es, "ms6", [128, 512], F32)
            rstd = sbt(es, "rstd6", [128, 512], F32)
            tmp = [sbt(es, "tmp6_%d" % i, [128, 512], F32) for i in range(2)]
            B_ms, B_rstd = Buf(), Buf()
            B_tmp = [Buf() for _ in range(2)]
            h2s = [sbt(es, "h2s%d" % i, [128, 512], BF16) for i in range(2)]
            B_h2s = [Buf() for _ in range(2)]
            d_h2 = [rec.dsem() for _ in range(2)]
            ptx = [pst(es, "ptx%d" % i, [128, 4, 128], F32) for i in range(2)]
            B_ptx = [PBuf() for _ in range(2)]
            pyo = [pst(es, "pyo%d" % i, [128, 512], F32) for i in range(2)]
            B_pyo = [PBuf() for _ in range(2)]
            pss = pst(es, "pss", [128, 512], F32)
            B_pss = PBuf()

            dma("sp", lambda e: e.dma_start(out=mT[:], in_=mT_s.rearrange("p (k t) -> p k t", t=OWN)), d_m, reads=[B_mT],
                writes=[B_m])
            w_out_v = w_out.rearrange("(k p) n -> p k n", p=128)

            def load_wo(nbk):
                i = nbk % 2
                dma("pool", lambda e, i=i, nbk=nbk: e.dma_start(out=wo[i][:], in_=w_out_v[:, :, nbk * 256:(nbk + 1) * 256]),
                    d_wo[i], writes=[B_wo[i]])

            load_wo(0)
            tcn = 0
            for t in range(8):
                i = t % 2
                dma("sp", lambda e, i=i, t=t: e.dma_start(out=xt[i][:], in_=xown[t * 128:(t + 1) * 128, :]), d_xt[i],
                    writes=[B_xt[i]])
                for k4 in range(4):
                    pi = tcn % 2
                    tcn += 1
                    for k in range(4):
                        kc = k4 * 4 + k
                        op("pe", lambda e, i=i, pi=pi, k=k, kc=kc: e.transpose(out=ptx[pi][:, k, :], in_=xt[i][:, kc * 128:(kc + 1) * 128],
                                                                                identity=ident_f[:]),
                           reads=[B_xt[i], B_idf], writes=[B_ptx[pi]], signal=(k == 3))
                    op("act", lambda e, pi=pi, k4=k4, t=t: e.activation(out=xT[:, k4 * 4:(k4 + 1) * 4, t * 128:(t + 1) * 128],
                                                                         in_=ptx[pi][:], func=AF.Copy),
                       reads=[B_ptx[pi]], writes=[B_xT])
            pctr = 0
            for nbk in range(8):
                if nbk + 1 < 8:
                    load_wo(nbk + 1)
                wi = nbk % 2
                for j in range(2):
                    n = nbk * 2 + j
                    for ch in range(2):
                        pi = pctr % 2
                        pctr += 1
                        for kc in range(16):
                            op("pe", lambda e, wi=wi, pi=pi, kc=kc, j=j, ch=ch: e.matmul(
                                pyo[pi][:], lhsT=wo[wi][:, kc, j * 128:(j + 1) * 128], rhs=mT[:, kc, ch * 512:(ch + 1) * 512],
                                start=(kc == 0), stop=(kc == 15)),
                               reads=[B_wo[wi], B_m], writes=[B_pyo[pi]], signal=(kc == 15))
                        op("dve", lambda e, pi=pi, n=n, ch=ch: e.scalar_tensor_tensor(
                            out=xT[:, n, ch * 512:(ch + 1) * 512], in0=pyo[pi][:], scalar=vecs[:, V_GT1, n:n + 1],
                            in1=xT[:, n, ch * 512:(ch + 1) * 512], op0=ALU.mult, op1=ALU.add),
                           reads=[B_pyo[pi], B_vecs, B_xT], writes=[B_xT])
            dma("sp", lambda e: e.dma_start(out=xmid_s.rearrange("p (k t) -> p k t", t=OWN), in_=xT[:]), d_x, reads=[B_xT],
                writes=[B_xmid])
            hctr = 0
            for ch in range(2):
                for n in range(16):
                    si = n % 2
                    op("act", lambda e, si=si, n=n, ch=ch: e.activation(out=sq[si][:], in_=xT[:, n, ch * 512:(ch + 1) * 512],
                                                                         func=AF.Square), reads=[B_xT], writes=[B_sq[si]])
                    op("pe", lambda e, si=si, n=n: e.matmul(pss[:], lhsT=ones_bf[:], rhs=sq[si][:], start=(n == 0), stop=(n == 15)),
                       reads=[B_ones, B_sq[si]], writes=[B_pss])
                op("dve", lambda e: e.tensor_scalar(out=ms[:], in0=pss[:], scalar1=1.0 / D, scalar2=EPS, op0=ALU.mult, op1=ALU.add),
                   reads=[B_pss], writes=[B_ms])
                op("pool", lambda e: e.tensor_tensor(out=rstd[:], in0=ms[:], in1=nhalf[:], op=ALU.pow),
                   reads=[B_ms, B_nhalf], writes=[B_rstd])
                for n in range(16):
                    ti = hctr % 2
                    hctr += 1
                    op("dve", lambda e, ti=ti, n=n, ch=ch: e.tensor_tensor(out=tmp[ti][:], in0=xT[:, n, ch * 512:(ch + 1) * 512],
                                                                           in1=rstd[:], op=ALU.mult),
                       reads=[B_xT, B_rstd], writes=[B_tmp[ti]])
                    op("dve", lambda e, ti=ti, n=n: e.tensor_scalar(out=h2s[ti][:], in0=tmp[ti][:], scalar1=vecs[:, V_G2, n:n + 1],
                                                                    scalar2=vecs[:, V_SH2, n:n + 1], op0=ALU.mult, op1=ALU.add),
                       reads=[B_tmp[ti], B_vecs], writes=[B_h2s[ti]])
                    dma("sp", lambda e, ti=ti, n=n, ch=ch: e.dma_start(
                        out=h2T_s[:, n * OWN + ch * 512:n * OWN + (ch + 1) * 512], in_=h2s[ti][:]), d_h2[ti],
                        reads=[B_h2s[ti]], writes=[B_h2T])
            rec.flush()
            if STOP == 6:
                return nc

        with ExitStack() as es:
            zT = sbt(es, "zT", [128, 64, OWN], BF16)
            B_z = Buf()
            with ExitStack() as es1:
                h2T = sbt(es1, "h2T", [128, 16, OWN], BF16)
                B_h2 = Buf()
                d_h2l = rec.dsem()
                w1 = [sbt(es1, "w1_%d" % i, [128, 16, 512], BF16) for i in range(2)]
                B_w1 = [Buf() for _ in range(2)]
                d_w1 = [rec.dsem() for _ in range(2)]
                rl = [sbt(es1, "rl%d" % i, [128, 512], F32) for i in range(2)]
                B_rl = [Buf() for _ in range(2)]
                pz = [pst(es1, "pz%d" % i, [128, 512], F32) for i in range(4)]
                B_pz = [PBuf() for _ in range(4)]
                dma("sp", lambda e: e.dma_start(out=h2T[:], in_=h2T_s.rearrange("p (k t) -> p k t", t=OWN)), d_h2l,
                    reads=[B_h2T], writes=[B_h2])
                w_m1_v = w_m1.rearrange("(k p) n -> p k n", p=128)

                def load_w1(cb):
                    i = cb % 2
                    dma("pool", lambda e, i=i, cb=cb: e.dma_start(out=w1[i][:], in_=w_m1_v[:, :, cb * 512:(cb + 1) * 512]),
                        d_w1[i], writes=[B_w1[i]])

                load_w1(0)
                pctr = 0
                for cb in range(16):
                    if cb + 1 < 16:
                        load_w1(cb + 1)
                    wi = cb % 2
                    for j in range(4):
                        f = cb * 4 + j
                        for ch in range(2):
                            pi = pctr % 4
                            ri = pctr % 2
                            pctr += 1
                            for kc in range(16):
                                op("pe", lambda e, wi=wi, pi=pi, kc=kc, j=j, ch=ch: e.matmul(
                                    pz[pi][:], lhsT=w1[wi][:, kc, j * 128:(j + 1) * 128], rhs=h2T[:, kc, ch * 512:(ch + 1) * 512],
                                    start=(kc == 0), stop=(kc == 15)),
                                   reads=[B_w1[wi], B_h2], writes=[B_pz[pi]], signal=(kc == 15))
                            op("act", lambda e, pi=pi, ri=ri: e.activation(out=rl[ri][:], in_=pz[pi][:], func=AF.Relu),
                               reads=[B_pz[pi]], writes=[B_rl[ri]])
                            eng = "dve" if (pctr % 2 == 0) else "pool"
                            op(eng, lambda e, ri=ri, f=f, ch=ch: e.tensor_tensor(out=zT[:, f, ch * 512:(ch + 1) * 512], in0=rl[ri][:],
                                                                                  in1=rl[ri][:], op=ALU.mult),
                               reads=[B_rl[ri]], writes=[B_z])
                rec.flush()
                if STOP == 7:
                    return nc
            with ExitStack() as es2:
                w2 = [sbt(es2, "w2_%d" % i, [128, 64, 128], BF16) for i in range(3)]
                B_w2 = [Buf() for _ in range(3)]
                d_w2 = [rec.dsem() for _ in range(3)]
                xm = [sbt(es2, "xm%d" % i, [128, 512], F32) for i in range(2)]
                B_xm = [Buf() for _ in range(2)]
                d_xm = [rec.dsem() for _ in range(2)]
                xo = [sbt(es2, "xo%d" % i, [128, 512], F32) for i in range(2)]
                B_xo = [Buf() for _ in range(2)]
                d_xo = [rec.dsem() for _ in range(2)]
                po = [pst(es2, "po%d" % i, [128, 512], F32) for i in range(2)]
                B_po = [PBuf() for _ in range(2)]
                w_m2_v = w_m2.rearrange("(k p) n -> p k n", p=128)

                def load_w2(n):
                    i = n % 3
                    for q in range(2):
                        dma("pool", lambda e, i=i, n=n, q=q: e.dma_start(
                            out=w2[i][:, q * 32:(q + 1) * 32, :], in_=w_m2_v[:, q * 32:(q + 1) * 32, n * 128:(n + 1) * 128]),
                            d_w2[i], writes=[B_w2[i]])

                load_w2(0)
                load_w2(1)
                pctr = 0
                for n in range(16):
                    if n + 2 < 16:
                        load_w2(n + 2)
                    wi = n % 3
                    for ch in range(2):
                        pi = pctr % 2
                        pctr += 1
                        dma("sp", lambda e, pi=pi, n=n, ch=ch: e.dma_start(
                            out=xm[pi][:], in_=xmid_s[:, n * OWN + ch * 512:n * OWN + (ch + 1) * 512]), d_xm[pi],
                            reads=[B_xmid], writes=[B_xm[pi]])
                        for kc in range(64):
                            op("pe", lambda e, wi=wi, pi=pi, kc=kc, ch=ch: e.matmul(
                                po[pi][:], lhsT=w2[wi][:, kc, :], rhs=zT[:, kc, ch * 512:(ch + 1) * 512],
                                start=(kc == 0), stop=(kc == 63)),
                               reads=[B_w2[wi], B_z], writes=[B_po[pi]], signal=(kc == 63))
                        op("dve", lambda e, pi=pi, n=n: e.scalar_tensor_tensor(
                            out=xo[pi][:], in0=po[pi][:], scalar=vecs[:, V_GT2, n:n + 1], in1=xm[pi][:],
                            op0=ALU.mult, op1=ALU.add), reads=[B_po[pi], B_vecs, B_xm[pi]], writes=[B_xo[pi]])
                        dma("sp", lambda e, pi=pi, n=n, ch=ch: e.dma_start(
                            out=xout_s[:, n * OWN + ch * 512:n * OWN + (ch + 1) * 512], in_=xo[pi][:]), d_xo[pi],
                            reads=[B_xo[pi]], writes=[B_xout])
                rec.flush()
                if STOP == 8:
                    return nc

        with ExitStack() as es:
            gf = sbt(es, "gf", [128, D], F32)
            B_gf = Buf()
            d_gf = rec.dsem()
            xl = [sbt(es, "xl%d" % i, [128, 16, 128], F32) for i in range(2)]
            B_xl = [Buf() for _ in range(2)]
            d_xl = [rec.dsem() for _ in range(2)]
            junk = sbt(es, "fjunk", [128, 512], BF16)
            B_junk = Buf()
            st = [sbt(es, "fst%d" % i, [128, 8], F32) for i in range(2)]
            B_st = [Buf() for _ in range(2)]
            ot = [sbt(es, "ot%d" % i, [128, D], F32) for i in range(2)]
            B_ot = [Buf() for _ in range(2)]
            d_ot = [rec.dsem() for _ in range(2)]
            pf = [[pst(es, "pf%d_%d" % (i, k), [128, 4, 128], F32) for k in range(4)] for i in range(2)]
            B_pf = [[PBuf() for k in range(4)] for i in range(2)]
            dma("sp", lambda e: e.dma_start(out=gf[:], in_=gfin.partition_broadcast(128)), d_gf, writes=[B_gf])
            xout_v = xout_s.rearrange("p (k t) -> p k t", t=OWN)
            for t in range(8):
                i = t % 2
                dma("sp", lambda e, i=i, t=t: e.dma_start(out=xl[i][:], in_=xout_v[:, :, t * 128:(t + 1) * 128]), d_xl[i],
                    reads=[B_xout], writes=[B_xl[i]])
                for k4 in range(4):
                    for k in range(4):
                        kc = k4 * 4 + k
                        op("pe", lambda e, i=i, k4=k4, k=k, kc=kc: e.transpose(out=pf[i][k4][:, k, :], in_=xl[i][:, kc, :],
                                                                                identity=ident_f[:]),
                           reads=[B_xl[i], B_idf], writes=[B_pf[i][k4]], signal=(k == 3))
                    op("act", lambda e, i=i, k4=k4: e.activation(out=junk[:], in_=pf[i][k4][:].rearrange("p a b -> p (a b)"),
                                                                 func=AF.Square, accum_out=st[i][:, k4:k4 + 1]),
                       reads=[B_pf[i][k4]], writes=[B_junk, B_st[i]])
                op("dve", lambda e, i=i: e.tensor_reduce(out=st[i][:, 4:5], in_=st[i][:, 0:4], axis=mybir.AxisListType.X, op=ALU.add),
                   reads=[B_st[i]], writes=[B_st[i]])
                op("dve", lambda e, i=i: e.tensor_scalar(out=st[i][:, 5:6], in0=st[i][:, 4:5], scalar1=1.0 / D, scalar2=EPS,
                                                         op0=ALU.mult, op1=ALU.add), reads=[B_st[i]], writes=[B_st[i]])
                op("pool", lambda e, i=i: e.tensor_tensor(out=st[i][:, 6:7], in0=st[i][:, 5:6], in1=nhalf[:, 0:1], op=ALU.pow),
                   reads=[B_st[i], B_nhalf], writes=[B_st[i]])
                for k4 in range(4):
                    op("dve", lambda e, i=i, k4=k4: e.scalar_tensor_tensor(
                        out=ot[i][:, k4 * 512:(k4 + 1) * 512], in0=pf[i][k4][:].rearrange("p a b -> p (a b)"), scalar=st[i][:, 6:7],
                        in1=gf[:, k4 * 512:(k4 + 1) * 512], op0=ALU.mult, op1=ALU.mult),
                       reads=[B_pf[i][k4], B_st[i], B_gf], writes=[B_ot[i]])
                dma("sp", lambda e, i=i, t=t: e.dma_start(out=out_d[t * 128:(t + 1) * 128, :], in_=ot[i][:]), d_ot[i],
                    reads=[B_ot[i]], writes=[Buf()])
            rec.flush()
            if STOP == 9:
                return nc
    return nc


_NC_CACHE = {}


def _get_nc():
    if "nc" not in _NC_CACHE:
        _NC_CACHE["nc"] = build_nc()
    return _NC_CACHE["nc"]


def make_in_maps(x, c, ctx, c_ctx, w_ada, b_ada, g_norm1, w_in, lam_q1, lam_k1, lam_q2, lam_k2,
                 g_subln, w_attn_br, w_four_br, w_out, g_norm2, w_mlp_in, w_mlp_out, g_final):
    f32 = np.float32
    A = lambda a: np.ascontiguousarray(np.asarray(a, dtype=f32))
    cst = _consts()
    x = A(x)
    ctx = A(ctx)
    c = A(c)
    shared = {
        "bada_r": A(b_ada)[0].reshape(96, 128),
        "lamv": np.concatenate([A(lam_q1)[0], A(lam_k1)[0], A(lam_q2)[0], A(lam_k2)[0]]),
        "gsub": A(g_subln)[0].reshape(128, 1),
        "gfin": A(g_final),
        "w_ada": A(w_ada)[0],
        "w_in": A(w_in)[0],
        "w_abr": A(w_attn_br)[0],
        "w_fbr": A(w_four_br)[0],
        "w_out": A(w_out)[0],
        "w_m1": A(w_mlp_in)[0],
        "w_m2": A(w_mlp_out)[0],
        "ropeC": cst["ropeC"],
        "ropeS": cst["ropeS"],
        "perm": cst["perm"],
        "ident_bf": cst["ident_bf"],
        "ident_f": cst["ident_f"],
        "dftg": cst["dftg"],
    }
    in_maps = []
    for core in range(8):
        b, j = core // 4, core % 4
        m = dict(shared)
        m["xb"] = x[b]
        m["ctxb"] = ctx[b]
        m["xown"] = np.ascontiguousarray(x[b, j * OWN:(j + 1) * OWN])
        m["small_r"] = np.concatenate([c[b].reshape(16, 128), A(c_ctx).reshape(16, 128), A(g_norm1)[0].reshape(16, 128),
                                       A(g_norm2)[0].reshape(16, 128)], axis=0)
        m["ropeCo"] = np.ascontiguousarray(cst["ropeC"][:, j * OWN:(j + 1) * OWN] * f32(0.125))
        m["ropeSo"] = np.ascontiguousarray(cst["ropeS"][:, j * OWN:(j + 1) * OWN] * f32(0.125))
        m["dftp"] = cst["dftp"][j].reshape(2, 128, 32 * 2 * 512)
        in_maps.append(m)
    return in_maps


def kernel(**inputs):
    in_maps = make_in_maps(**inputs)
    nc = _get_nc()
    res = run_bass_kernel_spmd(nc, in_maps, core_ids=list(range(8)))
    out = np.empty((2, SEQ, D), np.float32)
    for core in range(8):
        b, j = core // 4, core % 4
        out[b, j * OWN:(j + 1) * OWN] = res.results[core]["out"]
    return out
```

```python
import math
from contextlib import ExitStack

import numpy as np
import ml_dtypes

import concourse.bass as bass
import concourse.mybir as mybir
from concourse.bass_utils import run_bass_kernel_spmd

F32 = mybir.dt.float32
BF16 = mybir.dt.bfloat16
AF = mybir.ActivationFunctionType
ALU = mybir.AluOpType

D = 2048
SEQ = 4096
CTX = 256
NKEY = SEQ + CTX
NKT = NKEY // 128
OWN = 1024
NH = 8
DFF = 8192
EPS = 1e-6
LAM_INIT = 0.8 - 0.6 * math.exp(0.0)
VW = 130

DEBUG = False
STOP = -1
SUB = -1


class Sem:
    __slots__ = ("h", "count")

    def __init__(self, h):
        self.h = h
        self.count = 0


class Buf:
    __slots__ = ("name", "w", "r", "x")

    def __init__(self, name="", x=False):
        self.name = name
        self.w = {}
        self.r = {}
        self.x = x


def PBuf():
    return Buf("psum", True)


ENGS = ("pe", "act", "dve", "pool", "sp")


class Rec:
    def __init__(self, nc):
        self.nc = nc
        self.esem = {e: Sem(nc.alloc_semaphore("es_" + e)) for e in ENGS}
        self.known = {e: {} for e in ENGS}
        self.stream = {e: [] for e in ENGS}
        self.dsems = []
        self.dset = set()
        self.nds = 0

    def dsem(self):
        s = Sem(self.nc.alloc_semaphore("ds%d" % self.nds))
        self.nds += 1
        self.dsems.append(s)
        self.dset.add(s)
        return s

    def _waits(self, e, reads, writes, dwrites=()):
        deps = {}
        own = self.esem[e]

        def add(k, v):
            if deps.get(k, 0) < v:
                deps[k] = v

        for b in reads:
            for k, v in b.w.items():
                add(k, v)
            if b.x:
                for k, v in b.r.items():
                    if k is not own:
                        add(k, v)
        for b in writes:
            for k, v in b.w.items():
                add(k, v)
            for k, v in b.r.items():
                add(k, v)
        for b in dwrites:
            for k, v in b.r.items():
                add(k, v)
        kn = self.known[e]
        for k, v in deps.items():
            if e == "pe" and k is own:
                continue
            if k in self.dset:
                v = k.count
            if kn.get(k, 0) >= v:
                continue
            kn[k] = v
            self.stream[e].append(lambda eng, h=k.h, v=v: eng.wait_ge(h, v))

    def _post(self, ev, reads, writes, dwrites=()):
        k, v = ev
        for b in reads:
            if b.r.get(k, 0) < v:
                b.r[k] = v
        for b in writes:
            b.w = {k: v}
            b.r = {}
        for b in dwrites:
            if b.w.get(k, 0) < v:
                b.w[k] = v

    def op(self, e, fn, reads=(), writes=(), signal=True, dwrites=()):
        self._waits(e, reads, writes, dwrites)
        s = self.esem[e]
        if signal:
            s.count += 1
            v = s.count
            self.stream[e].append(lambda eng, fn=fn, h=s.h: fn(eng).then_inc(h, 1))
        else:
            v = s.count + 1
            self.stream[e].append(lambda eng, fn=fn: fn(eng))
        self._post((s, v), reads, writes, dwrites)

    def dma(self, q, fn, ds, reads=(), writes=(), dwrites=()):
        self._waits(q, reads, writes, dwrites)
        ds.count += 16
        self.stream[q].append(lambda eng, fn=fn, h=ds.h: fn(eng).then_inc(h, 16))
        self._post((ds, ds.count), reads, writes, dwrites)

    def flush(self):
        for q in ("sp",):
            kn = self.known[q]
            for s in self.dsems:
                if s.count > 0 and kn.get(s, 0) < s.count:
                    kn[s] = s.count
                    self.stream[q].append(lambda eng, h=s.h, v=s.count: eng.wait_ge(h, v))
        st = self.stream
        with self.nc.Block() as blk:
            @blk.tensor
            def _(e):
                for f in st["pe"]:
                    f(e)

            @blk.scalar
            def _(e):
                for f in st["act"]:
                    f(e)

            @blk.vector
            def _(e):
                for f in st["dve"]:
                    f(e)

            @blk.gpsimd
            def _(e):
                for f in st["pool"]:
                    f(e)

            @blk.sync
            def _(e):
                for f in st["sp"]:
                    f(e)
        self.stream = {e: [] for e in ENGS}


def _rope_tables(pos):
    pos = np.asarray(pos, dtype=np.float64)
    row = pos // 64
    col = pos % 64
    inv = 10000.0 ** (-(np.arange(0, 32, 2, dtype=np.float64)) / 32.0)
    C = np.zeros((128, len(pos)), np.float64)
    S = np.zeros((128, len(pos)), np.float64)
    for p in range(128):
        d = p % 64
        pp = row if d < 32 else col
        f = inv[d % 16]
        C[p] = np.cos(pp * f)
        sgn = -1.0 if (d % 32) < 16 else 1.0
        S[p] = sgn * np.sin(pp * f)
    return C, S


def _perm_matrix():
    P = np.zeros((128, 128), np.float32)
    for m in range(128):
        d = m % 64
        pm = m + 16 if (d % 32) < 16 else m - 16
        P[pm, m] = 1.0
    return P


_CONST_CACHE = {}


def _consts():
    if _CONST_CACHE:
        return _CONST_CACHE
    bf = ml_dtypes.bfloat16
    C, S = _rope_tables(np.arange(SEQ))
    _CONST_CACHE["ropeC"] = C.astype(np.float32)
    _CONST_CACHE["ropeS"] = S.astype(np.float32)
    _CONST_CACHE["perm"] = _perm_matrix().astype(bf)
    _CONST_CACHE["ident_bf"] = np.eye(128, dtype=np.float32).astype(bf)
    _CONST_CACHE["ident_f"] = np.eye(128, dtype=np.float32)
    j = np.arange(256, dtype=np.float64)
    ang = 2.0 * np.pi * np.outer(j, j) / 256.0
    cg = np.cos(ang) / 16.0
    sg = -np.sin(ang) / 16.0
    tab = np.concatenate([cg, sg], axis=1)
    _CONST_CACHE["dftg"] = np.ascontiguousarray(tab.reshape(2, 128, 512).transpose(1, 0, 2)).astype(bf)
    t = np.arange(SEQ, dtype=np.int64)
    dftp = []
    for jq in range(4):
        k = np.arange(jq * OWN, (jq + 1) * OWN, dtype=np.int64)
        prod = (np.outer(t, k) % SEQ).astype(np.float64)
        ang = 2.0 * np.pi * prod / SEQ
        c = (np.cos(ang) / 64.0).astype(np.float32)
        s = (np.sin(ang) / 64.0).astype(np.float32)
        cs = np.stack([c, s], axis=1)
        cs = cs.reshape(32, 128, 2, 2, 512)
        cs = cs.transpose(3, 1, 0, 2, 4)
        dftp.append(np.ascontiguousarray(cs).astype(bf))
    _CONST_CACHE["dftp"] = dftp
    return _CONST_CACHE


def build_nc():
    nc = bass.Bass("TRN2", target_bir_lowering=False)
    rec = Rec(nc)

    def din(name, shape, dt=F32):
        return nc.dram_tensor(name, list(shape), dt, kind="ExternalInput").ap()

    skind = "ExternalOutput" if DEBUG else "Internal"

    def dscr(name, shape, dt):
        return nc.dram_tensor(name, list(shape), dt, kind=skind).ap()

    xb = din("xb", [SEQ, D])
    ctxb = din("ctxb", [CTX, D])
    xown = din("xown", [OWN, D])
    small_r = din("small_r", [64, 128])
    bada_r = din("bada_r", [96, 128])
    lamv = din("lamv", [256])
    gsub = din("gsub", [128, 1])
    gfin = din("gfin", [D])
    w_ada = din("w_ada", [D, 6 * D])
    w_in = din("w_in", [D, 8192])
    w_abr = din("w_abr", [1024, D])
    w_fbr = din("w_fbr", [1024, D])
    w_out = din("w_out", [D, D])
    w_m1 = din("w_m1", [D, DFF])
    w_m2 = din("w_m2", [DFF, D])
    ropeC = din("ropeC", [128, SEQ])
    ropeS = din("ropeS", [128, SEQ])
    ropeCo = din("ropeCo", [128, OWN])
    ropeSo = din("ropeSo", [128, OWN])
    perm_d = din("perm", [128, 128], BF16)
    identb_d = din("ident_bf", [128, 128], BF16)
    identf_d = din("ident_f", [128, 128], F32)
    dftg_d = din("dftg", [128, 2, 512], BF16)
    dftp_d = din("dftp", [2, 128, 32 * 2 * 512], BF16)
    out_d = nc.dram_tensor("out", [OWN, D], F32, kind="ExternalOutput").ap()

    hT_s = dscr("hT_s", [11, 128, 16 * 512], BF16)
    kT_s = dscr("kT_s", [NH, 128, NKEY], BF16)
    vA_s = dscr("vA_s", [NH, 128, NKT * VW], BF16)
    u_s = dscr("u_s", [4, 128, 32 * 512], BF16)
    qT_s = dscr("qT_s", [128, NH * OWN], BF16)
    aT_s = dscr("aT_s", [128, NH * OWN], BF16)
    yfT_s = dscr("yfT_s", [128, 8 * OWN], BF16)
    mT_s = dscr("mT_s", [128, 16 * OWN], BF16)
    xmid_s = dscr("xmid_s", [128, 16 * OWN], F32)
    h2T_s = dscr("h2T_s", [128, 16 * OWN], BF16)
    xout_s = dscr("xout_s", [128, 16 * OWN], F32)
    B_hT = [Buf("hT_s%d" % i) for i in range(11)]
    B_kT = Buf("kT_s")
    B_vA = Buf("vA_s")
    B_u = Buf("u_s")
    B_qT = Buf("qT_s")
    B_aT = Buf("aT_s")
    B_yfT = Buf("yfT_s")
    B_mT = Buf("mT_s")
    B_xmid = Buf("xmid_s")
    B_h2T = Buf("h2T_s")
    B_xout = Buf("xout_s")

    op = rec.op
    dma = rec.dma

    with ExitStack() as top:
        def sbt(es, name, shape, dt):
            return es.enter_context(nc.sbuf_tensor(name, list(shape), dt))

        def pst(es, name, shape, dt):
            return es.enter_context(nc.psum_tensor(name, list(shape), dt))

        vecs = sbt(top, "vecs", [128, 8, 16], F32)
        V_G1, V_SH1, V_G1C, V_SH1C, V_GT1, V_G2, V_SH2, V_GT2 = range(8)
        lamneg = sbt(top, "lamneg", [128, 1], F32)
        gsub_s = sbt(top, "gsub_s", [128, 1], F32)
        nhalf = sbt(top, "nhalf", [128, 512], F32)
        ident_bf = sbt(top, "ident_bf_s", [128, 128], BF16)
        ident_f = sbt(top, "ident_f_s", [128, 128], F32)
        perm = sbt(top, "perm_s", [128, 128], BF16)
        ones_bf = sbt(top, "ones_bf", [128, 128], BF16)
        B_vecs, B_lam, B_gsub, B_nhalf, B_idb, B_idf, B_perm, B_ones = [Buf() for _ in range(8)]

        with ExitStack() as es:
            bada_t = sbt(es, "bada_t", [96, 128], F32)
            small_t = sbt(es, "small_t", [64, 128], F32)
            lam_t = sbt(es, "lam_t", [128, 256], F32)
            junk64 = sbt(es, "junk64", [128, 64], F32)
            badaT = sbt(es, "badaT", [128, 96], F32)
            smallT = sbt(es, "smallT", [128, 64], F32)
            s_bf = sbt(es, "s_bf", [128, 16, 2], BF16)
            modT = sbt(es, "modT", [128, 96, 2], F32)
            lsc = sbt(es, "lsc", [128, 8], F32)
            wblk = [sbt(es, "wadab%d" % i, [128, 16, 512], BF16) for i in range(3)]
            p_tr1 = pst(es, "p_tr1", [128, 512], F32)
            p_tr2 = pst(es, "p_tr2", [128, 512], F32)
            p_mod = pst(es, "p_mod", [128, 512], F32)
            Bs = {n: Buf(n) for n in ("bada_t", "small_t", "lam_t", "junk64", "badaT", "smallT", "s_bf",
                                       "modT", "lsc", "p_tr1", "p_tr2", "p_mod")}
            for n_ in ("p_tr1", "p_tr2", "p_mod"):
                Bs[n_].x = True
            Bw = [Buf("wblk%d" % i) for i in range(3)]
            dw = [rec.dsem() for _ in range(3)]
            d0 = rec.dsem()

            dma("sp", lambda e: e.dma_start(out=bada_t[:], in_=bada_r), d0, writes=[Bs["bada_t"]])
            dma("sp", lambda e: e.dma_start(out=small_t[:], in_=small_r), d0, writes=[Bs["small_t"]])
            dma("sp", lambda e: e.dma_start(out=ident_f[:], in_=identf_d), d0, writes=[B_idf])
            dma("sp", lambda e: e.dma_start(out=ident_bf[:], in_=identb_d), d0, writes=[B_idb])
            dma("sp", lambda e: e.dma_start(out=perm[:], in_=perm_d), d0, writes=[B_perm])
            dma("sp", lambda e: e.dma_start(out=lam_t[:], in_=lamv.partition_broadcast(128)), d0,
                writes=[Bs["lam_t"]])
            dma("sp", lambda e: e.dma_start(out=gsub_s[:], in_=gsub), d0, writes=[B_gsub])
            op("dve", lambda e: e.memset(nhalf[:], -0.5), writes=[B_nhalf])
            op("dve", lambda e: e.memset(ones_bf[:], 1.0), writes=[B_ones])

            if SUB == 1:
                rec.flush()
                return nc
            w_ada_v = w_ada.rearrange("(k p) n -> p k n", p=128)
            NB = 24

            def load_wada(cb):
                i = cb % 3
                dma("pool", lambda e, i=i, cb=cb: e.dma_start(out=wblk[i][:], in_=w_ada_v[:, :, cb * 512:(cb + 1) * 512]),
                    dw[i], writes=[Bw[i]])

            load_wada(0)
            load_wada(1)

            op("pe", lambda e: e.transpose(out=p_tr1[:, 0:96], in_=bada_t[:], identity=ident_f[0:96, 0:96]),
               reads=[Bs["bada_t"], B_idf], writes=[Bs["p_tr1"]])
            op("pe", lambda e: e.transpose(out=p_tr2[:, 0:64], in_=small_t[:], identity=ident_f[0:64, 0:64]),
               reads=[Bs["small_t"], B_idf], writes=[Bs["p_tr2"]])
            op("dve", lambda e: e.tensor_copy(out=badaT[:], in_=p_tr1[:, 0:96]), reads=[Bs["p_tr1"]], writes=[Bs["badaT"]])
            op("dve", lambda e: e.tensor_copy(out=smallT[:], in_=p_tr2[:, 0:64]), reads=[Bs["p_tr2"]], writes=[Bs["smallT"]])
            if SUB == 2:
                rec.flush()
                return nc
            for v in range(2):
                op("act", lambda e, v=v: e.activation(out=s_bf[:, :, v], in_=smallT[:, v * 16:(v + 1) * 16], func=AF.Silu),
                   reads=[Bs["smallT"]], writes=[Bs["s_bf"]])

            if SUB == 3:
                rec.flush()
                return nc
            for cb in range(NB):
                if cb + 2 < NB:
                    load_wada(cb + 2)
                i = cb % 3
                for j in range(4):
                    n = cb * 4 + j
                    for kc in range(16):
                        op("pe", lambda e, i=i, j=j, n=n, kc=kc: e.matmul(
                            p_mod[:, n * 2:(n + 1) * 2], lhsT=wblk[i][:, kc, j * 128:(j + 1) * 128], rhs=s_bf[:, kc, :],
                            start=(kc == 0), stop=(kc == 15)),
                           reads=[Bw[i], Bs["s_bf"]], writes=[Bs["p_mod"]], signal=(kc == 15 and j == 3))

            if SUB == 4:
                rec.flush()
                return nc
            for v in range(2):
                op("dve", lambda e, v=v: e.tensor_tensor(out=modT[:, :, v], in0=p_mod[:, v:192:2], in1=badaT[:], op=ALU.add),
                   reads=[Bs["p_mod"], Bs["badaT"]], writes=[Bs["modT"]])
            g1T = smallT[:, 32:48]
            g2T = smallT[:, 48:64]
            RB = [Bs["modT"], Bs["smallT"]]
            op("dve", lambda e: e.scalar_tensor_tensor(out=vecs[:, V_G1, :], in0=modT[:, 16:32, 0], scalar=1.0, in1=g1T,
                                                       op0=ALU.add, op1=ALU.mult), reads=RB, writes=[B_vecs])
            op("dve", lambda e: e.scalar_tensor_tensor(out=vecs[:, V_G1C, :], in0=modT[:, 16:32, 1], scalar=1.0, in1=g1T,
                                                       op0=ALU.add, op1=ALU.mult), reads=RB, writes=[B_vecs])
            op("dve", lambda e: e.scalar_tensor_tensor(out=vecs[:, V_G2, :], in0=modT[:, 64:80, 0], scalar=1.0, in1=g2T,
                                                       op0=ALU.add, op1=ALU.mult), reads=RB, writes=[B_vecs])
            op("dve", lambda e: e.tensor_copy(out=vecs[:, V_SH1, :], in_=modT[:, 0:16, 0]), reads=RB, writes=[B_vecs])
            op("dve", lambda e: e.tensor_copy(out=vecs[:, V_SH1C, :], in_=modT[:, 0:16, 1]), reads=RB, writes=[B_vecs])
            op("dve", lambda e: e.tensor_copy(out=vecs[:, V_GT1, :], in_=modT[:, 32:48, 0]), reads=RB, writes=[B_vecs])
            op("dve", lambda e: e.tensor_copy(out=vecs[:, V_SH2, :], in_=modT[:, 48:64, 0]), reads=RB, writes=[B_vecs])
            op("dve", lambda e: e.tensor_copy(out=vecs[:, V_GT2, :], in_=modT[:, 80:96, 0]), reads=RB, writes=[B_vecs])
            if SUB == 5:
                rec.flush()
                return nc
            for q in range(2):
                op("dve", lambda e, q=q: e.tensor_tensor(out=junk64[:], in0=lam_t[:, q * 128:q * 128 + 64],
                                                         in1=lam_t[:, q * 128 + 64:q * 128 + 128], op=ALU.mult),
                   reads=[Bs["lam_t"]], writes=[Bs["junk64"]])
                op("dve", lambda e, q=q: e.tensor_reduce(out=lsc[:, q:q + 1], in_=junk64[:], axis=mybir.AxisListType.X, op=ALU.add),
                   reads=[Bs["junk64"]], writes=[Bs["lsc"]])
            if SUB == 6:
                rec.flush()
                return nc
            op("act", lambda e: e.activation(out=lsc[:, 2:4], in_=lsc[:, 0:2], func=AF.Exp), reads=[Bs["lsc"]], writes=[Bs["lsc"]])
            if SUB == 7:
                rec.flush()
                return nc
            op("dve", lambda e: e.tensor_tensor(out=lsc[:, 4:5], in0=lsc[:, 3:4], in1=lsc[:, 2:3], op=ALU.subtract),
               reads=[Bs["lsc"]], writes=[Bs["lsc"]])
            op("dve", lambda e: e.tensor_scalar(out=lamneg[:], in0=lsc[:, 4:5], scalar1=-LAM_INIT, scalar2=None, op0=ALU.add),
               reads=[Bs["lsc"]], writes=[B_lam])
            if SUB == 8:
                rec.flush()
                return nc
            op("dve", lambda e: e.tensor_scalar(out=gsub_s[:], in0=gsub_s[:], scalar1=1.0 - LAM_INIT, scalar2=None, op0=ALU.mult),
               reads=[B_gsub], writes=[B_gsub])
            rec.flush()
            if STOP == 0:
                return nc

        def norm_transpose_group(es_bufs, src_rows, ntile, gvec, shvec, hT_dst, B_hT_dst):
            (xt, Bxt, dxt, xn, Bxn, st, Bst, ptr, Bptr) = es_bufs
            for t in range(ntile):
                i = norm_transpose_group.ctr % 2
                norm_transpose_group.ctr += 1
                dma("sp", lambda e, i=i, t=t: e.dma_start(out=xt[i][:], in_=src_rows[t * 128:(t + 1) * 128, :]), dxt[i],
                    writes=[Bxt[i]])
                op("act", lambda e, i=i: e.activation(out=xn[i][:], in_=xt[i][:], func=AF.Square, accum_out=st[i][:, 0:1]),
                   reads=[Bxt[i]], writes=[Bxn[i], Bst[i]])
                op("dve", lambda e, i=i: e.tensor_scalar(out=st[i][:, 1:2], in0=st[i][:, 0:1], scalar1=1.0 / D, scalar2=EPS,
                                                         op0=ALU.mult, op1=ALU.add), reads=[Bst[i]], writes=[Bst[i]])
                op("pool", lambda e, i=i: e.tensor_tensor(out=st[i][:, 2:3], in0=st[i][:, 1:2], in1=nhalf[:, 0:1], op=ALU.pow),
                   reads=[Bst[i], B_nhalf], writes=[Bst[i]])
                op("act", lambda e, i=i: e.activation(out=xn[i][:], in_=xt[i][:], func=AF.Copy, scale=st[i][:, 2:3]),
                   reads=[Bxt[i], Bst[i]], writes=[Bxn[i]])
                for half in range(2):
                    for k8 in range(8):
                        kc = half * 8 + k8
                        op("pe", lambda e, i=i, half=half, k8=k8, kc=kc: e.transpose(
                            out=ptr[half][:, k8, :], in_=xn[i][:, kc * 128:(kc + 1) * 128], identity=ident_bf[:]),
                           reads=[Bxn[i], B_idb], writes=[Bptr[half]], signal=(k8 == 7))
                    for k8 in range(8):
                        kc = half * 8 + k8
                        op("dve", lambda e, half=half, k8=k8, kc=kc, t=t: e.tensor_scalar(
                            out=hT_dst[:, kc, t * 128:(t + 1) * 128], in0=ptr[half][:, k8, :],
                            scalar1=vecs[:, gvec, kc:kc + 1], scalar2=vecs[:, shvec, kc:kc + 1], op0=ALU.mult, op1=ALU.add),
                           reads=[Bptr[half], B_vecs], writes=[B_hT_dst])

        norm_transpose_group.ctr = 0

        def alloc_norm_bufs(es):
            xt = [sbt(es, "xt%d" % i, [128, D], F32) for i in range(2)]
            xn = [sbt(es, "xn%d" % i, [128, D], BF16) for i in range(2)]
            st = [sbt(es, "nst%d" % i, [128, 4], F32) for i in range(2)]
            ptr = [pst(es, "ptr%d" % i, [128, 8, 128], BF16) for i in range(2)]
            return (xt, [Buf() for _ in range(2)], [rec.dsem() for _ in range(2)], xn, [Buf() for _ in range(2)],
                    st, [Buf() for _ in range(2)], ptr, [PBuf() for _ in range(2)])

        def rope_evac(pk, Bpk, ntok, tabC, tabS, Btab, ksb, Bksb, pk2, Bpk2, t1, Bt1, t2, Bt2, kout, Bkout):
            op("act", lambda e: e.activation(out=ksb[:, 0:ntok], in_=pk[:, 0:ntok], func=AF.Copy), reads=[Bpk], writes=[Bksb])
            op("pe", lambda e: e.matmul(pk2[:, 0:ntok], lhsT=perm[:], rhs=ksb[:, 0:ntok], start=True, stop=True),
               reads=[B_perm, Bksb], writes=[Bpk2])
            op("dve", lambda e: e.tensor_tensor(out=t1[:, 0:ntok], in0=pk[:, 0:ntok], in1=tabC, op=ALU.mult),
               reads=[Bpk, Btab], writes=[Bt1])
            op("dve", lambda e: e.tensor_tensor(out=t2[:, 0:ntok], in0=pk2[:, 0:ntok], in1=tabS, op=ALU.mult),
               reads=[Bpk2, Btab], writes=[Bt2])
            op("pool", lambda e: e.tensor_tensor(out=kout, in0=t1[:, 0:ntok], in1=t2[:, 0:ntok], op=ALU.add),
               reads=[Bt1, Bt2], writes=[Bkout])

        w_in_v = w_in.rearrange("(k p) n -> p k n", p=128)

        with ExitStack() as es:
            xt = [sbt(es, "xt%d" % i, [128, D], F32) for i in range(2)]
            B_xt = [Buf() for _ in range(2)]
            d_xt = [rec.dsem() for _ in range(2)]
            xn = [sbt(es, "xn%d" % i, [128, D], BF16) for i in range(4)]
            B_xn = [Buf() for _ in range(4)]
            nst = [sbt(es, "nst%d" % i, [128, 4], F32) for i in range(4)]
            B_nst = [Buf() for _ in range(4)]
            ptr = [pst(es, "ptr%d" % i, [128, 8, 128], BF16) for i in range(3)]
            B_ptr = [PBuf() for _ in range(3)]
            wkv = sbt(es, "wkv", [128, 16, 2048], BF16)
            B_wkv = Buf()
            d_wkv = rec.dsem()
            hT_g = [sbt(es, "hT_g%d" % i, [128, 16, 512], BF16) for i in range(2)]
            B_hTg = [Buf() for _ in range(2)]
            d_hst = [rec.dsem() for _ in range(2)]
            rC = [sbt(es, "rC%d" % i, [128, 512], F32) for i in range(2)]
            rS = [sbt(es, "rS%d" % i, [128, 512], F32) for i in range(2)]
            B_r = [Buf() for _ in range(2)]
            d_r = [rec.dsem() for _ in range(2)]
            ksb = [sbt(es, "ksb%d" % i, [128, 512], BF16) for i in range(2)]
            t1 = [sbt(es, "t1_%d" % i, [128, 512], F32) for i in range(2)]
            t2 = sbt(es, "t2", [128, 512], F32)
            B_ksb = [Buf() for _ in range(2)]
            B_t1 = [Buf() for _ in range(2)]
            B_t2 = Buf()
            kout = [sbt(es, "kout%d" % i, [128, 512], BF16) for i in range(2)]
            B_kout = [Buf() for _ in range(2)]
            d_kout = [rec.dsem() for _ in range(2)]
            vbuf = sbt(es, "vbuf", [128, NH, 4, VW], BF16)
            B_vbuf = Buf()
            d_vbuf = rec.dsem()
            pk = [pst(es, "pk%d" % i, [128, 512], F32) for i in range(2)]
            B_pk = [PBuf() for _ in range(2)]
            pk2 = pst(es, "pk2", [128, 512], F32)
            B_pk2 = PBuf()
            pv = [pst(es, "pv%d" % i, [128, 512], F32) for i in range(2)]
            B_pv = [PBuf() for _ in range(2)]

            for half in range(2):
                dma("pool", lambda e, half=half: e.dma_start(out=wkv[:, :, half * 1024:(half + 1) * 1024],
                                                             in_=w_in_v[:, :, 1024 + half * 1024:2048 + half * 1024]),
                    d_wkv, writes=[B_wkv])
            op("dve", lambda e: e.memset(vbuf[:, :, :, 128:VW], 1.0), writes=[B_vbuf])

            groups = [("lat", g) for g in range(8)] + [("ctx", 0), ("own", 0), ("own", 1)]
            cn = {"x": 0, "p": 0, "ev": 0, "pk": 0, "pv": 0, "ko": 0}

            def ginfo(gi):
                kind, g = groups[gi]
                ntile = 2 if kind == "ctx" else 4
                if kind == "lat":
                    return kind, g, ntile, xb[g * 512:(g + 1) * 512, :], V_G1, V_SH1
                if kind == "ctx":
                    return kind, g, ntile, ctxb, V_G1C, V_SH1C
                return kind, g, ntile, xown[g * 512:(g + 1) * 512, :], V_G1, V_SH1

            def norm_units(gi):
                kind, g, ntile, src, gv, sv = ginfo(gi)
                hi = gi % 2
                units = []
                for t in range(ntile):
                    def part1(t=t):
                        xi = cn["x"] % 2
                        ni = cn["x"] % 4
                        cn["x"] += 1
                        dma("sp", lambda e: e.dma_start(out=xt[xi][:], in_=src[t * 128:(t + 1) * 128, :]), d_xt[xi], writes=[B_xt[xi]])
                        op("act", lambda e: e.activation(out=xn[ni][:], in_=xt[xi][:], func=AF.Square, accum_out=nst[ni][:, 0:1]),
                           reads=[B_xt[xi]], writes=[B_xn[ni], B_nst[ni]])
                        op("dve", lambda e: e.tensor_scalar(out=nst[ni][:, 1:2], in0=nst[ni][:, 0:1], scalar1=1.0 / D, scalar2=EPS,
                                                            op0=ALU.mult, op1=ALU.add), reads=[B_nst[ni]], writes=[B_nst[ni]])
                        op("pool", lambda e: e.tensor_tensor(out=nst[ni][:, 2:3], in0=nst[ni][:, 1:2], in1=nhalf[:, 0:1], op=ALU.pow),
                           reads=[B_nst[ni], B_nhalf], writes=[B_nst[ni]])
                        op("act", lambda e: e.activation(out=xn[ni][:], in_=xt[xi][:], func=AF.Copy, scale=nst[ni][:, 2:3]),
                           reads=[B_xt[xi], B_nst[ni]], writes=[B_xn[ni]])
                        return ni

                    def part2(ni, t=t):
                        for half in range(2):
                            pi = cn["p"] % 3
                            cn["p"] += 1
                            for k8 in range(8):
                                kc = half * 8 + k8
                                op("pe", lambda e, pi=pi, k8=k8, kc=kc: e.transpose(
                                    out=ptr[pi][:, k8, :], in_=xn[ni][:, kc * 128:(kc + 1) * 128], identity=ident_bf[:]),
                                   reads=[B_xn[ni], B_idb], writes=[B_ptr[pi]], signal=(k8 == 7))
                            useact = (cn["p"] % 2 == 0)
                            for k8 in range(8):
                                kc = half * 8 + k8
                                if useact:
                                    op("act", lambda e, pi=pi, k8=k8, kc=kc: e.activation(
                                        out=hT_g[hi][:, kc, t * 128:(t + 1) * 128], in_=ptr[pi][:, k8, :], func=AF.Identity,
                                        scale=vecs[:, gv, kc:kc + 1], bias=vecs[:, sv, kc:kc + 1]),
                                       reads=[B_ptr[pi], B_vecs], dwrites=[B_hTg[hi]])
                                else:
                                    op("dve", lambda e, pi=pi, k8=k8, kc=kc: e.tensor_scalar(
                                        out=hT_g[hi][:, kc, t * 128:(t + 1) * 128], in0=ptr[pi][:, k8, :],
                                        scalar1=vecs[:, gv, kc:kc + 1], scalar2=vecs[:, sv, kc:kc + 1], op0=ALU.mult, op1=ALU.add),
                                       reads=[B_ptr[pi], B_vecs], dwrites=[B_hTg[hi]])
                    units.append((part1, part2))

                def fin():
                    ntok = ntile * 128
                    dma("sp", lambda e: e.dma_start(out=hT_s[gi].rearrange("p (k t) -> p k t", t=512)[:, :, 0:ntok],
                                                    in_=hT_g[hi][:, :, 0:ntok]), d_hst[hi], reads=[B_hTg[hi]], writes=[B_hT[gi]])
                return units, fin

            def kv_units(gi):
                kind, g, ntile, src, gv, sv = ginfo(gi)
                if kind == "own":
                    return []
                hi = gi % 2
                ntok = ntile * 128
                tok0 = g * 512 if kind == "lat" else SEQ
                units = []
                if kind == "lat":
                    def ld():
                        dma("sp", lambda e: e.dma_start(out=rC[hi][:], in_=ropeC[:, g * 512:(g + 1) * 512]), d_r[hi], writes=[B_r[hi]])
                        dma("sp", lambda e: e.dma_start(out=rS[hi][:], in_=ropeS[:, g * 512:(g + 1) * 512]), d_r[hi], writes=[B_r[hi]])
                    units.append(ld)
                lag = []
                for h in range(NH):
                    def kunit(h=h):
                        pi = cn["pk"] % 2
                        cn["pk"] += 1
                        for kc in range(16):
                            op("pe", lambda e, kc=kc: e.matmul(
                                pk[pi][:, 0:ntok], lhsT=wkv[:, kc, h * 128:(h + 1) * 128], rhs=hT_g[hi][:, kc, 0:ntok],
                                start=(kc == 0), stop=(kc == 15)),
                               reads=[B_wkv, B_hTg[hi]], writes=[B_pk[pi]], signal=(kc == 15))
                        ko = cn["ko"] % 2
                        cn["ko"] += 1
                        if kind == "lat":
                            op("act", lambda e: e.activation(out=ksb[ko][:], in_=pk[pi][:], func=AF.Copy), reads=[B_pk[pi]],
                               writes=[B_ksb[ko]])
                            op("dve", lambda e: e.tensor_tensor(out=t1[ko][:], in0=pk[pi][:], in1=rC[hi][:], op=ALU.mult),
                               reads=[B_pk[pi], B_r[hi]], writes=[B_t1[ko]])

                            def part2():
                                op("pe", lambda e: e.matmul(pk2[:], lhsT=perm[:], rhs=ksb[ko][:], start=True, stop=True),
                                   reads=[B_perm, B_ksb[ko]], writes=[B_pk2])
                                op("dve", lambda e: e.tensor_tensor(out=t2[:], in0=pk2[:], in1=rS[hi][:], op=ALU.mult),
                                   reads=[B_pk2, B_r[hi]], writes=[B_t2])
                                op("pool", lambda e: e.tensor_tensor(out=kout[ko][:], in0=t1[ko][:], in1=t2[:], op=ALU.add),
                                   reads=[B_t1[ko], B_t2], writes=[B_kout[ko]])
                                dma("sp", lambda e: e.dma_start(out=kT_s[h, :, tok0:tok0 + ntok], in_=kout[ko][:, 0:ntok]), d_kout[ko],
                                    reads=[B_kout[ko]], writes=[B_kT])
                            lag.append(part2)
                        else:
                            op("act", lambda e: e.activation(out=kout[ko][:, 0:ntok], in_=pk[pi][:, 0:ntok], func=AF.Copy),
                               reads=[B_pk[pi]], writes=[B_kout[ko]])
                            dma("sp", lambda e: e.dma_start(out=kT_s[h, :, tok0:tok0 + ntok], in_=kout[ko][:, 0:ntok]), d_kout[ko],
                                reads=[B_kout[ko]], writes=[B_kT])
                        if len(lag) > 1:
                            lag.pop(0)()
                    units.append(kunit)

                def kflush():
                    while lag:
                        lag.pop(0)()
                units.append(kflush)
                for t in range(ntile):
                    for cc in range(2):
                        def vunit(t=t, cc=cc):
                            pi = cn["pv"] % 2
                            cn["pv"] += 1
                            for kc in range(16):
                                op("pe", lambda e, kc=kc: e.matmul(
                                    pv[pi][:], lhsT=hT_g[hi][:, kc, t * 128:(t + 1) * 128],
                                    rhs=wkv[:, kc, 1024 + cc * 512:1024 + (cc + 1) * 512], start=(kc == 0), stop=(kc == 15)),
                                   reads=[B_wkv, B_hTg[hi]], writes=[B_pv[pi]], signal=(kc == 15))
                            eng = "act" if cc == 0 else "dve"
                            if eng == "act":
                                op("act", lambda e: e.activation(
                                    out=vbuf[:, cc * 4:(cc + 1) * 4, t, 0:128], in_=pv[pi][:].rearrange("p (h c) -> p h c", c=128),
                                    func=AF.Copy), reads=[B_pv[pi]], dwrites=[B_vbuf])
                            else:
                                op("dve", lambda e: e.tensor_copy(
                                    out=vbuf[:, cc * 4:(cc + 1) * 4, t, 0:128], in_=pv[pi][:].rearrange("p (h c) -> p h c", c=128)),
                                   reads=[B_pv[pi]], dwrites=[B_vbuf])
                        units.append(vunit)

                def vstore():
                    tile0 = tok0 // 128
                    for h in range(NH):
                        dma("sp", lambda e, h=h: e.dma_start(
                            out=vA_s[h].rearrange("p (t c) -> p t c", c=VW)[:, tile0:tile0 + ntile, :],
                            in_=vbuf[:, h, 0:ntile, :]), d_vbuf, reads=[B_vbuf], writes=[B_vA])
                units.append(vstore)
                return units

            NG = len(groups)
            nu, nfin = norm_units(0)
            for p1, p2 in nu:
                p2(p1())
            nfin()
            for gi in range(NG):
                ku = kv_units(gi)
                if gi + 1 < NG:
                    nu, nfin = norm_units(gi + 1)
                else:
                    nu, nfin = [], None
                nk = max(len(ku), 1)
                ntl = len(nu)
                sched1 = {}
                sched2 = {}
                for t in range(ntl):
                    a = (t * nk) // max(ntl, 1)
                    b = min(nk - 1, a + max(1, nk // (2 * max(ntl, 1))))
                    sched1.setdefault(a, []).append(t)
                    sched2.setdefault(b, []).append(t)
                nis = {}
                if not ku:
                    for p1, p2 in nu:
                        p2(p1())
                else:
                    for ui, u in enumerate(ku):
                        for t in sched1.get(ui, []):
                            nis[t] = nu[t][0]()
                        u()
                        for t in sched2.get(ui, []):
                            nu[t][1](nis[t])
                if nfin is not None:
                    nfin()
            rec.flush()
            if STOP == 1:
                return nc

        with ExitStack() as es:
            wqf = sbt(es, "wqf", [128, 16, 2048], BF16)
            B_wqf = Buf()
            d_wqf = rec.dsem()
            hT_g = [sbt(es, "hT2_g%d" % i, [128, 16, 512], BF16) for i in range(2)]
            B_hTg = [Buf() for _ in range(2)]
            d_hl = [rec.dsem() for _ in range(2)]
            rC = [sbt(es, "rC2%d" % i, [128, 512], F32) for i in range(2)]
            rS = [sbt(es, "rS2%d" % i, [128, 512], F32) for i in range(2)]
            B_r = [Buf() for _ in range(2)]
            d_r = [rec.dsem() for _ in range(2)]
            ksb = sbt(es, "ksb2", [128, 512], BF16)
            t1 = sbt(es, "t12", [128, 512], F32)
            t2 = sbt(es, "t22", [128, 512], F32)
            B_ksb, B_t1, B_t2 = Buf(), Buf(), Buf()
            kout = [sbt(es, "qout%d" % i, [128, 512], BF16) for i in range(2)]
            B_kout = [Buf() for _ in range(2)]
            d_kout = [rec.dsem() for _ in range(2)]
            dftg = sbt(es, "dftg_s", [128, 2, 512], BF16)
            B_dftg = Buf()
            fT = [sbt(es, "fT%d" % i, [128, 8, 512], BF16) for i in range(2)]
            B_fT = [Buf() for _ in range(2)]
            ubuf = [sbt(es, "ubuf%d" % i, [128, 4, 4, 512], BF16) for i in range(2)]
            B_ubuf = [Buf() for _ in range(2)]
            d_ubuf = [rec.dsem() for _ in range(2)]
            pk = [pst(es, "pq%d" % i, [128, 512], F32) for i in range(2)]
            B_pk = [PBuf() for _ in range(2)]
            pk2 = pst(es, "pq2", [128, 512], F32)
            B_pk2 = PBuf()
            pu = [pst(es, "pu%d" % i, [128, 512], F32) for i in range(2)]
            B_pu = [PBuf() for _ in range(2)]

            dma("pool", lambda e: e.dma_start(out=wqf[:, :, 0:1024], in_=w_in_v[:, :, 0:1024]), d_wqf, writes=[B_wqf])
            dma("pool", lambda e: e.dma_start(out=wqf[:, :, 1024:2048], in_=w_in_v[:, :, 3072:4096]), d_wqf, writes=[B_wqf])
            d_dftg = rec.dsem()
            dma("sp", lambda e: e.dma_start(out=dftg[:], in_=dftg_d), d_dftg, writes=[B_dftg])
            groups = [("own", 0), ("own", 1)] + [("lat", g) for g in range(8)]
            pctr = 0
            kctr = 0
            def load_hT(gi):
                kind, g = groups[gi]
                i = gi % 2
                hidx = (9 + g) if kind == "own" else g
                dma("sp", lambda e, i=i, hidx=hidx: e.dma_start(out=hT_g[i][:], in_=hT_s[hidx].rearrange("p (k t) -> p k t", t=512)),
                    d_hl[i], reads=[B_hT[hidx]], writes=[B_hTg[i]])

            load_hT(0)
            for gi, (kind, g) in enumerate(groups):
                i = gi % 2
                if gi + 1 < len(groups):
                    load_hT(gi + 1)
                if kind == "own":
                    dma("sp", lambda e, i=i, g=g: e.dma_start(out=rC[i][:], in_=ropeCo[:, g * 512:(g + 1) * 512]), d_r[i],
                        writes=[B_r[i]])
                    dma("sp", lambda e, i=i, g=g: e.dma_start(out=rS[i][:], in_=ropeSo[:, g * 512:(g + 1) * 512]), d_r[i],
                        writes=[B_r[i]])
                    for h in range(NH):
                        pi = pctr % 2
                        pctr += 1
                        for kc in range(16):
                            op("pe", lambda e, i=i, pi=pi, h=h, kc=kc: e.matmul(
                                pk[pi][:], lhsT=wqf[:, kc, h * 128:(h + 1) * 128], rhs=hT_g[i][:, kc, :],
                                start=(kc == 0), stop=(kc == 15)),
                               reads=[B_wqf, B_hTg[i]], writes=[B_pk[pi]], signal=(kc == 15))
                        ko = kctr % 2
                        kctr += 1
                        rope_evac(pk[pi], B_pk[pi], 512, rC[i][:], rS[i][:], B_r[i], ksb, B_ksb, pk2, B_pk2, t1, B_t1, t2, B_t2,
                                  kout[ko][:], B_kout[ko])
                        dma("sp", lambda e, ko=ko, h=h, g=g: e.dma_start(
                            out=qT_s[:, h * OWN + g * 512:h * OWN + (g + 1) * 512], in_=kout[ko][:]), d_kout[ko],
                            reads=[B_kout[ko]], writes=[B_qT])
                    continue
                fi = g % 2
                for ct in range(8):
                    pi = pctr % 2
                    pctr += 1
                    for kc in range(16):
                        op("pe", lambda e, i=i, pi=pi, ct=ct, kc=kc: e.matmul(
                            pk[pi][:], lhsT=wqf[:, kc, 1024 + ct * 128:1024 + (ct + 1) * 128], rhs=hT_g[i][:, kc, :],
                            start=(kc == 0), stop=(kc == 15)),
                           reads=[B_wqf, B_hTg[i]], writes=[B_pk[pi]], signal=(kc == 15))
                    op("act", lambda e, pi=pi, fi=fi, ct=ct: e.activation(out=fT[fi][:, ct, :], in_=pk[pi][:], func=AF.Copy),
                       reads=[B_pk[pi]], dwrites=[B_fT[fi]])
                for gg in range(4):
                    for t in range(4):
                        ui = (gg * 4 + t) % 2
                        for k2 in range(2):
                            op("pe", lambda e, fi=fi, ui=ui, gg=gg, t=t, k2=k2: e.matmul(
                                pu[ui][:], lhsT=fT[fi][:, gg * 2 + k2, t * 128:(t + 1) * 128], rhs=dftg[:, k2, :],
                                start=(k2 == 0), stop=(k2 == 1)),
                               reads=[B_fT[fi], B_dftg], writes=[B_pu[ui]], signal=(k2 == 1))
                        if (gg * 4 + t) % 2 == 0:
                            op("dve", lambda e, fi=fi, ui=ui, gg=gg, t=t: e.tensor_copy(out=ubuf[fi][:, gg, t, :], in_=pu[ui][:]),
                               reads=[B_pu[ui]], dwrites=[B_ubuf[fi]])
                        else:
                            op("act", lambda e, fi=fi, ui=ui, gg=gg, t=t: e.activation(out=ubuf[fi][:, gg, t, :], in_=pu[ui][:],
                                                                                      func=AF.Copy),
                               reads=[B_pu[ui]], dwrites=[B_ubuf[fi]])
                for gg in range(4):
                    dma("sp", lambda e, fi=fi, gg=gg, g=g: e.dma_start(
                        out=u_s[gg].rearrange("p (t c) -> p t c", c=512)[:, g * 4:(g + 1) * 4, :], in_=ubuf[fi][:, gg, :, :]),
                        d_ubuf[fi], reads=[B_ubuf[fi]], writes=[B_u])
            rec.flush()
            if STOP == 2:
                return nc

        with ExitStack() as es:
            qT = sbt(es, "qT", [128, NH, OWN], BF16)
            B_q = Buf()
            d_q = rec.dsem()
            kTh = [sbt(es, "kTh%d" % i, [128, NKEY], BF16) for i in range(2)]
            vAh = [sbt(es, "vAh%d" % i, [128, NKT, VW], BF16) for i in range(2)]
            B_kv = [Buf() for _ in range(2)]
            d_kv = [rec.dsem() for _ in range(2)]
            NE = 3
            E = [[sbt(es, "E%d_%d" % (i, c), [128, 512], BF16) for c in range(2)] for i in range(NE)]
            B_E = [[Buf() for c in range(2)] for i in range(NE)]
            osb = sbt(es, "osb", [128, 128], F32)
            o2 = sbt(es, "o2", [128, 128], F32)
            junk = sbt(es, "ajunk", [128, 128], F32)
            abf = sbt(es, "abf", [128, 128], BF16)
            sc = sbt(es, "asc", [128, 8], F32)
            B_osb, B_o2, B_junk, B_abf, B_sc = [Buf() for _ in range(5)]
            aT = sbt(es, "aT", [128, NH, OWN], BF16)
            B_aTs = Buf()
            d_a = rec.dsem()
            pS = [[pst(es, "pS%d_%d" % (i, c), [128, 512], F32) for c in range(2)] for i in range(2)]
            B_pS = [[PBuf() for c in range(2)] for i in range(2)]
            pO = [pst(es, "pO%d" % i, [128, 512], F32) for i in range(3)]
            B_pO = [PBuf() for _ in range(3)]
            pT = pst(es, "pT", [128, 8, 128], BF16)
            B_pT = PBuf()

            dma("sp", lambda e: e.dma_start(out=qT[:], in_=qT_s.rearrange("p (h t) -> p h t", t=OWN)), d_q, reads=[B_qT],
                writes=[B_q])

            def load_kv(h):
                i = h % 2
                dma("sp", lambda e, i=i, h=h: e.dma_start(out=kTh[i][:], in_=kT_s[h]), d_kv[i], reads=[B_kT], writes=[B_kv[i]])
                dma("sp", lambda e, i=i, h=h: e.dma_start(out=vAh[i][:], in_=vA_s[h].rearrange("p (t c) -> p t c", c=VW)),
                    d_kv[i], reads=[B_vA], writes=[B_kv[i]])

            ocp = [[sbt(es, "ocp%d_%d" % (i, b), [128, 3 * VW], F32) for b in range(3)] for i in range(2)]
            B_ocp = [[Buf() for b in range(3)] for i in range(2)]
            load_kv(0)
            steps = [(h, qc, kt) for h in range(NH) for qc in range(2) for kt in range(NKT)]
            NS = len(steps)

            def emit_qk(s):
                h, qc, kt = steps[s]
                hi, si, q0 = h % 2, s % 2, qc * 512
                for c in range(2):
                    op("pe", lambda e, hi=hi, si=si, c=c, kt=kt, h=h, q0=q0: e.matmul(
                        pS[si][c][:], lhsT=kTh[hi][c * 64:(c + 1) * 64, kt * 128:(kt + 1) * 128],
                        rhs=qT[c * 64:(c + 1) * 64, h, q0:q0 + 512], start=True, stop=True),
                       reads=[B_kv[hi], B_q], writes=[B_pS[si][c]])

            def emit_exp(s):
                si, ei = s % 2, s % NE
                for c in range(2):
                    op("act", lambda e, si=si, ei=ei, c=c: e.activation(out=E[ei][c][:], in_=pS[si][c][:], func=AF.Exp),
                       reads=[B_pS[si][c]], writes=[B_E[ei][c]])

            def emit_av(s):
                h, qc, kt = steps[s]
                hi, ei = h % 2, s % NE
                for c in range(2):
                    for qs in range(4):
                        idx = c * 4 + qs
                        bank, pos = idx // 3, idx % 3
                        op("pe", lambda e, ei=ei, c=c, qs=qs, bank=bank, pos=pos, hi=hi, kt=kt: e.matmul(
                            pO[bank][:, pos * VW:(pos + 1) * VW], lhsT=E[ei][c][:, qs * 128:(qs + 1) * 128],
                            rhs=vAh[hi][:, kt, :], start=(kt == 0 and pos == 0), stop=(kt == NKT - 1),
                            skip_group_check=True),
                           reads=[B_E[ei][c], B_kv[hi]], writes=[B_pO[bank]], signal=(qs == 3))

            ectr = [0, 0]
            abfs = [sbt(es, "abfs%d" % i, [128, 128], BF16) for i in range(8)]
            B_abfs = [Buf() for _ in range(8)]

            def emit_epi_a(h, qc):
                oi = ectr[0] % 2
                ectr[0] += 1
                for b in range(3):
                    op("dve", lambda e, oi=oi, b=b: e.tensor_copy(out=ocp[oi][b][:], in_=pO[b][:, 0:3 * VW]),
                       reads=[B_pO[b]], writes=[B_ocp[oi][b]])
                for qs in range(4):
                    b0, p0 = qs // 3, qs % 3
                    b1, p1 = (4 + qs) // 3, (4 + qs) % 3
                    O0 = ocp[oi][b0][:, p0 * VW:p0 * VW + 128]
                    S0 = ocp[oi][b0][:, p0 * VW + 128:p0 * VW + 129]
                    O1 = ocp[oi][b1][:, p1 * VW:p1 * VW + 128]
                    S1 = ocp[oi][b1][:, p1 * VW + 128:p1 * VW + 129]
                    R0, R1 = B_ocp[oi][b0], B_ocp[oi][b1]
                    ab, Bab = abfs[oi * 4 + qs], B_abfs[oi * 4 + qs]
                    op("dve", lambda e, S0=S0: e.reciprocal(out=sc[:, 0:1], in_=S0), reads=[R0], writes=[B_sc])
                    op("dve", lambda e, S1=S1: e.reciprocal(out=sc[:, 1:2], in_=S1), reads=[R1], writes=[B_sc])
                    op("dve", lambda e: e.tensor_tensor(out=sc[:, 2:3], in0=sc[:, 1:2], in1=lamneg[:], op=ALU.mult),
                       reads=[B_sc, B_lam], writes=[B_sc])
                    op("dve", lambda e, O0=O0: e.tensor_scalar(out=osb[:], in0=O0, scalar1=sc[:, 0:1], scalar2=None, op0=ALU.mult),
                       reads=[R0, B_sc], writes=[B_osb])
                    op("dve", lambda e, O1=O1: e.scalar_tensor_tensor(out=o2[:], in0=O1, scalar=sc[:, 2:3], in1=osb[:],
                                                                      op0=ALU.mult, op1=ALU.add),
                       reads=[R1, B_sc, B_osb], writes=[B_o2])
                    op("dve", lambda e: e.tensor_tensor(out=junk[:], in0=o2[:], in1=o2[:], op=ALU.mult),
                       reads=[B_o2], writes=[B_junk])
                    op("dve", lambda e: e.tensor_reduce(out=sc[:, 3:4], in_=junk[:], axis=mybir.AxisListType.X, op=ALU.add),
                       reads=[B_junk], writes=[B_sc])
                    op("dve", lambda e: e.tensor_scalar(out=sc[:, 4:5], in0=sc[:, 3:4], scalar1=1.0 / 128, scalar2=EPS,
                                                        op0=ALU.mult, op1=ALU.add), reads=[B_sc], writes=[B_sc])
                    op("pool", lambda e: e.tensor_tensor(out=sc[:, 5:6], in0=sc[:, 4:5], in1=nhalf[:, 0:1], op=ALU.pow),
                       reads=[B_sc, B_nhalf], writes=[B_sc])
                    op("dve", lambda e, ab=ab: e.tensor_scalar(out=ab[:], in0=o2[:], scalar1=sc[:, 5:6], scalar2=None, op0=ALU.mult),
                       reads=[B_o2, B_sc], writes=[Bab])
                return oi

            def emit_epi_b(h, qc, oi):
                q0 = qc * 512
                for qs in range(4):
                    ab, Bab = abfs[oi * 4 + qs], B_abfs[oi * 4 + qs]
                    ti = ectr[1] % 8
                    ectr[1] += 1
                    op("pe", lambda e, ti=ti, ab=ab: e.transpose(out=pT[:, ti, :], in_=ab[:], identity=ident_bf[:]),
                       reads=[Bab, B_idb], writes=[B_pT])
                    op("dve", lambda e, ti=ti, h=h, q0=q0, qs=qs: e.tensor_scalar(
                        out=aT[:, h, q0 + qs * 128:q0 + (qs + 1) * 128], in0=pT[:, ti, :], scalar1=gsub_s[:, 0:1], scalar2=None,
                        op0=ALU.mult), reads=[B_pT, B_gsub], dwrites=[B_aTs])

            pend = []
            emit_qk(0)
            emit_qk(1)
            for s in range(NS):
                h, qc, kt = steps[s]
                if qc == 0 and kt == 0 and h + 1 < NH:
                    load_kv(h + 1)
                emit_exp(s)
                emit_av(s)
                if s + 2 < NS:
                    emit_qk(s + 2)
                if kt == NKT - 1:
                    oi = emit_epi_a(h, qc)
                    pend.append((s + 10, h, qc, oi))
                while pend and pend[0][0] <= s:
                    _, ph, pqc, poi = pend.pop(0)
                    emit_epi_b(ph, pqc, poi)
            for _, ph, pqc, poi in pend:
                emit_epi_b(ph, pqc, poi)
            dma("sp", lambda e: e.dma_start(out=aT_s.rearrange("p (h t) -> p h t", t=OWN), in_=aT[:]), d_a, reads=[B_aTs],
                writes=[B_aT])
            rec.flush()
            if STOP == 3:
                return nc

        with ExitStack() as es:
            tab = sbt(es, "ptab", [128, 32, 2, 512], BF16)
            B_tab = Buf()
            d_tab = rec.dsem()
            ug = [sbt(es, "ug%d" % i, [128, 32, 512], BF16) for i in range(2)]
            B_ug = [Buf() for _ in range(2)]
            d_ug = [rec.dsem() for _ in range(2)]
            yfT = sbt(es, "yfT", [128, 8, OWN], BF16)
            B_yf = Buf()
            d_yf = rec.dsem()
            py = [pst(es, "pyf%d" % i, [128, 512], F32) for i in range(2)]
            B_py = [PBuf() for _ in range(2)]
            uctr = 0
            pctr = 0
            for kch in range(2):
                dma("sp", lambda e, kch=kch: e.dma_start(out=tab[:], in_=dftp_d[kch].rearrange("p (t c k) -> p t c k", c=2, k=512)),
                    d_tab, writes=[B_tab])
                for gg in range(4):
                    ui = uctr % 2
                    uctr += 1
                    dma("sp", lambda e, ui=ui, gg=gg: e.dma_start(out=ug[ui][:], in_=u_s[gg].rearrange("p (t c) -> p t c", c=512)),
                        d_ug[ui], reads=[B_u], writes=[B_ug[ui]])
                    for half in range(2):
                        pi = pctr % 2
                        pctr += 1
                        for tt in range(32):
                            for cs in range(2):
                                op("pe", lambda e, ui=ui, pi=pi, half=half, tt=tt, cs=cs: e.matmul(
                                    py[pi][:], lhsT=ug[ui][:, tt, cs * 256 + half * 128:cs * 256 + (half + 1) * 128],
                                    rhs=tab[:, tt, cs, :], start=(tt == 0 and cs == 0), stop=(tt == 31 and cs == 1)),
                                   reads=[B_ug[ui], B_tab], writes=[B_py[pi]], signal=(tt == 31 and cs == 1))
                        op("act", lambda e, pi=pi, gg=gg, half=half, kch=kch: e.activation(
                            out=yfT[:, gg * 2 + half, kch * 512:(kch + 1) * 512], in_=py[pi][:], func=AF.Copy),
                           reads=[B_py[pi]], dwrites=[B_yf])
            dma("sp", lambda e: e.dma_start(out=yfT_s.rearrange("p (c t) -> p c t", t=OWN), in_=yfT[:]), d_yf, reads=[B_yf],
                writes=[B_yfT])
            rec.flush()
            if STOP == 4:
                return nc

        with ExitStack() as es:
            hTo = sbt(es, "hTo", [128, 16, OWN], BF16)
            aT = sbt(es, "aT5", [128, 8, OWN], BF16)
            yfT = sbt(es, "yfT5", [128, 8, OWN], BF16)
            B_in = Buf()
            d_in = rec.dsem()
            mT = sbt(es, "mT", [128, 16, OWN], BF16)
            B_m = Buf()
            d_m = rec.dsem()
            wb = [sbt(es, "wb5_%d" % i, [128, 48, 256], BF16) for i in range(2)]
            B_wb = [Buf() for _ in range(2)]
            d_wb = [rec.dsem() for _ in range(2)]
            sg = [sbt(es, "sg%d" % i, [128, 512], F32) for i in range(2)]
            m1 = [sbt(es, "m1_%d" % i, [128, 512], F32) for i in range(2)]
            B_sg = [Buf() for _ in range(2)]
            B_m1 = [Buf() for _ in range(2)]
            pp = [[pst(es, "p5_%d_%d" % (i, k), [128, 512], F32) for k in range(4)] for i in range(2)]
            B_pp = [[PBuf() for k in range(4)] for i in range(2)]

            for g in range(2):
                dma("sp", lambda e, g=g: e.dma_start(out=hTo[:, :, g * 512:(g + 1) * 512],
                                                      in_=hT_s[9 + g].rearrange("p (k t) -> p k t", t=512)), d_in,
                    reads=[B_hT[9 + g]], writes=[B_in])
            dma("sp", lambda e: e.dma_start(out=aT[:], in_=aT_s.rearrange("p (h t) -> p h t", t=OWN)), d_in, reads=[B_aT],
                writes=[B_in])
            dma("sp", lambda e: e.dma_start(out=yfT[:], in_=yfT_s.rearrange("p (c t) -> p c t", t=OWN)), d_in, reads=[B_yfT],
                writes=[B_in])
            w_abr_v = w_abr.rearrange("(k p) n -> p k n", p=128)
            w_fbr_v = w_fbr.rearrange("(k p) n -> p k n", p=128)

            def load_wb(nbk):
                i = nbk % 2
                c0 = nbk * 256
                dma("pool", lambda e, i=i, c0=c0: e.dma_start(out=wb[i][:, 0:8, :], in_=w_abr_v[:, :, c0:c0 + 256]), d_wb[i],
                    writes=[B_wb[i]])
                dma("pool", lambda e, i=i, c0=c0: e.dma_start(out=wb[i][:, 8:16, :], in_=w_fbr_v[:, :, c0:c0 + 256]), d_wb[i],
                    writes=[B_wb[i]])
                dma("pool", lambda e, i=i, c0=c0: e.dma_start(out=wb[i][:, 16:32, :], in_=w_in_v[:, :, 4096 + c0:4096 + c0 + 256]),
                    d_wb[i], writes=[B_wb[i]])
                dma("pool", lambda e, i=i, c0=c0: e.dma_start(out=wb[i][:, 32:48, :], in_=w_in_v[:, :, 6144 + c0:6144 + c0 + 256]),
                    d_wb[i], writes=[B_wb[i]])

            load_wb(0)
            pctr = 0
            for nbk in range(8):
                if nbk + 1 < 8:
                    load_wb(nbk + 1)
                wi = nbk % 2
                for j in range(2):
                    n = nbk * 2 + j
                    for ch in range(2):
                        pi = pctr % 2
                        pctr += 1
                        specs = [(0, 8, aT), (8, 8, yfT), (16, 16, hTo), (32, 16, hTo)]
                        for k, (w0, nk, src) in enumerate(specs):
                            for kc in range(nk):
                                op("pe", lambda e, wi=wi, pi=pi, k=k, w0=w0, kc=kc, nk=nk, src=src, j=j, ch=ch: e.matmul(
                                    pp[pi][k][:], lhsT=wb[wi][:, w0 + kc, j * 128:(j + 1) * 128],
                                    rhs=src[:, kc, ch * 512:(ch + 1) * 512], start=(kc == 0), stop=(kc == nk - 1)),
                                   reads=[B_wb[wi], B_in], writes=[B_pp[pi][k]], signal=(kc == nk - 1))
                        op("act", lambda e, pi=pi: e.activation(out=sg[0][:], in_=pp[pi][2][:], func=AF.Sigmoid),
                           reads=[B_pp[pi][2]], writes=[B_sg[0]])
                        op("act", lambda e, pi=pi: e.activation(out=sg[1][:], in_=pp[pi][3][:], func=AF.Sigmoid),
                           reads=[B_pp[pi][3]], writes=[B_sg[1]])
                        op("dve", lambda e, pi=pi: e.tensor_tensor(out=m1[0][:], in0=pp[pi][0][:], in1=sg[0][:], op=ALU.mult),
                           reads=[B_pp[pi][0], B_sg[0]], writes=[B_m1[0]])
                        op("dve", lambda e, pi=pi: e.tensor_tensor(out=m1[1][:], in0=pp[pi][1][:], in1=sg[1][:], op=ALU.mult),
                           reads=[B_pp[pi][1], B_sg[1]], writes=[B_m1[1]])
                        op("pool", lambda e, n=n, ch=ch: e.tensor_tensor(out=mT[:, n, ch * 512:(ch + 1) * 512], in0=m1[0][:],
                                                                         in1=m1[1][:], op=ALU.add),
                           reads=[B_m1[0], B_m1[1]], dwrites=[B_m])
            dma("sp", lambda e: e.dma_start(out=mT_s.rearrange("p (k t) -> p k t", t=OWN), in_=mT[:]), d_m, reads=[B_m],
                writes=[B_mT])
            rec.flush()
            if STOP == 5:
                return nc

        with ExitStack() as es:
            mT = sbt(es, "mT6", [128, 16, OWN], BF16)
            B_m = Buf()
            d_m = rec.dsem()
            xT = sbt(es, "xT6", [128, 16, OWN], F32)
            B_xT = Buf()
            d_x = rec.dsem()
            xt = [sbt(es, "xt6_%d" % i, [128, D], F32) for i in range(2)]
            B_xt = [Buf() for _ in range(2)]
            d_xt = [rec.dsem() for _ in range(2)]
            wo = [sbt(es, "wo%d" % i, [128, 16, 256], BF16) for i in range(2)]
            B_wo = [Buf() for _ in range(2)]
            d_wo = [rec.dsem() for _ in range(2)]
            sq = [sbt(es, "sq%d" % i, [128, 512], BF16) for i in range(2)]
            B_sq = [Buf() for _ in range(2)]
            ms = sbt(es, "ms6", [128, 512], F32)
            rstd = sbt(es, "rstd6", [128, 512], F32)
            tmp = [sbt(es, "tmp6_%d" % i, [128, 512], F32) for i in range(2)]
            B_ms, B_rstd = Buf(), Buf()
            B_tmp = [Buf() for _ in range(2)]
            h2s = [sbt(es, "h2s%d" % i, [128, 512], BF16) for i in range(2)]
            B_h2s = [Buf() for _ in range(2)]
            d_h2 = [rec.dsem() for _ in range(2)]
            ptx = [pst(es, "ptx%d" % i, [128, 4, 128], F32) for i in range(2)]
            B_ptx = [PBuf() for _ in range(2)]
            pyo = [pst(es, "pyo%d" % i, [128, 512], F32) for i in range(2)]
            B_pyo = [PBuf() for _ in range(2)]
            pss = pst(es, "pss", [128, 512], F32)
            B_pss = PBuf()

            dma("sp", lambda e: e.dma_start(out=mT[:], in_=mT_s.rearrange("p (k t) -> p k t", t=OWN)), d_m, reads=[B_mT],
                writes=[B_m])
            w_out_v = w_out.rearrange("(k p) n -> p k n", p=128)

            def load_wo(nbk):
                i = nbk % 2
                dma("pool", lambda e, i=i, nbk=nbk: e.dma_start(out=wo[i][:], in_=w_out_v[:, :, nbk * 256:(nbk + 1) * 256]),
                    d_wo[i], writes=[B_wo[i]])

            load_wo(0)
            tcn = 0
            for t in range(8):
                i = t % 2
                dma("sp", lambda e, i=i, t=t: e.dma_start(out=xt[i][:], in_=xown[t * 128:(t + 1) * 128, :]), d_xt[i],
                    writes=[B_xt[i]])
                for k4 in range(4):
                    pi = tcn % 2
                    tcn += 1
                    for k in range(4):
                        kc = k4 * 4 + k
                        op("pe", lambda e, i=i, pi=pi, k=k, kc=kc: e.transpose(out=ptx[pi][:, k, :], in_=xt[i][:, kc * 128:(kc + 1) * 128],
                                                                                identity=ident_f[:]),
                           reads=[B_xt[i], B_idf], writes=[B_ptx[pi]], signal=(k == 3))
                    op("act", lambda e, pi=pi, k4=k4, t=t: e.activation(out=xT[:, k4 * 4:(k4 + 1) * 4, t * 128:(t + 1) * 128],
                                                                         in_=ptx[pi][:], func=AF.Copy),
                       reads=[B_ptx[pi]], dwrites=[B_xT])
            pctr = 0
            for nbk in range(8):
                if nbk + 1 < 8:
                    load_wo(nbk + 1)
                wi = nbk % 2
                for j in range(2):
                    n = nbk * 2 + j
                    for ch in range(2):
                        pi = pctr % 2
                        pctr += 1
                        for kc in range(16):
                            op("pe", lambda e, wi=wi, pi=pi, kc=kc, j=j, ch=ch: e.matmul(
                                pyo[pi][:], lhsT=wo[wi][:, kc, j * 128:(j + 1) * 128], rhs=mT[:, kc, ch * 512:(ch + 1) * 512],
                                start=(kc == 0), stop=(kc == 15)),
                               reads=[B_wo[wi], B_m], writes=[B_pyo[pi]], signal=(kc == 15))
                        op("dve", lambda e, pi=pi, n=n, ch=ch: e.scalar_tensor_tensor(
                            out=xT[:, n, ch * 512:(ch + 1) * 512], in0=pyo[pi][:], scalar=vecs[:, V_GT1, n:n + 1],
                            in1=xT[:, n, ch * 512:(ch + 1) * 512], op0=ALU.mult, op1=ALU.add),
                           reads=[B_pyo[pi], B_vecs, B_xT], dwrites=[B_xT])
            dma("sp", lambda e: e.dma_start(out=xmid_s.rearrange("p (k t) -> p k t", t=OWN), in_=xT[:]), d_x, reads=[B_xT],
                writes=[B_xmid])
            hctr = 0
            for ch in range(2):
                for n in range(16):
                    si = n % 2
                    op("act", lambda e, si=si, n=n, ch=ch: e.activation(out=sq[si][:], in_=xT[:, n, ch * 512:(ch + 1) * 512],
                                                                         func=AF.Square), reads=[B_xT], writes=[B_sq[si]])
                    op("pe", lambda e, si=si, n=n: e.matmul(pss[:], lhsT=ones_bf[:], rhs=sq[si][:], start=(n == 0), stop=(n == 15)),
                       reads=[B_ones, B_sq[si]], writes=[B_pss])
                op("dve", lambda e: e.tensor_scalar(out=ms[:], in0=pss[:], scalar1=1.0 / D, scalar2=EPS, op0=ALU.mult, op1=ALU.add),
                   reads=[B_pss], writes=[B_ms])
                op("act", lambda e: e.activation(out=rstd[:], in_=ms[:], func=AF.Sqrt), reads=[B_ms], writes=[B_rstd])
                op("dve", lambda e: e.reciprocal(out=rstd[:], in_=rstd[:]), reads=[B_rstd], writes=[B_rstd])
                for n in range(16):
                    ti = hctr % 2
                    hctr += 1
                    op("dve", lambda e, ti=ti, n=n, ch=ch: e.tensor_tensor(out=tmp[ti][:], in0=xT[:, n, ch * 512:(ch + 1) * 512],
                                                                           in1=rstd[:], op=ALU.mult),
                       reads=[B_xT, B_rstd], writes=[B_tmp[ti]])
                    op("dve", lambda e, ti=ti, n=n: e.tensor_scalar(out=h2s[ti][:], in0=tmp[ti][:], scalar1=vecs[:, V_G2, n:n + 1],
                                                                    scalar2=vecs[:, V_SH2, n:n + 1], op0=ALU.mult, op1=ALU.add),
                       reads=[B_tmp[ti], B_vecs], writes=[B_h2s[ti]])
                    dma("sp", lambda e, ti=ti, n=n, ch=ch: e.dma_start(
                        out=h2T_s[:, n * OWN + ch * 512:n * OWN + (ch + 1) * 512], in_=h2s[ti][:]), d_h2[ti],
                        reads=[B_h2s[ti]], writes=[B_h2T])
            rec.flush()
            if STOP == 6:
                return nc

        with ExitStack() as es:
            zT = sbt(es, "zT", [128, 64, OWN], BF16)
            B_z = Buf()
            with ExitStack() as es1:
                h2T = sbt(es1, "h2T", [128, 16, OWN], BF16)
                B_h2 = Buf()
                d_h2l = rec.dsem()
                w1 = [sbt(es1, "w1_%d" % i, [128, 16, 512], BF16) for i in range(2)]
                B_w1 = [Buf() for _ in range(2)]
                d_w1 = [rec.dsem() for _ in range(2)]
                rl = [sbt(es1, "rl%d" % i, [128, 512], F32) for i in range(2)]
                B_rl = [Buf() for _ in range(2)]
                pz = [pst(es1, "pz%d" % i, [128, 512], F32) for i in range(4)]
                B_pz = [PBuf() for _ in range(4)]
                dma("sp", lambda e: e.dma_start(out=h2T[:], in_=h2T_s.rearrange("p (k t) -> p k t", t=OWN)), d_h2l,
                    reads=[B_h2T], writes=[B_h2])
                w_m1_v = w_m1.rearrange("(k p) n -> p k n", p=128)

                def load_w1(cb):
                    i = cb % 2
                    dma("pool", lambda e, i=i, cb=cb: e.dma_start(out=w1[i][:], in_=w_m1_v[:, :, cb * 512:(cb + 1) * 512]),
                        d_w1[i], writes=[B_w1[i]])

                load_w1(0)
                pctr = 0
                for cb in range(16):
                    if cb + 1 < 16:
                        load_w1(cb + 1)
                    wi = cb % 2
                    for j in range(4):
                        f = cb * 4 + j
                        for ch in range(2):
                            pi = pctr % 4
                            ri = pctr % 2
                            pctr += 1
                            for kc in range(16):
                                op("pe", lambda e, wi=wi, pi=pi, kc=kc, j=j, ch=ch: e.matmul(
                                    pz[pi][:], lhsT=w1[wi][:, kc, j * 128:(j + 1) * 128], rhs=h2T[:, kc, ch * 512:(ch + 1) * 512],
                                    start=(kc == 0), stop=(kc == 15)),
                                   reads=[B_w1[wi], B_h2], writes=[B_pz[pi]], signal=(kc == 15))
                            op("act", lambda e, pi=pi, ri=ri: e.activation(out=rl[ri][:], in_=pz[pi][:], func=AF.Relu),
                               reads=[B_pz[pi]], writes=[B_rl[ri]])
                            eng = "dve" if (pctr % 2 == 0) else "pool"
                            op(eng, lambda e, ri=ri, f=f, ch=ch: e.tensor_tensor(out=zT[:, f, ch * 512:(ch + 1) * 512], in0=rl[ri][:],
                                                                                  in1=rl[ri][:], op=ALU.mult),
                               reads=[B_rl[ri]], dwrites=[B_z])
                rec.flush()
                if STOP == 7:
                    return nc
            with ExitStack() as es2:
                w2 = [sbt(es2, "w2_%d" % i, [128, 64, 128], BF16) for i in range(3)]
                B_w2 = [Buf() for _ in range(3)]
                d_w2 = [rec.dsem() for _ in range(3)]
                xm = [sbt(es2, "xm%d" % i, [128, 512], F32) for i in range(2)]
                B_xm = [Buf() for _ in range(2)]
                d_xm = [rec.dsem() for _ in range(2)]
                xo = [sbt(es2, "xo%d" % i, [128, 512], F32) for i in range(2)]
                B_xo = [Buf() for _ in range(2)]
                d_xo = [rec.dsem() for _ in range(2)]
                po = [pst(es2, "po%d" % i, [128, 512], F32) for i in range(2)]
                B_po = [PBuf() for _ in range(2)]
                w_m2_v = w_m2.rearrange("(k p) n -> p k n", p=128)

                def load_w2(n):
                    i = n % 3
                    for q in range(2):
                        dma("pool", lambda e, i=i, n=n, q=q: e.dma_start(
                            out=w2[i][:, q * 32:(q + 1) * 32, :], in_=w_m2_v[:, q * 32:(q + 1) * 32, n * 128:(n + 1) * 128]),
                            d_w2[i], writes=[B_w2[i]])

                load_w2(0)
                load_w2(1)
                pctr = 0
                for n in range(16):
                    if n + 2 < 16:
                        load_w2(n + 2)
                    wi = n % 3
                    for ch in range(2):
                        pi = pctr % 2
                        pctr += 1
                        dma("sp", lambda e, pi=pi, n=n, ch=ch: e.dma_start(
                            out=xm[pi][:], in_=xmid_s[:, n * OWN + ch * 512:n * OWN + (ch + 1) * 512]), d_xm[pi],
                            reads=[B_xmid], writes=[B_xm[pi]])
                        for kc in range(64):
                            op("pe", lambda e, wi=wi, pi=pi, kc=kc, ch=ch: e.matmul(
                                po[pi][:], lhsT=w2[wi][:, kc, :], rhs=zT[:, kc, ch * 512:(ch + 1) * 512],
                                start=(kc == 0), stop=(kc == 63)),
                               reads=[B_w2[wi], B_z], writes=[B_po[pi]], signal=(kc == 63))
                        op("dve", lambda e, pi=pi, n=n: e.scalar_tensor_tensor(
                            out=xo[pi][:], in0=po[pi][:], scalar=vecs[:, V_GT2, n:n + 1], in1=xm[pi][:],
                            op0=ALU.mult, op1=ALU.add), reads=[B_po[pi], B_vecs, B_xm[pi]], writes=[B_xo[pi]])
                        dma("sp", lambda e, pi=pi, n=n, ch=ch: e.dma_start(
                            out=xout_s[:, n * OWN + ch * 512:n * OWN + (ch + 1) * 512], in_=xo[pi][:]), d_xo[pi],
                            reads=[B_xo[pi]], writes=[B_xout])
                rec.flush()
                if STOP == 8:
                    return nc

        with ExitStack() as es:
            gf = sbt(es, "gf", [128, D], F32)
            B_gf = Buf()
            d_gf = rec.dsem()
            xl = [sbt(es, "xl%d" % i, [128, 16, 128], F32) for i in range(2)]
            B_xl = [Buf() for _ in range(2)]
            d_xl = [rec.dsem() for _ in range(2)]
            junk = sbt(es, "fjunk", [128, 512], BF16)
            B_junk = Buf()
            st = [sbt(es, "fst%d" % i, [128, 8], F32) for i in range(2)]
            B_st = [Buf() for _ in range(2)]
            ot = [sbt(es, "ot%d" % i, [128, D], F32) for i in range(2)]
            B_ot = [Buf() for _ in range(2)]
            d_ot = [rec.dsem() for _ in range(2)]
            pf = [[pst(es, "pf%d_%d" % (i, k), [128, 4, 128], F32) for k in range(4)] for i in range(2)]
            B_pf = [[PBuf() for k in range(4)] for i in range(2)]
            dma("sp", lambda e: e.dma_start(out=gf[:], in_=gfin.partition_broadcast(128)), d_gf, writes=[B_gf])
            xout_v = xout_s.rearrange("p (k t) -> p k t", t=OWN)
            for t in range(8):
                i = t % 2
                dma("sp", lambda e, i=i, t=t: e.dma_start(out=xl[i][:], in_=xout_v[:, :, t * 128:(t + 1) * 128]), d_xl[i],
                    reads=[B_xout], writes=[B_xl[i]])
                for k4 in range(4):
                    for k in range(4):
                        kc = k4 * 4 + k
                        op("pe", lambda e, i=i, k4=k4, k=k, kc=kc: e.transpose(out=pf[i][k4][:, k, :], in_=xl[i][:, kc, :],
                                                                                identity=ident_f[:]),
                           reads=[B_xl[i], B_idf], writes=[B_pf[i][k4]], signal=(k == 3))
                    op("act", lambda e, i=i, k4=k4: e.activation(out=junk[:], in_=pf[i][k4][:].rearrange("p a b -> p (a b)"),
                                                                 func=AF.Square, accum_out=st[i][:, k4:k4 + 1]),
                       reads=[B_pf[i][k4]], writes=[B_junk, B_st[i]])
                op("dve", lambda e, i=i: e.tensor_reduce(out=st[i][:, 4:5], in_=st[i][:, 0:4], axis=mybir.AxisListType.X, op=ALU.add),
                   reads=[B_st[i]], writes=[B_st[i]])
                op("dve", lambda e, i=i: e.tensor_scalar(out=st[i][:, 5:6], in0=st[i][:, 4:5], scalar1=1.0 / D, scalar2=EPS,
                                                         op0=ALU.mult, op1=ALU.add), reads=[B_st[i]], writes=[B_st[i]])
                op("pool", lambda e, i=i: e.tensor_tensor(out=st[i][:, 6:7], in0=st[i][:, 5:6], in1=nhalf[:, 0:1], op=ALU.pow),
                   reads=[B_st[i], B_nhalf], writes=[B_st[i]])
                for k4 in range(4):
                    op("dve", lambda e, i=i, k4=k4: e.scalar_tensor_tensor(
                        out=ot[i][:, k4 * 512:(k4 + 1) * 512], in0=pf[i][k4][:].rearrange("p a b -> p (a b)"), scalar=st[i][:, 6:7],
                        in1=gf[:, k4 * 512:(k4 + 1) * 512], op0=ALU.mult, op1=ALU.mult),
                       reads=[B_pf[i][k4], B_st[i], B_gf], dwrites=[B_ot[i]])
                dma("sp", lambda e, i=i, t=t: e.dma_start(out=out_d[t * 128:(t + 1) * 128, :], in_=ot[i][:]), d_ot[i],
                    reads=[B_ot[i]], writes=[Buf()])
            rec.flush()
            if STOP == 9:
                return nc
    return nc


_NC_CACHE = {}


def _get_nc():
    if "nc" not in _NC_CACHE:
        _NC_CACHE["nc"] = build_nc()
    return _NC_CACHE["nc"]


def make_in_maps(x, c, ctx, c_ctx, w_ada, b_ada, g_norm1, w_in, lam_q1, lam_k1, lam_q2, lam_k2,
                 g_subln, w_attn_br, w_four_br, w_out, g_norm2, w_mlp_in, w_mlp_out, g_final):
    f32 = np.float32
    A = lambda a: np.ascontiguousarray(np.asarray(a, dtype=f32))
    cst = _consts()
    x = A(x)
    ctx = A(ctx)
    c = A(c)
    shared = {
        "bada_r": A(b_ada)[0].reshape(96, 128),
        "lamv": np.concatenate([A(lam_q1)[0], A(lam_k1)[0], A(lam_q2)[0], A(lam_k2)[0]]),
        "gsub": A(g_subln)[0].reshape(128, 1),
        "gfin": A(g_final),
        "w_ada": A(w_ada)[0],
        "w_in": A(w_in)[0],
        "w_abr": A(w_attn_br)[0],
        "w_fbr": A(w_four_br)[0],
        "w_out": A(w_out)[0],
        "w_m1": A(w_mlp_in)[0],
        "w_m2": A(w_mlp_out)[0],
        "ropeC": cst["ropeC"],
        "ropeS": cst["ropeS"],
        "perm": cst["perm"],
        "ident_bf": cst["ident_bf"],
        "ident_f": cst["ident_f"],
        "dftg": cst["dftg"],
    }
    in_maps = []
    for core in range(8):
        b, j = core // 4, core % 4
        m = dict(shared)
        m["xb"] = x[b]
        m["ctxb"] = ctx[b]
        m["xown"] = np.ascontiguousarray(x[b, j * OWN:(j + 1) * OWN])
        m["small_r"] = np.concatenate([c[b].reshape(16, 128), A(c_ctx).reshape(16, 128), A(g_norm1)[0].reshape(16, 128),
                                       A(g_norm2)[0].reshape(16, 128)], axis=0)
        m["ropeCo"] = np.ascontiguousarray(cst["ropeC"][:, j * OWN:(j + 1) * OWN] * f32(0.125))
        m["ropeSo"] = np.ascontiguousarray(cst["ropeS"][:, j * OWN:(j + 1) * OWN] * f32(0.125))
        m["dftp"] = cst["dftp"][j].reshape(2, 128, 32 * 2 * 512)
        in_maps.append(m)
    return in_maps


def kernel(**inputs):
    in_maps = make_in_maps(**inputs)
    nc = _get_nc()
    res = run_bass_kernel_spmd(nc, in_maps, core_ids=list(range(8)))
    out = np.empty((2, SEQ, D), np.float32)
    for core in range(8):
        b, j = core // 4, core % 4
        out[b, j * OWN:(j + 1) * OWN] = res.results[core]["out"]
    return out
```

```python
import math
from contextlib import ExitStack

import numpy as np
import ml_dtypes

import concourse.bass as bass
import concourse.mybir as mybir
from concourse.bass_utils import run_bass_kernel_spmd

F32 = mybir.dt.float32
BF16 = mybir.dt.bfloat16
AF = mybir.ActivationFunctionType
ALU = mybir.AluOpType

D = 2048
SEQ = 4096
CTX = 256
NKEY = SEQ + CTX
NKT = NKEY // 128
OWN = 1024
NH = 8
DFF = 8192
EPS = 1e-6
LAM_INIT = 0.8 - 0.6 * math.exp(0.0)
VW = 130

DEBUG = False
STOP = -1
SUB = -1


class Sem:
    __slots__ = ("h", "count")

    def __init__(self, h):
        self.h = h
        self.count = 0


class Buf:
    __slots__ = ("name", "w", "r", "x")

    def __init__(self, name="", x=False):
        self.name = name
        self.w = {}
        self.r = {}
        self.x = x


def PBuf():
    return Buf("psum", True)


ENGS = ("pe", "act", "dve", "pool", "sp")


class Rec:
    def __init__(self, nc):
        self.nc = nc
        self.esem = {e: Sem(nc.alloc_semaphore("es_" + e)) for e in ENGS}
        self.known = {e: {} for e in ENGS}
        self.stream = {e: [] for e in ENGS}
        self.dsems = []
        self.dset = set()
        self.nds = 0

    def dsem(self):
        s = Sem(self.nc.alloc_semaphore("ds%d" % self.nds))
        self.nds += 1
        self.dsems.append(s)
        self.dset.add(s)
        return s

    def _waits(self, e, reads, writes, dwrites=()):
        deps = {}
        own = self.esem[e]

        def add(k, v):
            if deps.get(k, 0) < v:
                deps[k] = v

        for b in reads:
            for k, v in b.w.items():
                add(k, v)
            if b.x:
                for k, v in b.r.items():
                    if k is not own:
                        add(k, v)
        for b in writes:
            for k, v in b.w.items():
                add(k, v)
            for k, v in b.r.items():
                add(k, v)
        for b in dwrites:
            for k, v in b.r.items():
                add(k, v)
        kn = self.known[e]
        for k, v in deps.items():
            if e == "pe" and k is own:
                continue
            if k in self.dset:
                v = k.count
            if kn.get(k, 0) >= v:
                continue
            kn[k] = v
            self.stream[e].append(lambda eng, h=k.h, v=v: eng.wait_ge(h, v))

    def _post(self, ev, reads, writes, dwrites=()):
        k, v = ev
        for b in reads:
            if b.r.get(k, 0) < v:
                b.r[k] = v
        for b in writes:
            b.w = {k: v}
            b.r = {}
        for b in dwrites:
            if b.w.get(k, 0) < v:
                b.w[k] = v

    def op(self, e, fn, reads=(), writes=(), signal=True, dwrites=()):
        self._waits(e, reads, writes, dwrites)
        s = self.esem[e]
        if signal:
            s.count += 1
            v = s.count
            self.stream[e].append(lambda eng, fn=fn, h=s.h: fn(eng).then_inc(h, 1))
        else:
            v = s.count + 1
            self.stream[e].append(lambda eng, fn=fn: fn(eng))
        self._post((s, v), reads, writes, dwrites)

    def dma(self, q, fn, ds, reads=(), writes=(), dwrites=()):
        self._waits(q, reads, writes, dwrites)
        ds.count += 16
        self.stream[q].append(lambda eng, fn=fn, h=ds.h: fn(eng).then_inc(h, 16))
        self._post((ds, ds.count), reads, writes, dwrites)

    def flush(self):
        for q in ("sp",):
            kn = self.known[q]
            for s in self.dsems:
                if s.count > 0 and kn.get(s, 0) < s.count:
                    kn[s] = s.count
                    self.stream[q].append(lambda eng, h=s.h, v=s.count: eng.wait_ge(h, v))
        st = self.stream
        with self.nc.Block() as blk:
            @blk.tensor
            def _(e):
                for f in st["pe"]:
                    f(e)

            @blk.scalar
            def _(e):
                for f in st["act"]:
                    f(e)

            @blk.vector
            def _(e):
                for f in st["dve"]:
                    f(e)

            @blk.gpsimd
            def _(e):
                for f in st["pool"]:
                    f(e)

            @blk.sync
            def _(e):
                for f in st["sp"]:
                    f(e)
        self.stream = {e: [] for e in ENGS}


def _rope_tables(pos):
    pos = np.asarray(pos, dtype=np.float64)
    row = pos // 64
    col = pos % 64
    inv = 10000.0 ** (-(np.arange(0, 32, 2, dtype=np.float64)) / 32.0)
    C = np.zeros((128, len(pos)), np.float64)
    S = np.zeros((128, len(pos)), np.float64)
    for p in range(128):
        d = p % 64
        pp = row if d < 32 else col
        f = inv[d % 16]
        C[p] = np.cos(pp * f)
        sgn = -1.0 if (d % 32) < 16 else 1.0
        S[p] = sgn * np.sin(pp * f)
    return C, S


def _perm_matrix():
    P = np.zeros((128, 128), np.float32)
    for m in range(128):
        d = m % 64
        pm = m + 16 if (d % 32) < 16 else m - 16
        P[pm, m] = 1.0
    return P


_CONST_CACHE = {}


def _consts():
    if _CONST_CACHE:
        return _CONST_CACHE
    bf = ml_dtypes.bfloat16
    C, S = _rope_tables(np.arange(SEQ))
    _CONST_CACHE["ropeC"] = C.astype(np.float32)
    _CONST_CACHE["ropeS"] = S.astype(np.float32)
    _CONST_CACHE["perm"] = _perm_matrix().astype(bf)
    _CONST_CACHE["ident_bf"] = np.eye(128, dtype=np.float32).astype(bf)
    _CONST_CACHE["ident_f"] = np.eye(128, dtype=np.float32)
    j = np.arange(256, dtype=np.float64)
    ang = 2.0 * np.pi * np.outer(j, j) / 256.0
    cg = np.cos(ang) / 16.0
    sg = -np.sin(ang) / 16.0
    tab = np.concatenate([cg, sg], axis=1)
    _CONST_CACHE["dftg"] = np.ascontiguousarray(tab.reshape(2, 128, 512).transpose(1, 0, 2)).astype(bf)
    t = np.arange(SEQ, dtype=np.int64)
    dftp = []
    for jq in range(4):
        k = np.arange(jq * OWN, (jq + 1) * OWN, dtype=np.int64)
        prod = (np.outer(t, k) % SEQ).astype(np.float64)
        ang = 2.0 * np.pi * prod / SEQ
        c = (np.cos(ang) / 64.0).astype(np.float32)
        s = (np.sin(ang) / 64.0).astype(np.float32)
        cs = np.stack([c, s], axis=1)
        cs = cs.reshape(32, 128, 2, 2, 512)
        cs = cs.transpose(3, 1, 0, 2, 4)
        dftp.append(np.ascontiguousarray(cs).astype(bf))
    _CONST_CACHE["dftp"] = dftp
    return _CONST_CACHE


def build_nc():
    nc = bass.Bass("TRN2", target_bir_lowering=False)
    rec = Rec(nc)

    def din(name, shape, dt=F32):
        return nc.dram_tensor(name, list(shape), dt, kind="ExternalInput").ap()

    skind = "ExternalOutput" if DEBUG else "Internal"

    def dscr(name, shape, dt):
        return nc.dram_tensor(name, list(shape), dt, kind=skind).ap()

    xb = din("xb", [SEQ, D])
    ctxb = din("ctxb", [CTX, D])
    xown = din("xown", [OWN, D])
    small_r = din("small_r", [64, 128])
    bada_r = din("bada_r", [96, 128])
    lamv = din("lamv", [256])
    gsub = din("gsub", [128, 1])
    gfin = din("gfin", [D])
    w_ada = din("w_ada", [D, 6 * D])
    w_in = din("w_in", [D, 8192])
    w_abr = din("w_abr", [1024, D])
    w_fbr = din("w_fbr", [1024, D])
    w_out = din("w_out", [D, D])
    w_m1 = din("w_m1", [D, DFF])
    w_m2 = din("w_m2", [DFF, D])
    ropeC = din("ropeC", [128, SEQ])
    ropeS = din("ropeS", [128, SEQ])
    ropeCo = din("ropeCo", [128, OWN])
    ropeSo = din("ropeSo", [128, OWN])
    perm_d = din("perm", [128, 128], BF16)
    identb_d = din("ident_bf", [128, 128], BF16)
    identf_d = din("ident_f", [128, 128], F32)
    dftg_d = din("dftg", [128, 2, 512], BF16)
    dftp_d = din("dftp", [2, 128, 32 * 2 * 512], BF16)
    out_d = nc.dram_tensor("out", [OWN, D], F32, kind="ExternalOutput").ap()

    hT_s = dscr("hT_s", [11, 128, 16 * 512], BF16)
    kT_s = dscr("kT_s", [NH, 128, NKEY], BF16)
    vA_s = dscr("vA_s", [NH, 128, NKT * VW], BF16)
    u_s = dscr("u_s", [4, 128, 32 * 512], BF16)
    qT_s = dscr("qT_s", [128, NH * OWN], BF16)
    aT_s = dscr("aT_s", [128, NH * OWN], BF16)
    yfT_s = dscr("yfT_s", [128, 8 * OWN], BF16)
    mT_s = dscr("mT_s", [128, 16 * OWN], BF16)
    xmid_s = dscr("xmid_s", [128, 16 * OWN], F32)
    h2T_s = dscr("h2T_s", [128, 16 * OWN], BF16)
    xout_s = dscr("xout_s", [128, 16 * OWN], F32)
    B_hT = [Buf("hT_s%d" % i) for i in range(11)]
    B_kT = Buf("kT_s")
    B_vA = Buf("vA_s")
    B_u = Buf("u_s")
    B_qT = Buf("qT_s")
    B_aT = Buf("aT_s")
    B_yfT = Buf("yfT_s")
    B_mT = Buf("mT_s")
    B_xmid = Buf("xmid_s")
    B_h2T = Buf("h2T_s")
    B_xout = Buf("xout_s")

    op = rec.op
    dma = rec.dma

    with ExitStack() as top:
        def sbt(es, name, shape, dt):
            return es.enter_context(nc.sbuf_tensor(name, list(shape), dt))

        def pst(es, name, shape, dt):
            return es.enter_context(nc.psum_tensor(name, list(shape), dt))

        vecs = sbt(top, "vecs", [128, 8, 16], F32)
        V_G1, V_SH1, V_G1C, V_SH1C, V_GT1, V_G2, V_SH2, V_GT2 = range(8)
        lamneg = sbt(top, "lamneg", [128, 1], F32)
        gsub_s = sbt(top, "gsub_s", [128, 1], F32)
        nhalf = sbt(top, "nhalf", [128, 512], F32)
        ident_bf = sbt(top, "ident_bf_s", [128, 128], BF16)
        ident_f = sbt(top, "ident_f_s", [128, 128], F32)
        perm = sbt(top, "perm_s", [128, 128], BF16)
        ones_bf = sbt(top, "ones_bf", [128, 128], BF16)
        B_vecs, B_lam, B_gsub, B_nhalf, B_idb, B_idf, B_perm, B_ones = [Buf() for _ in range(8)]
        badaT = sbt(top, "badaT", [128, 96], F32)
        smallT = sbt(top, "smallT", [128, 64], F32)
        s_bf = sbt(top, "s_bf", [128, 16, 2], BF16)
        B_badaT, B_smallT, B_sbf = Buf(), Buf(), Buf()

        with ExitStack() as es:
            bada_t = sbt(es, "bada_t", [96, 128], F32)
            small_t = sbt(es, "small_t", [64, 128], F32)
            lam_t = sbt(es, "lam_t", [128, 256], F32)
            junk64 = sbt(es, "junk64", [128, 64], F32)
            modT = sbt(es, "modT", [128, 32, 2], F32)
            lsc = sbt(es, "lsc", [128, 8], F32)
            wblk = [sbt(es, "wadab%d" % i, [128, 16, 512], BF16) for i in range(3)]
            p_tr1 = pst(es, "p_tr1", [128, 512], F32)
            p_tr2 = pst(es, "p_tr2", [128, 512], F32)
            p_mod = pst(es, "p_mod", [128, 512], F32)
            Bs = {n: Buf(n) for n in ("bada_t", "small_t", "lam_t", "junk64", "modT", "lsc", "p_tr1", "p_tr2", "p_mod")}
            Bs["badaT"], Bs["smallT"], Bs["s_bf"] = B_badaT, B_smallT, B_sbf
            for n_ in ("p_tr1", "p_tr2", "p_mod"):
                Bs[n_].x = True
            Bw = [Buf("wblk%d" % i) for i in range(3)]
            dw = [rec.dsem() for _ in range(3)]
            d0 = rec.dsem()

            dma("sp", lambda e: e.dma_start(out=bada_t[:], in_=bada_r), d0, writes=[Bs["bada_t"]])
            dma("sp", lambda e: e.dma_start(out=small_t[:], in_=small_r), d0, writes=[Bs["small_t"]])
            dma("sp", lambda e: e.dma_start(out=ident_f[:], in_=identf_d), d0, writes=[B_idf])
            dma("sp", lambda e: e.dma_start(out=ident_bf[:], in_=identb_d), d0, writes=[B_idb])
            dma("sp", lambda e: e.dma_start(out=perm[:], in_=perm_d), d0, writes=[B_perm])
            dma("sp", lambda e: e.dma_start(out=lam_t[:], in_=lamv.partition_broadcast(128)), d0,
                writes=[Bs["lam_t"]])
            dma("sp", lambda e: e.dma_start(out=gsub_s[:], in_=gsub), d0, writes=[B_gsub])
            op("dve", lambda e: e.memset(nhalf[:], -0.5), writes=[B_nhalf])
            op("dve", lambda e: e.memset(ones_bf[:], 1.0), writes=[B_ones])

            if SUB == 1:
                rec.flush()
                return nc
            w_ada_v = w_ada.rearrange("(k p) n -> p k n", p=128)
            NB = 8

            def load_wada(cb):
                i = cb % 3
                dma("pool", lambda e, i=i, cb=cb: e.dma_start(out=wblk[i][:], in_=w_ada_v[:, :, cb * 512:(cb + 1) * 512]),
                    dw[i], writes=[Bw[i]])

            load_wada(0)
            load_wada(1)

            op("pe", lambda e: e.transpose(out=p_tr1[:, 0:96], in_=bada_t[:], identity=ident_f[0:96, 0:96]),
               reads=[Bs["bada_t"], B_idf], writes=[Bs["p_tr1"]])
            op("pe", lambda e: e.transpose(out=p_tr2[:, 0:64], in_=small_t[:], identity=ident_f[0:64, 0:64]),
               reads=[Bs["small_t"], B_idf], writes=[Bs["p_tr2"]])
            op("dve", lambda e: e.tensor_copy(out=badaT[:], in_=p_tr1[:, 0:96]), reads=[Bs["p_tr1"]], writes=[Bs["badaT"]])
            op("dve", lambda e: e.tensor_copy(out=smallT[:], in_=p_tr2[:, 0:64]), reads=[Bs["p_tr2"]], writes=[Bs["smallT"]])
            if SUB == 2:
                rec.flush()
                return nc
            for v in range(2):
                op("act", lambda e, v=v: e.activation(out=s_bf[:, :, v], in_=smallT[:, v * 16:(v + 1) * 16], func=AF.Silu),
                   reads=[Bs["smallT"]], writes=[Bs["s_bf"]])

            if SUB == 3:
                rec.flush()
                return nc
            for cb in range(NB):
                if cb + 2 < NB:
                    load_wada(cb + 2)
                i = cb % 3
                for j in range(4):
                    n = cb * 4 + j
                    for kc in range(16):
                        op("pe", lambda e, i=i, j=j, n=n, kc=kc: e.matmul(
                            p_mod[:, n * 2:(n + 1) * 2], lhsT=wblk[i][:, kc, j * 128:(j + 1) * 128], rhs=s_bf[:, kc, :],
                            start=(kc == 0), stop=(kc == 15)),
                           reads=[Bw[i], Bs["s_bf"]], writes=[Bs["p_mod"]], signal=(kc == 15 and j == 3))

            if SUB == 4:
                rec.flush()
                return nc
            for v in range(2):
                op("dve", lambda e, v=v: e.tensor_tensor(out=modT[:, :, v], in0=p_mod[:, v:64:2], in1=badaT[:, 0:32], op=ALU.add),
                   reads=[Bs["p_mod"], Bs["badaT"]], writes=[Bs["modT"]])
            g1T = smallT[:, 32:48]
            g2T = smallT[:, 48:64]
            RB = [Bs["modT"], Bs["smallT"]]
            op("dve", lambda e: e.scalar_tensor_tensor(out=vecs[:, V_G1, :], in0=modT[:, 16:32, 0], scalar=1.0, in1=g1T,
                                                       op0=ALU.add, op1=ALU.mult), reads=RB, writes=[B_vecs])
            op("dve", lambda e: e.scalar_tensor_tensor(out=vecs[:, V_G1C, :], in0=modT[:, 16:32, 1], scalar=1.0, in1=g1T,
                                                       op0=ALU.add, op1=ALU.mult), reads=RB, writes=[B_vecs])
            op("dve", lambda e: e.tensor_copy(out=vecs[:, V_SH1, :], in_=modT[:, 0:16, 0]), reads=RB, writes=[B_vecs])
            op("dve", lambda e: e.tensor_copy(out=vecs[:, V_SH1C, :], in_=modT[:, 0:16, 1]), reads=RB, writes=[B_vecs])
            if SUB == 5:
                rec.flush()
                return nc
            for q in range(2):
                op("dve", lambda e, q=q: e.tensor_tensor(out=junk64[:], in0=lam_t[:, q * 128:q * 128 + 64],
                                                         in1=lam_t[:, q * 128 + 64:q * 128 + 128], op=ALU.mult),
                   reads=[Bs["lam_t"]], writes=[Bs["junk64"]])
                op("dve", lambda e, q=q: e.tensor_reduce(out=lsc[:, q:q + 1], in_=junk64[:], axis=mybir.AxisListType.X, op=ALU.add),
                   reads=[Bs["junk64"]], writes=[Bs["lsc"]])
            if SUB == 6:
                rec.flush()
                return nc
            op("act", lambda e: e.activation(out=lsc[:, 2:4], in_=lsc[:, 0:2], func=AF.Exp), reads=[Bs["lsc"]], writes=[Bs["lsc"]])
            if SUB == 7:
                rec.flush()
                return nc
            op("dve", lambda e: e.tensor_tensor(out=lsc[:, 4:5], in0=lsc[:, 3:4], in1=lsc[:, 2:3], op=ALU.subtract),
               reads=[Bs["lsc"]], writes=[Bs["lsc"]])
            op("dve", lambda e: e.tensor_scalar(out=lamneg[:], in0=lsc[:, 4:5], scalar1=-LAM_INIT, scalar2=None, op0=ALU.add),
               reads=[Bs["lsc"]], writes=[B_lam])
            if SUB == 8:
                rec.flush()
                return nc
            op("dve", lambda e: e.tensor_scalar(out=gsub_s[:], in0=gsub_s[:], scalar1=1.0 - LAM_INIT, scalar2=None, op0=ALU.mult),
               reads=[B_gsub], writes=[B_gsub])
            rec.flush()
            if STOP == 0:
                return nc

        def norm_transpose_group(es_bufs, src_rows, ntile, gvec, shvec, hT_dst, B_hT_dst):
            (xt, Bxt, dxt, xn, Bxn, st, Bst, ptr, Bptr) = es_bufs
            for t in range(ntile):
                i = norm_transpose_group.ctr % 2
                norm_transpose_group.ctr += 1
                dma("sp", lambda e, i=i, t=t: e.dma_start(out=xt[i][:], in_=src_rows[t * 128:(t + 1) * 128, :]), dxt[i],
                    writes=[Bxt[i]])
                op("act", lambda e, i=i: e.activation(out=xn[i][:], in_=xt[i][:], func=AF.Square, accum_out=st[i][:, 0:1]),
                   reads=[Bxt[i]], writes=[Bxn[i], Bst[i]])
                op("dve", lambda e, i=i: e.tensor_scalar(out=st[i][:, 1:2], in0=st[i][:, 0:1], scalar1=1.0 / D, scalar2=EPS,
                                                         op0=ALU.mult, op1=ALU.add), reads=[Bst[i]], writes=[Bst[i]])
                op("pool", lambda e, i=i: e.tensor_tensor(out=st[i][:, 2:3], in0=st[i][:, 1:2], in1=nhalf[:, 0:1], op=ALU.pow),
                   reads=[Bst[i], B_nhalf], writes=[Bst[i]])
                op("act", lambda e, i=i: e.activation(out=xn[i][:], in_=xt[i][:], func=AF.Copy, scale=st[i][:, 2:3]),
                   reads=[Bxt[i], Bst[i]], writes=[Bxn[i]])
                for half in range(2):
                    for k8 in range(8):
                        kc = half * 8 + k8
                        op("pe", lambda e, i=i, half=half, k8=k8, kc=kc: e.transpose(
                            out=ptr[half][:, k8, :], in_=xn[i][:, kc * 128:(kc + 1) * 128], identity=ident_bf[:]),
                           reads=[Bxn[i], B_idb], writes=[Bptr[half]], signal=(k8 == 7))
                    for k8 in range(8):
                        kc = half * 8 + k8
                        op("dve", lambda e, half=half, k8=k8, kc=kc, t=t: e.tensor_scalar(
                            out=hT_dst[:, kc, t * 128:(t + 1) * 128], in0=ptr[half][:, k8, :],
                            scalar1=vecs[:, gvec, kc:kc + 1], scalar2=vecs[:, shvec, kc:kc + 1], op0=ALU.mult, op1=ALU.add),
                           reads=[Bptr[half], B_vecs], writes=[B_hT_dst])

        norm_transpose_group.ctr = 0

        def alloc_norm_bufs(es):
            xt = [sbt(es, "xt%d" % i, [128, D], F32) for i in range(2)]
            xn = [sbt(es, "xn%d" % i, [128, D], BF16) for i in range(2)]
            st = [sbt(es, "nst%d" % i, [128, 4], F32) for i in range(2)]
            ptr = [pst(es, "ptr%d" % i, [128, 8, 128], BF16) for i in range(2)]
            return (xt, [Buf() for _ in range(2)], [rec.dsem() for _ in range(2)], xn, [Buf() for _ in range(2)],
                    st, [Buf() for _ in range(2)], ptr, [PBuf() for _ in range(2)])

        def rope_evac(pk, Bpk, ntok, tabC, tabS, Btab, ksb, Bksb, pk2, Bpk2, t1, Bt1, t2, Bt2, kout, Bkout):
            op("act", lambda e: e.activation(out=ksb[:, 0:ntok], in_=pk[:, 0:ntok], func=AF.Copy), reads=[Bpk], writes=[Bksb])
            op("pe", lambda e: e.matmul(pk2[:, 0:ntok], lhsT=perm[:], rhs=ksb[:, 0:ntok], start=True, stop=True),
               reads=[B_perm, Bksb], writes=[Bpk2])
            op("dve", lambda e: e.tensor_tensor(out=t1[:, 0:ntok], in0=pk[:, 0:ntok], in1=tabC, op=ALU.mult),
               reads=[Bpk, Btab], writes=[Bt1])
            op("dve", lambda e: e.tensor_tensor(out=t2[:, 0:ntok], in0=pk2[:, 0:ntok], in1=tabS, op=ALU.mult),
               reads=[Bpk2, Btab], writes=[Bt2])
            op("pool", lambda e: e.tensor_tensor(out=kout, in0=t1[:, 0:ntok], in1=t2[:, 0:ntok], op=ALU.add),
               reads=[Bt1, Bt2], writes=[Bkout])

        w_in_v = w_in.rearrange("(k p) n -> p k n", p=128)

        with ExitStack() as es:
            xt = [sbt(es, "xt%d" % i, [128, D], F32) for i in range(2)]
            B_xt = [Buf() for _ in range(2)]
            d_xt = [rec.dsem() for _ in range(2)]
            xn = [sbt(es, "xn%d" % i, [128, D], BF16) for i in range(4)]
            B_xn = [Buf() for _ in range(4)]
            nst = [sbt(es, "nst%d" % i, [128, 4], F32) for i in range(4)]
            B_nst = [Buf() for _ in range(4)]
            ptr = [pst(es, "ptr%d" % i, [128, 8, 128], BF16) for i in range(2)]
            B_ptr = [PBuf() for _ in range(2)]
            wkv = sbt(es, "wkv", [128, 16, 2048], BF16)
            B_wkv = Buf()
            d_wkv = rec.dsem()
            hT_g = [sbt(es, "hT_g%d" % i, [128, 16, 512], BF16) for i in range(2)]
            B_hTg = [Buf() for _ in range(2)]
            d_hst = [rec.dsem() for _ in range(2)]
            rC = [sbt(es, "rC%d" % i, [128, 512], F32) for i in range(2)]
            rS = [sbt(es, "rS%d" % i, [128, 512], F32) for i in range(2)]
            B_r = [Buf() for _ in range(2)]
            d_r = [rec.dsem() for _ in range(2)]
            ksb = [sbt(es, "ksbA%d" % i, [128, 512], BF16) for i in range(3)]
            t1 = [sbt(es, "t1A_%d" % i, [128, 512], F32) for i in range(3)]
            t2 = [sbt(es, "t2A_%d" % i, [128, 512], F32) for i in range(2)]
            B_ksb = [Buf() for _ in range(3)]
            B_t1 = [Buf() for _ in range(3)]
            B_t2 = [Buf() for _ in range(2)]
            kout = [sbt(es, "koutA%d" % i, [128, 512], BF16) for i in range(3)]
            B_kout = [Buf() for _ in range(3)]
            d_kout = [rec.dsem() for _ in range(3)]
            vbuf = sbt(es, "vbuf", [128, NH, 4, VW], BF16)
            B_vbuf = Buf()
            d_vbuf = rec.dsem()
            pk = [pst(es, "pk%d" % i, [128, 512], F32) for i in range(3)]
            B_pk = [PBuf() for _ in range(3)]
            pk2 = pst(es, "pkperm", [128, 512], F32)
            B_pk2 = PBuf()
            pv = [pst(es, "pv%d" % i, [128, 512], F32) for i in range(2)]
            B_pv = [PBuf() for _ in range(2)]

            for half in range(2):
                dma("pool", lambda e, half=half: e.dma_start(out=wkv[:, :, half * 1024:(half + 1) * 1024],
                                                             in_=w_in_v[:, :, 1024 + half * 1024:2048 + half * 1024]),
                    d_wkv, writes=[B_wkv])
            op("dve", lambda e: e.memset(vbuf[:, :, :, 128:VW], 1.0), writes=[B_vbuf])

            groups = [("lat", g) for g in range(8)] + [("ctx", 0), ("own", 0), ("own", 1)]
            cn = {"x": 0, "p": 0, "ev": 0, "pk": 0, "pv": 0, "ko": 0}

            def ginfo(gi):
                kind, g = groups[gi]
                ntile = 2 if kind == "ctx" else 4
                if kind == "lat":
                    return kind, g, ntile, xb[g * 512:(g + 1) * 512, :], V_G1, V_SH1
                if kind == "ctx":
                    return kind, g, ntile, ctxb, V_G1C, V_SH1C
                return kind, g, ntile, xown[g * 512:(g + 1) * 512, :], V_G1, V_SH1

            def norm_units(gi):
                kind, g, ntile, src, gv, sv = ginfo(gi)
                hi = gi % 2
                units = []
                for t in range(ntile):
                    def part1(t=t):
                        xi = cn["x"] % 2
                        ni = cn["x"] % 4
                        cn["x"] += 1
                        dma("sp", lambda e: e.dma_start(out=xt[xi][:], in_=src[t * 128:(t + 1) * 128, :]), d_xt[xi], writes=[B_xt[xi]])
                        op("act", lambda e: e.activation(out=xn[ni][:], in_=xt[xi][:], func=AF.Square, accum_out=nst[ni][:, 0:1]),
                           reads=[B_xt[xi]], writes=[B_xn[ni], B_nst[ni]])
                        op("dve", lambda e: e.tensor_scalar(out=nst[ni][:, 1:2], in0=nst[ni][:, 0:1], scalar1=1.0 / D, scalar2=EPS,
                                                            op0=ALU.mult, op1=ALU.add), reads=[B_nst[ni]], writes=[B_nst[ni]])
                        op("pool", lambda e: e.tensor_tensor(out=nst[ni][:, 2:3], in0=nst[ni][:, 1:2], in1=nhalf[:, 0:1], op=ALU.pow),
                           reads=[B_nst[ni], B_nhalf], writes=[B_nst[ni]])
                        op("act", lambda e: e.activation(out=xn[ni][:], in_=xt[xi][:], func=AF.Copy, scale=nst[ni][:, 2:3]),
                           reads=[B_xt[xi], B_nst[ni]], writes=[B_xn[ni]])
                        return ni

                    def part2(ni, t=t):
                        for half in range(2):
                            pi = cn["p"] % 2
                            cn["p"] += 1
                            for k8 in range(8):
                                kc = half * 8 + k8
                                op("pe", lambda e, pi=pi, k8=k8, kc=kc: e.transpose(
                                    out=ptr[pi][:, k8, :], in_=xn[ni][:, kc * 128:(kc + 1) * 128], identity=ident_bf[:]),
                                   reads=[B_xn[ni], B_idb], writes=[B_ptr[pi]], signal=(k8 == 7))
                            useact = (cn["p"] % 2 == 0)
                            for k8 in range(8):
                                kc = half * 8 + k8
                                if useact:
                                    op("act", lambda e, pi=pi, k8=k8, kc=kc: e.activation(
                                        out=hT_g[hi][:, kc, t * 128:(t + 1) * 128], in_=ptr[pi][:, k8, :], func=AF.Identity,
                                        scale=vecs[:, gv, kc:kc + 1], bias=vecs[:, sv, kc:kc + 1]),
                                       reads=[B_ptr[pi], B_vecs], dwrites=[B_hTg[hi]])
                                else:
                                    op("dve", lambda e, pi=pi, k8=k8, kc=kc: e.tensor_scalar(
                                        out=hT_g[hi][:, kc, t * 128:(t + 1) * 128], in0=ptr[pi][:, k8, :],
                                        scalar1=vecs[:, gv, kc:kc + 1], scalar2=vecs[:, sv, kc:kc + 1], op0=ALU.mult, op1=ALU.add),
                                       reads=[B_ptr[pi], B_vecs], dwrites=[B_hTg[hi]])
                    units.append((part1, part2))

                def fin():
                    ntok = ntile * 128
                    dma("sp", lambda e: e.dma_start(out=hT_s[gi].rearrange("p (k t) -> p k t", t=512)[:, :, 0:ntok],
                                                    in_=hT_g[hi][:, :, 0:ntok]), d_hst[hi], reads=[B_hTg[hi]], writes=[B_hT[gi]])
                return units, fin

            def kv_units(gi):
                kind, g, ntile, src, gv, sv = ginfo(gi)
                if kind == "own":
                    return []
                hi = gi % 2
                ntok = ntile * 128
                tok0 = g * 512 if kind == "lat" else SEQ
                units = []
                if kind == "lat":
                    def ld():
                        dma("sp", lambda e: e.dma_start(out=rC[hi][:], in_=ropeC[:, g * 512:(g + 1) * 512]), d_r[hi], writes=[B_r[hi]])
                        dma("sp", lambda e: e.dma_start(out=rS[hi][:], in_=ropeS[:, g * 512:(g + 1) * 512]), d_r[hi], writes=[B_r[hi]])
                    units.append(ld)
                lag = []
                klist = []
                vlist = []
                for h in range(NH):
                    def kunit(h=h):
                        pi = cn["pk"] % 3
                        cn["pk"] += 1
                        for kc in range(16):
                            op("pe", lambda e, kc=kc: e.matmul(
                                pk[pi][:, 0:ntok], lhsT=wkv[:, kc, h * 128:(h + 1) * 128], rhs=hT_g[hi][:, kc, 0:ntok],
                                start=(kc == 0), stop=(kc == 15)),
                               reads=[B_wkv, B_hTg[hi]], writes=[B_pk[pi]], signal=(kc == 15))
                        ko = cn["ko"] % 3
                        t2i = cn["ko"] % 2
                        cn["ko"] += 1
                        if kind == "lat":
                            op("act", lambda e: e.activation(out=ksb[ko][:], in_=pk[pi][:], func=AF.Copy), reads=[B_pk[pi]],
                               writes=[B_ksb[ko]])
                            op("dve", lambda e: e.tensor_tensor(out=t1[ko][:], in0=pk[pi][:], in1=rC[hi][:], op=ALU.mult),
                               reads=[B_pk[pi], B_r[hi]], writes=[B_t1[ko]])

                            def part2():
                                op("pe", lambda e: e.matmul(pk2[:], lhsT=perm[:], rhs=ksb[ko][:], start=True, stop=True),
                                   reads=[B_perm, B_ksb[ko]], writes=[B_pk2])
                                op("dve", lambda e: e.tensor_tensor(out=t2[t2i][:], in0=pk2[:], in1=rS[hi][:], op=ALU.mult),
                                   reads=[B_pk2, B_r[hi]], writes=[B_t2[t2i]])
                                op("pool", lambda e: e.tensor_tensor(out=kout[ko][:], in0=t1[ko][:], in1=t2[t2i][:], op=ALU.add),
                                   reads=[B_t1[ko], B_t2[t2i]], writes=[B_kout[ko]])
                                dma("sp", lambda e: e.dma_start(out=kT_s[h, :, tok0:tok0 + ntok], in_=kout[ko][:, 0:ntok]), d_kout[ko],
                                    reads=[B_kout[ko]], writes=[B_kT])
                            lag.append(part2)
                        else:
                            op("act", lambda e: e.activation(out=kout[ko][:, 0:ntok], in_=pk[pi][:, 0:ntok], func=AF.Copy),
                               reads=[B_pk[pi]], writes=[B_kout[ko]])
                            dma("sp", lambda e: e.dma_start(out=kT_s[h, :, tok0:tok0 + ntok], in_=kout[ko][:, 0:ntok]), d_kout[ko],
                                reads=[B_kout[ko]], writes=[B_kT])
                        if len(lag) > 2:
                            lag.pop(0)()
                    klist.append(kunit)

                def kflush():
                    while lag:
                        lag.pop(0)()
                for t in range(ntile):
                    for cc in range(2):
                        def vunit(t=t, cc=cc):
                            pi = cn["pv"] % 2
                            cn["pv"] += 1
                            for kc in range(16):
                                op("pe", lambda e, kc=kc: e.matmul(
                                    pv[pi][:], lhsT=hT_g[hi][:, kc, t * 128:(t + 1) * 128],
                                    rhs=wkv[:, kc, 1024 + cc * 512:1024 + (cc + 1) * 512], start=(kc == 0), stop=(kc == 15)),
                                   reads=[B_wkv, B_hTg[hi]], writes=[B_pv[pi]], signal=(kc == 15))
                            eng = "act" if cc == 0 else "dve"
                            if eng == "act":
                                op("act", lambda e: e.activation(
                                    out=vbuf[:, cc * 4:(cc + 1) * 4, t, 0:128], in_=pv[pi][:].rearrange("p (h c) -> p h c", c=128),
                                    func=AF.Copy), reads=[B_pv[pi]], dwrites=[B_vbuf])
                            else:
                                op("dve", lambda e: e.tensor_copy(
                                    out=vbuf[:, cc * 4:(cc + 1) * 4, t, 0:128], in_=pv[pi][:].rearrange("p (h c) -> p h c", c=128)),
                                   reads=[B_pv[pi]], dwrites=[B_vbuf])
                        vlist.append(vunit)

                for ui in range(max(len(klist), len(vlist))):
                    if ui < len(klist):
                        units.append(klist[ui])
                    if ui == len(klist) - 1:
                        units.append(kflush)
                    if ui < len(vlist):
                        units.append(vlist[ui])
                def vstore():
                    tile0 = tok0 // 128
                    for h in range(NH):
                        dma("sp", lambda e, h=h: e.dma_start(
                            out=vA_s[h].rearrange("p (t c) -> p t c", c=VW)[:, tile0:tile0 + ntile, :],
                            in_=vbuf[:, h, 0:ntile, :]), d_vbuf, reads=[B_vbuf], writes=[B_vA])
                units.append(vstore)
                return units

            NG = len(groups)
            nu, nfin = norm_units(0)
            for p1, p2 in nu:
                p2(p1())
            nfin()
            for gi in range(NG):
                ku = kv_units(gi)
                if gi + 1 < NG:
                    nu, nfin = norm_units(gi + 1)
                else:
                    nu, nfin = [], None
                nk = max(len(ku), 1)
                ntl = len(nu)
                sched1 = {}
                sched2 = {}
                for t in range(ntl):
                    a = (t * nk) // max(ntl, 1)
                    b = min(nk - 1, a + max(1, nk // (2 * max(ntl, 1))))
                    sched1.setdefault(a, []).append(t)
                    sched2.setdefault(b, []).append(t)
                nis = {}
                if not ku:
                    for p1, p2 in nu:
                        p2(p1())
                else:
                    for ui, u in enumerate(ku):
                        for t in sched1.get(ui, []):
                            nis[t] = nu[t][0]()
                        u()
                        for t in sched2.get(ui, []):
                            nu[t][1](nis[t])
                if nfin is not None:
                    nfin()
            rec.flush()
            if STOP == 1:
                return nc

        with ExitStack() as es:
            wqf = sbt(es, "wqf", [128, 16, 2048], BF16)
            B_wqf = Buf()
            d_wqf = rec.dsem()
            hT_g = [sbt(es, "hT2_g%d" % i, [128, 16, 512], BF16) for i in range(2)]
            B_hTg = [Buf() for _ in range(2)]
            d_hl = [rec.dsem() for _ in range(2)]
            rC = [sbt(es, "rC2%d" % i, [128, 512], F32) for i in range(2)]
            rS = [sbt(es, "rS2%d" % i, [128, 512], F32) for i in range(2)]
            B_r = [Buf() for _ in range(2)]
            d_r = [rec.dsem() for _ in range(2)]
            ksb = sbt(es, "ksb2", [128, 512], BF16)
            t1 = sbt(es, "t12", [128, 512], F32)
            t2 = sbt(es, "t22", [128, 512], F32)
            B_ksb, B_t1, B_t2 = Buf(), Buf(), Buf()
            kout = [sbt(es, "qout%d" % i, [128, 512], BF16) for i in range(2)]
            B_kout = [Buf() for _ in range(2)]
            d_kout = [rec.dsem() for _ in range(2)]
            dftg = sbt(es, "dftg_s", [128, 2, 512], BF16)
            B_dftg = Buf()
            fT = [sbt(es, "fT%d" % i, [128, 8, 512], BF16) for i in range(2)]
            B_fT = [Buf() for _ in range(2)]
            ubuf = [sbt(es, "ubuf%d" % i, [128, 4, 4, 512], BF16) for i in range(2)]
            B_ubuf = [Buf() for _ in range(2)]
            d_ubuf = [rec.dsem() for _ in range(2)]
            pk = [pst(es, "pq%d" % i, [128, 512], F32) for i in range(2)]
            B_pk = [PBuf() for _ in range(2)]
            pk2 = pst(es, "pq2", [128, 512], F32)
            B_pk2 = PBuf()
            pu = [pst(es, "pu%d" % i, [128, 512], F32) for i in range(2)]
            B_pu = [PBuf() for _ in range(2)]

            wblk2 = [sbt(es, "wada2_%d" % i, [128, 16, 512], BF16) for i in range(2)]
            B_wb2 = [Buf() for _ in range(2)]
            d_wb2 = [rec.dsem() for _ in range(2)]
            p_mod2 = pst(es, "p_mod2", [128, 512], F32)
            B_pm2 = PBuf()
            modT2 = sbt(es, "modT2", [128, 64], F32)
            B_modT2 = Buf()
            w_ada_v2 = w_ada.rearrange("(k p) n -> p k n", p=128)

            def load_wada2(cb):
                i = cb % 2
                dma("pool", lambda e, i=i, cb=cb: e.dma_start(out=wblk2[i][:], in_=w_ada_v2[:, :, cb * 512:(cb + 1) * 512]),
                    d_wb2[i], writes=[B_wb2[i]])

            def mm_wada2(cb):
                i = cb % 2
                for j in range(4):
                    n = cb * 4 + j - 32
                    for kc in range(16):
                        op("pe", lambda e, i=i, j=j, n=n, kc=kc: e.matmul(
                            p_mod2[:, n * 2:(n + 1) * 2], lhsT=wblk2[i][:, kc, j * 128:(j + 1) * 128], rhs=s_bf[:, kc, :],
                            start=(kc == 0), stop=(kc == 15)),
                           reads=[B_wb2[i], B_sbf], writes=[B_pm2], signal=(kc == 15 and j == 3))

            dma("pool", lambda e: e.dma_start(out=wqf[:, :, 0:1024], in_=w_in_v[:, :, 0:1024]), d_wqf, writes=[B_wqf])
            dma("pool", lambda e: e.dma_start(out=wqf[:, :, 1024:2048], in_=w_in_v[:, :, 3072:4096]), d_wqf, writes=[B_wqf])
            load_wada2(8)
            d_dftg = rec.dsem()
            dma("sp", lambda e: e.dma_start(out=dftg[:], in_=dftg_d), d_dftg, writes=[B_dftg])
            groups = [("own", 0), ("own", 1)] + [("lat", g) for g in range(8)]
            pctr = 0
            kctr = 0
            def load_hT(gi):
                kind, g = groups[gi]
                i = gi % 2
                hidx = (9 + g) if kind == "own" else g
                dma("sp", lambda e, i=i, hidx=hidx: e.dma_start(out=hT_g[i][:], in_=hT_s[hidx].rearrange("p (k t) -> p k t", t=512)),
                    d_hl[i], reads=[B_hT[hidx]], writes=[B_hTg[i]])

            load_hT(0)
            for gi, (kind, g) in enumerate(groups):
                i = gi % 2
                if gi + 1 < len(groups):
                    load_hT(gi + 1)
                if kind == "own":
                    dma("sp", lambda e, i=i, g=g: e.dma_start(out=rC[i][:], in_=ropeCo[:, g * 512:(g + 1) * 512]), d_r[i],
                        writes=[B_r[i]])
                    dma("sp", lambda e, i=i, g=g: e.dma_start(out=rS[i][:], in_=ropeSo[:, g * 512:(g + 1) * 512]), d_r[i],
                        writes=[B_r[i]])
                    for h in range(NH):
                        pi = pctr % 2
                        pctr += 1
                        for kc in range(16):
                            op("pe", lambda e, i=i, pi=pi, h=h, kc=kc: e.matmul(
                                pk[pi][:], lhsT=wqf[:, kc, h * 128:(h + 1) * 128], rhs=hT_g[i][:, kc, :],
                                start=(kc == 0), stop=(kc == 15)),
                               reads=[B_wqf, B_hTg[i]], writes=[B_pk[pi]], signal=(kc == 15))
                        ko = kctr % 2
                        kctr += 1
                        rope_evac(pk[pi], B_pk[pi], 512, rC[i][:], rS[i][:], B_r[i], ksb, B_ksb, pk2, B_pk2, t1, B_t1, t2, B_t2,
                                  kout[ko][:], B_kout[ko])
                        dma("sp", lambda e, ko=ko, h=h, g=g: e.dma_start(
                            out=qT_s[:, h * OWN + g * 512:h * OWN + (g + 1) * 512], in_=kout[ko][:]), d_kout[ko],
                            reads=[B_kout[ko]], writes=[B_qT])
                    continue
                fi = g % 2
                load_wada2(9 + 2 * g)
                mm_wada2(8 + 2 * g)
                if g < 7:
                    load_wada2(10 + 2 * g)
                for ct in range(8):
                    pi = pctr % 2
                    pctr += 1
                    for kc in range(16):
                        op("pe", lambda e, i=i, pi=pi, ct=ct, kc=kc: e.matmul(
                            pk[pi][:], lhsT=wqf[:, kc, 1024 + ct * 128:1024 + (ct + 1) * 128], rhs=hT_g[i][:, kc, :],
                            start=(kc == 0), stop=(kc == 15)),
                           reads=[B_wqf, B_hTg[i]], writes=[B_pk[pi]], signal=(kc == 15))
                    op("act", lambda e, pi=pi, fi=fi, ct=ct: e.activation(out=fT[fi][:, ct, :], in_=pk[pi][:], func=AF.Copy),
                       reads=[B_pk[pi]], dwrites=[B_fT[fi]])
                for gg in range(4):
                    for t in range(4):
                        ui = (gg * 4 + t) % 2
                        for k2 in range(2):
                            op("pe", lambda e, fi=fi, ui=ui, gg=gg, t=t, k2=k2: e.matmul(
                                pu[ui][:], lhsT=fT[fi][:, gg * 2 + k2, t * 128:(t + 1) * 128], rhs=dftg[:, k2, :],
                                start=(k2 == 0), stop=(k2 == 1)),
                               reads=[B_fT[fi], B_dftg], writes=[B_pu[ui]], signal=(k2 == 1))
                        if (gg * 4 + t) % 2 == 0:
                            op("dve", lambda e, fi=fi, ui=ui, gg=gg, t=t: e.tensor_copy(out=ubuf[fi][:, gg, t, :], in_=pu[ui][:]),
                               reads=[B_pu[ui]], dwrites=[B_ubuf[fi]])
                        else:
                            op("act", lambda e, fi=fi, ui=ui, gg=gg, t=t: e.activation(out=ubuf[fi][:, gg, t, :], in_=pu[ui][:],
                                                                                      func=AF.Copy),
                               reads=[B_pu[ui]], dwrites=[B_ubuf[fi]])
                for gg in range(4):
                    dma("sp", lambda e, fi=fi, gg=gg, g=g: e.dma_start(
                        out=u_s[gg].rearrange("p (t c) -> p t c", c=512)[:, g * 4:(g + 1) * 4, :], in_=ubuf[fi][:, gg, :, :]),
                        d_ubuf[fi], reads=[B_ubuf[fi]], writes=[B_u])
                mm_wada2(9 + 2 * g)
            op("dve", lambda e: e.tensor_tensor(out=modT2[:], in0=p_mod2[:, 0:128:2], in1=badaT[:, 32:96], op=ALU.add),
               reads=[B_pm2, B_badaT], writes=[B_modT2])
            op("dve", lambda e: e.tensor_copy(out=vecs[:, V_GT1, :], in_=modT2[:, 0:16]), reads=[B_modT2], dwrites=[B_vecs])
            op("dve", lambda e: e.tensor_copy(out=vecs[:, V_SH2, :], in_=modT2[:, 16:32]), reads=[B_modT2], dwrites=[B_vecs])
            op("dve", lambda e: e.scalar_tensor_tensor(out=vecs[:, V_G2, :], in0=modT2[:, 32:48], scalar=1.0, in1=smallT[:, 48:64],
                                                       op0=ALU.add, op1=ALU.mult), reads=[B_modT2, B_smallT], dwrites=[B_vecs])
            op("dve", lambda e: e.tensor_copy(out=vecs[:, V_GT2, :], in_=modT2[:, 48:64]), reads=[B_modT2], dwrites=[B_vecs])
            rec.flush()
            if STOP == 2:
                return nc

        with ExitStack() as es:
            qT = sbt(es, "qT", [128, NH, OWN], BF16)
            B_q = Buf()
            d_q = rec.dsem()
            kTh = [sbt(es, "kTh%d" % i, [128, NKEY], BF16) for i in range(2)]
            vAh = [sbt(es, "vAh%d" % i, [128, NKT, VW], BF16) for i in range(2)]
            B_kv = [Buf() for _ in range(2)]
            d_kv = [rec.dsem() for _ in range(2)]
            NE = 3
            E = [[sbt(es, "E%d_%d" % (i, c), [128, 512], BF16) for c in range(2)] for i in range(NE)]
            B_E = [[Buf() for c in range(2)] for i in range(NE)]
            osb = sbt(es, "osb", [128, 128], F32)
            o2 = sbt(es, "o2", [128, 128], F32)
            junk = sbt(es, "ajunk", [128, 128], F32)
            abf = sbt(es, "abf", [128, 128], BF16)
            sc = sbt(es, "asc", [128, 8], F32)
            B_osb, B_o2, B_junk, B_abf, B_sc = [Buf() for _ in range(5)]
            aT = sbt(es, "aT", [128, NH, OWN], BF16)
            B_aTs = Buf()
            d_a = rec.dsem()
            pS = [[pst(es, "pS%d_%d" % (i, c), [128, 512], F32) for c in range(2)] for i in range(2)]
            B_pS = [[PBuf() for c in range(2)] for i in range(2)]
            pO = [pst(es, "pO%d" % i, [128, 512], F32) for i in range(3)]
            B_pO = [PBuf() for _ in range(3)]
            pT = pst(es, "pT", [128, 8, 128], BF16)
            B_pT = PBuf()

            dma("sp", lambda e: e.dma_start(out=qT[:], in_=qT_s.rearrange("p (h t) -> p h t", t=OWN)), d_q, reads=[B_qT],
                writes=[B_q])

            def load_kv(h):
                i = h % 2
                dma("sp", lambda e, i=i, h=h: e.dma_start(out=kTh[i][:], in_=kT_s[h]), d_kv[i], reads=[B_kT], writes=[B_kv[i]])
                dma("sp", lambda e, i=i, h=h: e.dma_start(out=vAh[i][:], in_=vA_s[h].rearrange("p (t c) -> p t c", c=VW)),
                    d_kv[i], reads=[B_vA], writes=[B_kv[i]])

            ocp = [[sbt(es, "ocp%d_%d" % (i, b), [128, 3 * VW], F32) for b in range(3)] for i in range(2)]
            B_ocp = [[Buf() for b in range(3)] for i in range(2)]
            load_kv(0)
            steps = [(h, qc, kt) for h in range(NH) for qc in range(2) for kt in range(NKT)]
            NS = len(steps)

            def emit_qk(s):
                h, qc, kt = steps[s]
                hi, si, q0 = h % 2, s % 2, qc * 512
                for c in range(2):
                    op("pe", lambda e, hi=hi, si=si, c=c, kt=kt, h=h, q0=q0: e.matmul(
                        pS[si][c][:], lhsT=kTh[hi][c * 64:(c + 1) * 64, kt * 128:(kt + 1) * 128],
                        rhs=qT[c * 64:(c + 1) * 64, h, q0:q0 + 512], start=True, stop=True),
                       reads=[B_kv[hi], B_q], writes=[B_pS[si][c]])

            def emit_exp(s):
                si, ei = s % 2, s % NE
                for c in range(2):
                    op("act", lambda e, si=si, ei=ei, c=c: e.activation(out=E[ei][c][:], in_=pS[si][c][:], func=AF.Exp),
                       reads=[B_pS[si][c]], writes=[B_E[ei][c]])

            def emit_av(s):
                h, qc, kt = steps[s]
                hi, ei = h % 2, s % NE
                for c in range(2):
                    for qs in range(4):
                        idx = c * 4 + qs
                        bank, pos = idx // 3, idx % 3
                        op("pe", lambda e, ei=ei, c=c, qs=qs, bank=bank, pos=pos, hi=hi, kt=kt: e.matmul(
                            pO[bank][:, pos * VW:(pos + 1) * VW], lhsT=E[ei][c][:, qs * 128:(qs + 1) * 128],
                            rhs=vAh[hi][:, kt, :], start=(kt == 0 and pos == 0), stop=(kt == NKT - 1),
                            skip_group_check=True),
                           reads=[B_E[ei][c], B_kv[hi]], writes=[B_pO[bank]], signal=(qs == 3))

            ectr = [0, 0]
            abfs = [sbt(es, "abfs%d" % i, [128, 128], BF16) for i in range(8)]
            B_abfs = [Buf() for _ in range(8)]

            def emit_epi_a(h, qc):
                oi = ectr[0] % 2
                ectr[0] += 1
                for b in range(3):
                    nv = (3 if b < 2 else 2) * VW
                    op("dve", lambda e, oi=oi, b=b, nv=nv: e.tensor_copy(out=ocp[oi][b][:, 0:nv], in_=pO[b][:, 0:nv]),
                       reads=[B_pO[b]], writes=[B_ocp[oi][b]])
                for qs in range(4):
                    b0, p0 = qs // 3, qs % 3
                    b1, p1 = (4 + qs) // 3, (4 + qs) % 3
                    O0 = ocp[oi][b0][:, p0 * VW:p0 * VW + 128]
                    S0 = ocp[oi][b0][:, p0 * VW + 128:p0 * VW + 129]
                    O1 = ocp[oi][b1][:, p1 * VW:p1 * VW + 128]
                    S1 = ocp[oi][b1][:, p1 * VW + 128:p1 * VW + 129]
                    R0, R1 = B_ocp[oi][b0], B_ocp[oi][b1]
                    ab, Bab = abfs[oi * 4 + qs], B_abfs[oi * 4 + qs]
                    op("dve", lambda e, S0=S0: e.reciprocal(out=sc[:, 0:1], in_=S0), reads=[R0], writes=[B_sc])
                    op("dve", lambda e, S1=S1: e.reciprocal(out=sc[:, 1:2], in_=S1), reads=[R1], writes=[B_sc])
                    op("dve", lambda e: e.tensor_tensor(out=sc[:, 2:3], in0=sc[:, 1:2], in1=lamneg[:], op=ALU.mult),
                       reads=[B_sc, B_lam], writes=[B_sc])
                    op("dve", lambda e, O0=O0: e.tensor_scalar(out=osb[:], in0=O0, scalar1=sc[:, 0:1], scalar2=None, op0=ALU.mult),
                       reads=[R0, B_sc], writes=[B_osb])
                    op("dve", lambda e, O1=O1: e.scalar_tensor_tensor(out=o2[:], in0=O1, scalar=sc[:, 2:3], in1=osb[:],
                                                                      op0=ALU.mult, op1=ALU.add),
                       reads=[R1, B_sc, B_osb], writes=[B_o2])
                    op("dve", lambda e: e.tensor_tensor(out=junk[:], in0=o2[:], in1=o2[:], op=ALU.mult),
                       reads=[B_o2], writes=[B_junk])
                    op("dve", lambda e: e.tensor_reduce(out=sc[:, 3:4], in_=junk[:], axis=mybir.AxisListType.X, op=ALU.add),
                       reads=[B_junk], writes=[B_sc])
                    op("dve", lambda e: e.tensor_scalar(out=sc[:, 4:5], in0=sc[:, 3:4], scalar1=1.0 / 128, scalar2=EPS,
                                                        op0=ALU.mult, op1=ALU.add), reads=[B_sc], writes=[B_sc])
                    op("pool", lambda e: e.tensor_tensor(out=sc[:, 5:6], in0=sc[:, 4:5], in1=nhalf[:, 0:1], op=ALU.pow),
                       reads=[B_sc, B_nhalf], writes=[B_sc])
                    op("dve", lambda e, ab=ab: e.tensor_scalar(out=ab[:], in0=o2[:], scalar1=sc[:, 5:6], scalar2=None, op0=ALU.mult),
                       reads=[B_o2, B_sc], writes=[Bab])
                return oi

            def emit_epi_b(h, qc, oi):
                q0 = qc * 512
                for qs in range(4):
                    ab, Bab = abfs[oi * 4 + qs], B_abfs[oi * 4 + qs]
                    ti = ectr[1] % 8
                    ectr[1] += 1
                    op("pe", lambda e, ti=ti, ab=ab: e.transpose(out=pT[:, ti, :], in_=ab[:], identity=ident_bf[:]),
                       reads=[Bab, B_idb], writes=[B_pT])
                    op("dve", lambda e, ti=ti, h=h, q0=q0, qs=qs: e.tensor_scalar(
                        out=aT[:, h, q0 + qs * 128:q0 + (qs + 1) * 128], in0=pT[:, ti, :], scalar1=gsub_s[:, 0:1], scalar2=None,
                        op0=ALU.mult), reads=[B_pT, B_gsub], dwrites=[B_aTs])

            pend = []
            emit_qk(0)
            emit_qk(1)
            for s in range(NS):
                h, qc, kt = steps[s]
                if qc == 0 and kt == 0 and h + 1 < NH:
                    load_kv(h + 1)
                emit_exp(s)
                emit_av(s)
                if s + 2 < NS:
                    emit_qk(s + 2)
                if kt == NKT - 1:
                    oi = emit_epi_a(h, qc)
                    pend.append((s + 10, h, qc, oi))
                while pend and pend[0][0] <= s:
                    _, ph, pqc, poi = pend.pop(0)
                    emit_epi_b(ph, pqc, poi)
            for _, ph, pqc, poi in pend:
                emit_epi_b(ph, pqc, poi)
            dma("sp", lambda e: e.dma_start(out=aT_s.rearrange("p (h t) -> p h t", t=OWN), in_=aT[:]), d_a, reads=[B_aTs],
                writes=[B_aT])
            rec.flush()
            if STOP == 3:
                return nc

        with ExitStack() as es:
            tab = sbt(es, "ptab", [128, 32, 2, 512], BF16)
            B_tab = Buf()
            d_tab = rec.dsem()
            ug = [sbt(es, "ug%d" % i, [128, 32, 512], BF16) for i in range(2)]
            B_ug = [Buf() for _ in range(2)]
            d_ug = [rec.dsem() for _ in range(2)]
            yfT = sbt(es, "yfT", [128, 8, OWN], BF16)
            B_yf = Buf()
            d_yf = rec.dsem()
            py = [pst(es, "pyf%d" % i, [128, 512], F32) for i in range(2)]
            B_py = [PBuf() for _ in range(2)]
            uctr = 0
            pctr = 0
            for kch in range(2):
                dma("sp", lambda e, kch=kch: e.dma_start(out=tab[:], in_=dftp_d[kch].rearrange("p (t c k) -> p t c k", c=2, k=512)),
                    d_tab, writes=[B_tab])
                for gg in range(4):
                    ui = uctr % 2
                    uctr += 1
                    dma("sp", lambda e, ui=ui, gg=gg: e.dma_start(out=ug[ui][:], in_=u_s[gg].rearrange("p (t c) -> p t c", c=512)),
                        d_ug[ui], reads=[B_u], writes=[B_ug[ui]])
                    for half in range(2):
                        pi = pctr % 2
                        pctr += 1
                        for tt in range(32):
                            for cs in range(2):
                                op("pe", lambda e, ui=ui, pi=pi, half=half, tt=tt, cs=cs: e.matmul(
                                    py[pi][:], lhsT=ug[ui][:, tt, cs * 256 + half * 128:cs * 256 + (half + 1) * 128],
                                    rhs=tab[:, tt, cs, :], start=(tt == 0 and cs == 0), stop=(tt == 31 and cs == 1)),
                                   reads=[B_ug[ui], B_tab], writes=[B_py[pi]], signal=(tt == 31 and cs == 1))
                        op("act", lambda e, pi=pi, gg=gg, half=half, kch=kch: e.activation(
                            out=yfT[:, gg * 2 + half, kch * 512:(kch + 1) * 512], in_=py[pi][:], func=AF.Copy),
                           reads=[B_py[pi]], dwrites=[B_yf])
            dma("sp", lambda e: e.dma_start(out=yfT_s.rearrange("p (c t) -> p c t", t=OWN), in_=yfT[:]), d_yf, reads=[B_yf],
                writes=[B_yfT])
            rec.flush()
            if STOP == 4:
                return nc

        with ExitStack() as es:
            hTo = sbt(es, "hTo", [128, 16, OWN], BF16)
            aT = sbt(es, "aT5", [128, 8, OWN], BF16)
            yfT = sbt(es, "yfT5", [128, 8, OWN], BF16)
            B_in = Buf()
            d_in = rec.dsem()
            mT = sbt(es, "mT", [128, 16, OWN], BF16)
            B_m = Buf()
            d_m = rec.dsem()
            wb = [sbt(es, "wb5_%d" % i, [128, 48, 256], BF16) for i in range(2)]
            B_wb = [Buf() for _ in range(2)]
            d_wb = [rec.dsem() for _ in range(2)]
            sg = [sbt(es, "sg%d" % i, [128, 512], F32) for i in range(2)]
            m1 = [sbt(es, "m1_%d" % i, [128, 512], F32) for i in range(2)]
            B_sg = [Buf() for _ in range(2)]
            B_m1 = [Buf() for _ in range(2)]
            pp = [[pst(es, "p5_%d_%d" % (i, k), [128, 512], F32) for k in range(4)] for i in range(2)]
            B_pp = [[PBuf() for k in range(4)] for i in range(2)]

            for g in range(2):
                dma("sp", lambda e, g=g: e.dma_start(out=hTo[:, :, g * 512:(g + 1) * 512],
                                                      in_=hT_s[9 + g].rearrange("p (k t) -> p k t", t=512)), d_in,
                    reads=[B_hT[9 + g]], writes=[B_in])
            dma("sp", lambda e: e.dma_start(out=aT[:], in_=aT_s.rearrange("p (h t) -> p h t", t=OWN)), d_in, reads=[B_aT],
                writes=[B_in])
            dma("sp", lambda e: e.dma_start(out=yfT[:], in_=yfT_s.rearrange("p (c t) -> p c t", t=OWN)), d_in, reads=[B_yfT],
                writes=[B_in])
            w_abr_v = w_abr.rearrange("(k p) n -> p k n", p=128)
            w_fbr_v = w_fbr.rearrange("(k p) n -> p k n", p=128)

            def load_wb(nbk):
                i = nbk % 2
                c0 = nbk * 256
                dma("pool", lambda e, i=i, c0=c0: e.dma_start(out=wb[i][:, 0:8, :], in_=w_abr_v[:, :, c0:c0 + 256]), d_wb[i],
                    writes=[B_wb[i]])
                dma("pool", lambda e, i=i, c0=c0: e.dma_start(out=wb[i][:, 8:16, :], in_=w_fbr_v[:, :, c0:c0 + 256]), d_wb[i],
                    writes=[B_wb[i]])
                dma("pool", lambda e, i=i, c0=c0: e.dma_start(out=wb[i][:, 16:32, :], in_=w_in_v[:, :, 4096 + c0:4096 + c0 + 256]),
                    d_wb[i], writes=[B_wb[i]])
                dma("pool", lambda e, i=i, c0=c0: e.dma_start(out=wb[i][:, 32:48, :], in_=w_in_v[:, :, 6144 + c0:6144 + c0 + 256]),
                    d_wb[i], writes=[B_wb[i]])

            load_wb(0)
            pctr = 0
            for nbk in range(8):
                if nbk + 1 < 8:
                    load_wb(nbk + 1)
                wi = nbk % 2
                for j in range(2):
                    n = nbk * 2 + j
                    for ch in range(2):
                        pi = pctr % 2
                        pctr += 1
                        specs = [(0, 8, aT), (8, 8, yfT), (16, 16, hTo), (32, 16, hTo)]
                        for k, (w0, nk, src) in enumerate(specs):
                            for kc in range(nk):
                                op("pe", lambda e, wi=wi, pi=pi, k=k, w0=w0, kc=kc, nk=nk, src=src, j=j, ch=ch: e.matmul(
                                    pp[pi][k][:], lhsT=wb[wi][:, w0 + kc, j * 128:(j + 1) * 128],
                                    rhs=src[:, kc, ch * 512:(ch + 1) * 512], start=(kc == 0), stop=(kc == nk - 1)),
                                   reads=[B_wb[wi], B_in], writes=[B_pp[pi][k]], signal=(kc == nk - 1))
                        op("act", lambda e, pi=pi: e.activation(out=sg[0][:], in_=pp[pi][2][:], func=AF.Sigmoid),
                           reads=[B_pp[pi][2]], writes=[B_sg[0]])
                        op("act", lambda e, pi=pi: e.activation(out=sg[1][:], in_=pp[pi][3][:], func=AF.Sigmoid),
                           reads=[B_pp[pi][3]], writes=[B_sg[1]])
                        op("dve", lambda e, pi=pi: e.tensor_tensor(out=m1[0][:], in0=pp[pi][0][:], in1=sg[0][:], op=ALU.mult),
                           reads=[B_pp[pi][0], B_sg[0]], writes=[B_m1[0]])
                        op("dve", lambda e, pi=pi: e.tensor_tensor(out=m1[1][:], in0=pp[pi][1][:], in1=sg[1][:], op=ALU.mult),
                           reads=[B_pp[pi][1], B_sg[1]], writes=[B_m1[1]])
                        op("pool", lambda e, n=n, ch=ch: e.tensor_tensor(out=mT[:, n, ch * 512:(ch + 1) * 512], in0=m1[0][:],
                                                                         in1=m1[1][:], op=ALU.add),
                           reads=[B_m1[0], B_m1[1]], dwrites=[B_m])
            dma("sp", lambda e: e.dma_start(out=mT_s.rearrange("p (k t) -> p k t", t=OWN), in_=mT[:]), d_m, reads=[B_m],
                writes=[B_mT])
            rec.flush()
            if STOP == 5:
                return nc

        with ExitStack() as es:
            mT = sbt(es, "mT6", [128, 16, OWN], BF16)
            B_m = Buf()
            d_m = rec.dsem()
            xT = sbt(es, "xT6", [128, 16, OWN], F32)
            B_xT = Buf()
            d_x = rec.dsem()
            xt = [sbt(es, "xt6_%d" % i, [128, D], F32) for i in range(2)]
            B_xt = [Buf() for _ in range(2)]
            d_xt = [rec.dsem() for _ in range(2)]
            wo = [sbt(es, "wo%d" % i, [128, 16, 256], BF16) for i in range(2)]
            B_wo = [Buf() for _ in range(2)]
            d_wo = [rec.dsem() for _ in range(2)]
            sq = [sbt(es, "sq%d" % i, [128, 512], BF16) for i in range(2)]
            B_sq = [Buf() for _ in range(2)]
            ms = sbt(es, "ms6", [128, 512], F32)
            rstd = sbt(es, "rstd6", [128, 512], F32)
            tmp = [sbt(es, "tmp6_%d" % i, [128, 512], F32) for i in range(2)]
            B_ms, B_rstd = Buf(), Buf()
            B_tmp = [Buf() for _ in range(2)]
            h2s = [sbt(es, "h2s%d" % i, [128, 512], BF16) for i in range(2)]
            B_h2s = [Buf() for _ in range(2)]
            d_h2 = [rec.dsem() for _ in range(2)]
            ptx = [pst(es, "ptx%d" % i, [128, 4, 128], F32) for i in range(2)]
            B_ptx = [PBuf() for _ in range(2)]
            pyo = [pst(es, "pyo%d" % i, [128, 512], F32) for i in range(2)]
            B_pyo = [PBuf() for _ in range(2)]
            pss = pst(es, "pss", [128, 512], F32)
            B_pss = PBuf()

            dma("sp", lambda e: e.dma_start(out=mT[:], in_=mT_s.rearrange("p (k t) -> p k t", t=OWN)), d_m, reads=[B_mT],
                writes=[B_m])
            w_out_v = w_out.rearrange("(k p) n -> p k n", p=128)

            def load_wo(nbk):
                i = nbk % 2
                dma("pool", lambda e, i=i, nbk=nbk: e.dma_start(out=wo[i][:], in_=w_out_v[:, :, nbk * 256:(nbk + 1) * 256]),
                    d_wo[i], writes=[B_wo[i]])

            load_wo(0)
            tcn = 0
            for t in range(8):
                i = t % 2
                dma("sp", lambda e, i=i, t=t: e.dma_start(out=xt[i][:], in_=xown[t * 128:(t + 1) * 128, :]), d_xt[i],
                    writes=[B_xt[i]])
                for k4 in range(4):
                    pi = tcn % 2
                    tcn += 1
                    for k in range(4):
                        kc = k4 * 4 + k
                        op("pe", lambda e, i=i, pi=pi, k=k, kc=kc: e.transpose(out=ptx[pi][:, k, :], in_=xt[i][:, kc * 128:(kc + 1) * 128],
                                                                                identity=ident_f[:]),
                           reads=[B_xt[i], B_idf], writes=[B_ptx[pi]], signal=(k == 3))
                    op("act", lambda e, pi=pi, k4=k4, t=t: e.activation(out=xT[:, k4 * 4:(k4 + 1) * 4, t * 128:(t + 1) * 128],
                                                                         in_=ptx[pi][:], func=AF.Copy),
                       reads=[B_ptx[pi]], dwrites=[B_xT])
            pctr = 0
            for nbk in range(8):
                if nbk + 1 < 8:
                    load_wo(nbk + 1)
                wi = nbk % 2
                for j in range(2):
                    n = nbk * 2 + j
                    for ch in range(2):
                        pi = pctr % 2
                        pctr += 1
                        for kc in range(16):
                            op("pe", lambda e, wi=wi, pi=pi, kc=kc, j=j, ch=ch: e.matmul(
                                pyo[pi][:], lhsT=wo[wi][:, kc, j * 128:(j + 1) * 128], rhs=mT[:, kc, ch * 512:(ch + 1) * 512],
                                start=(kc == 0), stop=(kc == 15)),
                               reads=[B_wo[wi], B_m], writes=[B_pyo[pi]], signal=(kc == 15))
                        op("dve", lambda e, pi=pi, n=n, ch=ch: e.scalar_tensor_tensor(
                            out=xT[:, n, ch * 512:(ch + 1) * 512], in0=pyo[pi][:], scalar=vecs[:, V_GT1, n:n + 1],
                            in1=xT[:, n, ch * 512:(ch + 1) * 512], op0=ALU.mult, op1=ALU.add),
                           reads=[B_pyo[pi], B_vecs, B_xT], dwrites=[B_xT])
            dma("sp", lambda e: e.dma_start(out=xmid_s.rearrange("p (k t) -> p k t", t=OWN), in_=xT[:]), d_x, reads=[B_xT],
                writes=[B_xmid])
            hctr = 0
            for ch in range(2):
                for n in range(16):
                    si = n % 2
                    op("act", lambda e, si=si, n=n, ch=ch: e.activation(out=sq[si][:], in_=xT[:, n, ch * 512:(ch + 1) * 512],
                                                                         func=AF.Square), reads=[B_xT], writes=[B_sq[si]])
                    op("pe", lambda e, si=si, n=n: e.matmul(pss[:], lhsT=ones_bf[:], rhs=sq[si][:], start=(n == 0), stop=(n == 15)),
                       reads=[B_ones, B_sq[si]], writes=[B_pss])
                op("dve", lambda e: e.tensor_scalar(out=ms[:], in0=pss[:], scalar1=1.0 / D, scalar2=EPS, op0=ALU.mult, op1=ALU.add),
                   reads=[B_pss], writes=[B_ms])
                op("act", lambda e: e.activation(out=rstd[:], in_=ms[:], func=AF.Sqrt), reads=[B_ms], writes=[B_rstd])
                op("dve", lambda e: e.reciprocal(out=rstd[:], in_=rstd[:]), reads=[B_rstd], writes=[B_rstd])
                for n in range(16):
                    ti = hctr % 2
                    hctr += 1
                    op("dve", lambda e, ti=ti, n=n, ch=ch: e.tensor_tensor(out=tmp[ti][:], in0=xT[:, n, ch * 512:(ch + 1) * 512],
                                                                           in1=rstd[:], op=ALU.mult),
                       reads=[B_xT, B_rstd], writes=[B_tmp[ti]])
                    op("dve", lambda e, ti=ti, n=n: e.tensor_scalar(out=h2s[ti][:], in0=tmp[ti][:], scalar1=vecs[:, V_G2, n:n + 1],
                                                                    scalar2=vecs[:, V_SH2, n:n + 1], op0=ALU.mult, op1=ALU.add),
                       reads=[B_tmp[ti], B_vecs], writes=[B_h2s[ti]])
                    dma("sp", lambda e, ti=ti, n=n, ch=ch: e.dma_start(
                        out=h2T_s[:, n * OWN + ch * 512:n * OWN + (ch + 1) * 512], in_=h2s[ti][:]), d_h2[ti],
                        reads=[B_h2s[ti]], writes=[B_h2T])
            rec.flush()
            if STOP == 6:
                return nc

        with ExitStack() as es:
            zT = sbt(es, "zT", [128, 64, OWN], BF16)
            B_z = Buf()
            with ExitStack() as es1:
                h2T = sbt(es1, "h2T", [128, 16, OWN], BF16)
                B_h2 = Buf()
                d_h2l = rec.dsem()
                w1 = [sbt(es1, "w1_%d" % i, [128, 16, 512], BF16) for i in range(2)]
                B_w1 = [Buf() for _ in range(2)]
                d_w1 = [rec.dsem() for _ in range(2)]
                rl = [sbt(es1, "rl%d" % i, [128, 512], F32) for i in range(2)]
                B_rl = [Buf() for _ in range(2)]
                pz = [pst(es1, "pz%d" % i, [128, 512], F32) for i in range(4)]
                B_pz = [PBuf() for _ in range(4)]
                dma("sp", lambda e: e.dma_start(out=h2T[:], in_=h2T_s.rearrange("p (k t) -> p k t", t=OWN)), d_h2l,
                    reads=[B_h2T], writes=[B_h2])
                w_m1_v = w_m1.rearrange("(k p) n -> p k n", p=128)

                def load_w1(cb):
                    i = cb % 2
                    dma("pool", lambda e, i=i, cb=cb: e.dma_start(out=w1[i][:], in_=w_m1_v[:, :, cb * 512:(cb + 1) * 512]),
                        d_w1[i], writes=[B_w1[i]])

                load_w1(0)
                pctr = 0
                for cb in range(16):
                    if cb + 1 < 16:
                        load_w1(cb + 1)
                    wi = cb % 2
                    for j in range(4):
                        f = cb * 4 + j
                        for ch in range(2):
                            pi = pctr % 4
                            ri = pctr % 2
                            pctr += 1
                            for kc in range(16):
                                op("pe", lambda e, wi=wi, pi=pi, kc=kc, j=j, ch=ch: e.matmul(
                                    pz[pi][:], lhsT=w1[wi][:, kc, j * 128:(j + 1) * 128], rhs=h2T[:, kc, ch * 512:(ch + 1) * 512],
                                    start=(kc == 0), stop=(kc == 15)),
                                   reads=[B_w1[wi], B_h2], writes=[B_pz[pi]], signal=(kc == 15))
                            op("act", lambda e, pi=pi, ri=ri: e.activation(out=rl[ri][:], in_=pz[pi][:], func=AF.Relu),
                               reads=[B_pz[pi]], writes=[B_rl[ri]])
                            eng = "dve" if (pctr % 2 == 0) else "pool"
                            op(eng, lambda e, ri=ri, f=f, ch=ch: e.tensor_tensor(out=zT[:, f, ch * 512:(ch + 1) * 512], in0=rl[ri][:],
                                                                                  in1=rl[ri][:], op=ALU.mult),
                               reads=[B_rl[ri]], dwrites=[B_z])
                rec.flush()
                if STOP == 7:
                    return nc
            with ExitStack() as es2:
                w2 = [sbt(es2, "w2_%d" % i, [128, 64, 128], BF16) for i in range(3)]
                B_w2 = [Buf() for _ in range(3)]
                d_w2 = [rec.dsem() for _ in range(3)]
                xm = [sbt(es2, "xm%d" % i, [128, 512], F32) for i in range(2)]
                B_xm = [Buf() for _ in range(2)]
                d_xm = [rec.dsem() for _ in range(2)]
                xo = [sbt(es2, "xo%d" % i, [128, 512], F32) for i in range(2)]
                B_xo = [Buf() for _ in range(2)]
                d_xo = [rec.dsem() for _ in range(2)]
                po = [pst(es2, "po%d" % i, [128, 512], F32) for i in range(2)]
                B_po = [PBuf() for _ in range(2)]
                w_m2_v = w_m2.rearrange("(k p) n -> p k n", p=128)

                def load_w2(n):
                    i = n % 3
                    for q in range(2):
                        dma("pool", lambda e, i=i, n=n, q=q: e.dma_start(
                            out=w2[i][:, q * 32:(q + 1) * 32, :], in_=w_m2_v[:, q * 32:(q + 1) * 32, n * 128:(n + 1) * 128]),
                            d_w2[i], writes=[B_w2[i]])

                load_w2(0)
                load_w2(1)
                pctr = 0
                for n in range(16):
                    if n + 2 < 16:
                        load_w2(n + 2)
                    wi = n % 3
                    for ch in range(2):
                        pi = pctr % 2
                        pctr += 1
                        dma("sp", lambda e, pi=pi, n=n, ch=ch: e.dma_start(
                            out=xm[pi][:], in_=xmid_s[:, n * OWN + ch * 512:n * OWN + (ch + 1) * 512]), d_xm[pi],
                            reads=[B_xmid], writes=[B_xm[pi]])
                        for kc in range(64):
                            op("pe", lambda e, wi=wi, pi=pi, kc=kc, ch=ch: e.matmul(
                                po[pi][:], lhsT=w2[wi][:, kc, :], rhs=zT[:, kc, ch * 512:(ch + 1) * 512],
                                start=(kc == 0), stop=(kc == 63)),
                               reads=[B_w2[wi], B_z], writes=[B_po[pi]], signal=(kc == 63))
                        op("dve", lambda e, pi=pi, n=n: e.scalar_tensor_tensor(
                            out=xo[pi][:], in0=po[pi][:], scalar=vecs[:, V_GT2, n:n + 1], in1=xm[pi][:],
                            op0=ALU.mult, op1=ALU.add), reads=[B_po[pi], B_vecs, B_xm[pi]], writes=[B_xo[pi]])
                        dma("sp", lambda e, pi=pi, n=n, ch=ch: e.dma_start(
                            out=xout_s[:, n * OWN + ch * 512:n * OWN + (ch + 1) * 512], in_=xo[pi][:]), d_xo[pi],
                            reads=[B_xo[pi]], writes=[B_xout])
                rec.flush()
                if STOP == 8:
                    return nc

        with ExitStack() as es:
            gf = sbt(es, "gf", [128, D], F32)
            B_gf = Buf()
            d_gf = rec.dsem()
            xl = [sbt(es, "xl%d" % i, [128, 16, 128], F32) for i in range(2)]
            B_xl = [Buf() for _ in range(2)]
            d_xl = [rec.dsem() for _ in range(2)]
            junk = sbt(es, "fjunk", [128, 512], BF16)
            B_junk = Buf()
            st = [sbt(es, "fst%d" % i, [128, 8], F32) for i in range(2)]
            B_st = [Buf() for _ in range(2)]
            ot = [sbt(es, "ot%d" % i, [128, D], F32) for i in range(2)]
            B_ot = [Buf() for _ in range(2)]
            d_ot = [rec.dsem() for _ in range(2)]
            pf = [[pst(es, "pf%d_%d" % (i, k), [128, 4, 128], F32) for k in range(4)] for i in range(2)]
            B_pf = [[PBuf() for k in range(4)] for i in range(2)]
            dma("sp", lambda e: e.dma_start(out=gf[:], in_=gfin.partition_broadcast(128)), d_gf, writes=[B_gf])
            xout_v = xout_s.rearrange("p (k t) -> p k t", t=OWN)
            for t in range(8):
                i = t % 2
                dma("sp", lambda e, i=i, t=t: e.dma_start(out=xl[i][:], in_=xout_v[:, :, t * 128:(t + 1) * 128]), d_xl[i],
                    reads=[B_xout], writes=[B_xl[i]])
                for k4 in range(4):
                    for k in range(4):
                        kc = k4 * 4 + k
                        op("pe", lambda e, i=i, k4=k4, k=k, kc=kc: e.transpose(out=pf[i][k4][:, k, :], in_=xl[i][:, kc, :],
                                                                                identity=ident_f[:]),
                           reads=[B_xl[i], B_idf], writes=[B_pf[i][k4]], signal=(k == 3))
                    op("act", lambda e, i=i, k4=k4: e.activation(out=junk[:], in_=pf[i][k4][:].rearrange("p a b -> p (a b)"),
                                                                 func=AF.Square, accum_out=st[i][:, k4:k4 + 1]),
                       reads=[B_pf[i][k4]], writes=[B_junk, B_st[i]])
                op("dve", lambda e, i=i: e.tensor_reduce(out=st[i][:, 4:5], in_=st[i][:, 0:4], axis=mybir.AxisListType.X, op=ALU.add),
                   reads=[B_st[i]], writes=[B_st[i]])
                op("dve", lambda e, i=i: e.tensor_scalar(out=st[i][:, 5:6], in0=st[i][:, 4:5], scalar1=1.0 / D, scalar2=EPS,
                                                         op0=ALU.mult, op1=ALU.add), reads=[B_st[i]], writes=[B_st[i]])
                op("pool", lambda e, i=i: e.tensor_tensor(out=st[i][:, 6:7], in0=st[i][:, 5:6], in1=nhalf[:, 0:1], op=ALU.pow),
                   reads=[B_st[i], B_nhalf], writes=[B_st[i]])
                for k4 in range(4):
                    op("dve", lambda e, i=i, k4=k4: e.scalar_tensor_tensor(
                        out=ot[i][:, k4 * 512:(k4 + 1) * 512], in0=pf[i][k4][:].rearrange("p a b -> p (a b)"), scalar=st[i][:, 6:7],
                        in1=gf[:, k4 * 512:(k4 + 1) * 512], op0=ALU.mult, op1=ALU.mult),
                       reads=[B_pf[i][k4], B_st[i], B_gf], dwrites=[B_ot[i]])
                dma("sp", lambda e, i=i, t=t: e.dma_start(out=out_d[t * 128:(t + 1) * 128, :], in_=ot[i][:]), d_ot[i],
                    reads=[B_ot[i]], writes=[Buf()])
            rec.flush()
            if STOP == 9:
                return nc
    return nc


_NC_CACHE = {}


def _get_nc():
    if "nc" not in _NC_CACHE:
        _NC_CACHE["nc"] = build_nc()
    return _NC_CACHE["nc"]


def make_in_maps(x, c, ctx, c_ctx, w_ada, b_ada, g_norm1, w_in, lam_q1, lam_k1, lam_q2, lam_k2,
                 g_subln, w_attn_br, w_four_br, w_out, g_norm2, w_mlp_in, w_mlp_out, g_final):
    f32 = np.float32
    A = lambda a: np.ascontiguousarray(np.asarray(a, dtype=f32))
    cst = _consts()
    x = A(x)
    ctx = A(ctx)
    c = A(c)
    shared = {
        "bada_r": A(b_ada)[0].reshape(96, 128),
        "lamv": np.concatenate([A(lam_q1)[0], A(lam_k1)[0], A(lam_q2)[0], A(lam_k2)[0]]),
        "gsub": A(g_subln)[0].reshape(128, 1),
        "gfin": A(g_final),
        "w_ada": A(w_ada)[0],
        "w_in": A(w_in)[0],
        "w_abr": A(w_attn_br)[0],
        "w_fbr": A(w_four_br)[0],
        "w_out": A(w_out)[0],
        "w_m1": A(w_mlp_in)[0],
        "w_m2": A(w_mlp_out)[0],
        "ropeC": cst["ropeC"],
        "ropeS": cst["ropeS"],
        "perm": cst["perm"],
        "ident_bf": cst["ident_bf"],
        "ident_f": cst["ident_f"],
        "dftg": cst["dftg"],
    }
    in_maps = []
    for core in range(8):
        b, j = core // 4, core % 4
        m = dict(shared)
        m["xb"] = x[b]
        m["ctxb"] = ctx[b]
        m["xown"] = np.ascontiguousarray(x[b, j * OWN:(j + 1) * OWN])
        m["small_r"] = np.concatenate([c[b].reshape(16, 128), A(c_ctx).reshape(16, 128), A(g_norm1)[0].reshape(16, 128),
                                       A(g_norm2)[0].reshape(16, 128)], axis=0)
        m["ropeCo"] = np.ascontiguousarray(cst["ropeC"][:, j * OWN:(j + 1) * OWN] * f32(0.125))
        m["ropeSo"] = np.ascontiguousarray(cst["ropeS"][:, j * OWN:(j + 1) * OWN] * f32(0.125))
        m["dftp"] = cst["dftp"][j].reshape(2, 128, 32 * 2 * 512)
        in_maps.append(m)
    return in_maps


def kernel(**inputs):
    in_maps = make_in_maps(**inputs)
    nc = _get_nc()
    res = run_bass_kernel_spmd(nc, in_maps, core_ids=list(range(8)))
    out = np.empty((2, SEQ, D), np.float32)
    for core in range(8):
        b, j = core // 4, core % 4
        out[b, j * OWN:(j + 1) * OWN] = res.results[core]["out"]
    return out
```
